# Optimizing a Trainium2 kernel written in Bass

```python
import math
import jax, jax.numpy as jnp
from jax import lax
import numpy as np

D_MODEL = 2048
BATCH = 1
SEQ = 16384
DEPTH = 1
DEC_BATCH = 16
DEC_SEQ = 32
PAST_LEN = 4096

CHUNK = 64
BLOCK_Q = 128
N_HEADS = 16
Q_RANK = 512
KV_RANK = 512
QK_NOPE = 128
QK_ROPE = 64
QK_HEAD = QK_NOPE + QK_ROPE
V_HEAD = 128
CONV_CH = 1024
CONV_WIDTH = 31
D_FF = -(-8 * D_MODEL // (3 * 256)) * 256
ROPE_THETA = 10000.0
EPS = 1e-6
NEG_INF = -1e30
SCALE = QK_HEAD ** -0.5
IN_SPLITS = (Q_RANK, Q_RANK + KV_RANK, Q_RANK + KV_RANK + QK_ROPE,
             Q_RANK + KV_RANK + QK_ROPE + 2 * CONV_CH)
IN_DIM = Q_RANK + KV_RANK + QK_ROPE + 2 * CONV_CH + 2 * D_MODEL

kernel_name = 'mla_conformer_gated_streaming_step'


def _rms_norm(x, g):
    xf = x.astype(jnp.float32)
    y = xf * lax.rsqrt(jnp.mean(xf * xf, axis=-1, keepdims=True) + EPS)
    return (y * g.astype(jnp.float32)).astype(x.dtype)


def _layer_norm(x, g, b):
    xf = x.astype(jnp.float32)
    xc = xf - jnp.mean(xf, axis=-1, keepdims=True)
    y = xc * lax.rsqrt(jnp.mean(xc * xc, axis=-1, keepdims=True) + EPS)
    return (y * g.astype(jnp.float32) + b.astype(jnp.float32)).astype(x.dtype)


def _rope_angles(pos):
    inv_freq = 1.0 / (ROPE_THETA ** (jnp.arange(0, QK_ROPE, 2, dtype=jnp.float32) / QK_ROPE))
    ang = pos.astype(jnp.float32)[:, None] * inv_freq[None, :]
    return jnp.cos(ang), jnp.sin(ang)


def _apply_rope(x, cos, sin):
    half = QK_ROPE // 2
    xf = x.astype(jnp.float32)
    x1, x2 = xf[..., :half], xf[..., half:]
    return jnp.concatenate([x1 * cos - x2 * sin, x2 * cos + x1 * sin], axis=-1).astype(x.dtype)


def _queries(c_q, cos, sin, g_q_a, w_q_up, g_q_norm):
    B, S, _ = c_q.shape
    q = (_rms_norm(c_q, g_q_a) @ w_q_up).reshape(B, S, N_HEADS, QK_HEAD)
    q = jnp.concatenate([q[..., :QK_NOPE],
                         _apply_rope(q[..., QK_NOPE:], cos[:, None, :], sin[:, None, :])], axis=-1)
    return _rms_norm(q, g_q_norm)


def _keys_values(c_kv, k_pe, w_kv_up, g_k_norm):
    B, L, _ = c_kv.shape
    kv = (c_kv @ w_kv_up).reshape(B, L, N_HEADS, QK_NOPE + V_HEAD)
    k_rope = jnp.broadcast_to(k_pe[:, :, None, :], (B, L, N_HEADS, QK_ROPE))
    k = jnp.concatenate([kv[..., :QK_NOPE], k_rope], axis=-1)
    return _rms_norm(k, g_k_norm), kv[..., QK_NOPE:]


def _attend_prompt(q, k, v):
    B, S, H, Dh = q.shape
    nb = S // BLOCK_Q
    qb = jnp.moveaxis(q.reshape(B, nb, BLOCK_Q, H, Dh), 1, 0)
    kpos = jnp.arange(S)

    def one_block(args):
        i, qi = args
        qpos = i * BLOCK_Q + jnp.arange(BLOCK_Q)
        limit = (qpos // CHUNK + 1) * CHUNK
        s = jnp.einsum('bqhd,bkhd->bhqk', qi, k, preferred_element_type=jnp.float32) * SCALE
        s = jnp.where(kpos[None, :] < limit[:, None], s, NEG_INF)
        p = jax.nn.softmax(s, axis=-1).astype(v.dtype)
        return jnp.einsum('bhqk,bkhd->bqhd', p, v)

    out = lax.map(one_block, (jnp.arange(nb), qb))
    return jnp.moveaxis(out, 0, 1).reshape(B, S, H * V_HEAD)


def _attend_past(q, k, v):
    B, T, H, _ = q.shape
    s = jnp.einsum('bqhd,bkhd->bhqk', q, k, preferred_element_type=jnp.float32) * SCALE
    p = jax.nn.softmax(s, axis=-1).astype(v.dtype)
    return jnp.einsum('bhqk,bkhd->bqhd', p, v).reshape(B, T, H * V_HEAD)


def _conv_branch(glu, past, b_glu, w_dw, b_dw, g_ln, b_ln, w_pw, b_pw):
    a, g = jnp.split(glu + b_glu, 2, axis=-1)
    u = a * jax.nn.sigmoid(g)
    padded = jnp.concatenate([past.astype(u.dtype), u], axis=1)
    y = lax.conv_general_dilated(padded, w_dw[:, None, :], window_strides=(1,), padding='VALID',
                                 dimension_numbers=('NWC', 'WIO', 'NWC'),
                                 feature_group_count=CONV_CH) + b_dw
    y = jax.nn.silu(_layer_norm(y, g_ln, b_ln))
    return y @ w_pw + b_pw, padded[:, -(CONV_WIDTH - 1):, :]


def _layer(x, pos, past_ckv, past_kpe, past_conv, lw):
    (g_mix_norm, w_in, b_glu, b_gate, g_q_a, w_q_up, g_q_norm, g_kv_a, w_kv_up, g_k_norm,
     w_attn_out, w_dw, b_dw, g_conv_ln, b_conv_ln, w_conv_out, b_conv_out, w_out,
     g_ffn_norm, w_ffn_gate, w_ffn_up, w_ffn_down) = lw
    B = x.shape[0]
    h = _rms_norm(x, g_mix_norm)
    c_q, c_kv, k_pe, glu, gates = jnp.split(h @ w_in, IN_SPLITS, axis=-1)
    cos, sin = _rope_angles(pos)
    c_kv = _rms_norm(c_kv, g_kv_a)
    k_pe = _apply_rope(k_pe, cos, sin)
    q = _queries(c_q, cos, sin, g_q_a, w_q_up, g_q_norm)
    if past_ckv is None:
        k, v = _keys_values(c_kv, k_pe, w_kv_up, g_k_norm)
        attn = _attend_prompt(q, k, v)
        past_conv = jnp.zeros((B, CONV_WIDTH - 1, CONV_CH), x.dtype)
    else:
        k, v = _keys_values(jnp.concatenate([past_ckv.astype(c_kv.dtype), c_kv], axis=1),
                            jnp.concatenate([past_kpe.astype(k_pe.dtype), k_pe], axis=1),
                            w_kv_up, g_k_norm)
        attn = _attend_past(q, k, v)
    y_a = attn @ w_attn_out
    y_b, conv_state = _conv_branch(glu, past_conv, b_glu, w_dw, b_dw, g_conv_ln, b_conv_ln,
                                   w_conv_out, b_conv_out)
    g_a, g_b = jnp.split(jax.nn.sigmoid(gates + b_gate), 2, axis=-1)
    x = x + (g_a * y_a + g_b * y_b) @ w_out
    h2 = _rms_norm(x, g_ffn_norm)
    x = x + (jax.nn.silu(h2 @ w_ffn_gate) * (h2 @ w_ffn_up)) @ w_ffn_down
    return x, c_kv, k_pe, conv_state


def _normal(k, shape, scale):
    return scale * jax.random.normal(k, shape, jnp.float32)


def setup_inputs(seed: int = 0) -> dict:
    key = jax.random.key(seed)
    ks = jax.random.split(key, 32)
    L = DEPTH
    HQK = N_HEADS * QK_HEAD
    HKV = N_HEADS * (QK_NOPE + V_HEAD)
    HV = N_HEADS * V_HEAD
    return {
        'x_prompt': _normal(ks[0], (BATCH, SEQ, D_MODEL), 1.0),
        'x_sample': _normal(ks[1], (DEC_BATCH, DEC_SEQ, D_MODEL), 1.0),
        'cache_ckv': _normal(ks[2], (L, DEC_BATCH, PAST_LEN, KV_RANK), 1.0),
        'cache_kpe': _normal(ks[3], (L, DEC_BATCH, PAST_LEN, QK_ROPE), 1.0),
        'state_conv': _normal(ks[4], (L, DEC_BATCH, CONV_WIDTH - 1, CONV_CH), 0.5),
        'g_mix_norm': 1.0 + _normal(ks[5], (L, D_MODEL), 0.02),
        'w_in': _normal(ks[6], (L, D_MODEL, IN_DIM), D_MODEL ** -0.5),
        'b_glu': _normal(ks[7], (L, 2 * CONV_CH), 0.02),
        'b_gate': _normal(ks[8], (L, 2 * D_MODEL), 0.02),
        'g_q_a': 1.0 + _normal(ks[9], (L, Q_RANK), 0.02),
        'w_q_up': _normal(ks[10], (L, Q_RANK, HQK), Q_RANK ** -0.5),
        'g_q_norm': 1.0 + _normal(ks[11], (L, QK_HEAD), 0.02),
        'g_kv_a': 1.0 + _normal(ks[12], (L, KV_RANK), 0.02),
        'w_kv_up': _normal(ks[13], (L, KV_RANK, HKV), KV_RANK ** -0.5),
        'g_k_norm': 1.0 + _normal(ks[14], (L, QK_HEAD), 0.02),
        'w_attn_out': _normal(ks[15], (L, HV, D_MODEL), HV ** -0.5),
        'w_dw': _normal(ks[16], (L, CONV_WIDTH, CONV_CH), CONV_WIDTH ** -0.5),
        'b_dw': _normal(ks[17], (L, CONV_CH), 0.02),
        'g_conv_ln': 1.0 + _normal(ks[18], (L, CONV_CH), 0.02),
        'b_conv_ln': _normal(ks[19], (L, CONV_CH), 0.02),
        'w_conv_out': _normal(ks[20], (L, CONV_CH, D_MODEL), CONV_CH ** -0.5),
        'b_conv_out': _normal(ks[21], (L, D_MODEL), 0.02),
        'w_out': _normal(ks[22], (L, D_MODEL, D_MODEL), D_MODEL ** -0.5),
        'g_ffn_norm': 1.0 + _normal(ks[23], (L, D_MODEL), 0.02),
        'w_ffn_gate': _normal(ks[24], (L, D_MODEL, D_FF), D_MODEL ** -0.5),
        'w_ffn_up': _normal(ks[25], (L, D_MODEL, D_FF), D_MODEL ** -0.5),
        'w_ffn_down': _normal(ks[26], (L, D_FF, D_MODEL), D_FF ** -0.5),
    }


def reference(x_prompt, x_sample, cache_ckv, cache_kpe, state_conv, g_mix_norm, w_in, b_glu,
              b_gate, g_q_a, w_q_up, g_q_norm, g_kv_a, w_kv_up, g_k_norm, w_attn_out, w_dw,
              b_dw, g_conv_ln, b_conv_ln, w_conv_out, b_conv_out, w_out, g_ffn_norm,
              w_ffn_gate, w_ffn_up, w_ffn_down):
    weights = (g_mix_norm, w_in, b_glu, b_gate, g_q_a, w_q_up, g_q_norm, g_kv_a, w_kv_up,
               g_k_norm, w_attn_out, w_dw, b_dw, g_conv_ln, b_conv_ln, w_conv_out, b_conv_out,
               w_out, g_ffn_norm, w_ffn_gate, w_ffn_up, w_ffn_down)
    pos_prompt = jnp.arange(x_prompt.shape[1])
    pos_sample = cache_ckv.shape[2] + jnp.arange(x_sample.shape[1])
    y_prompt, y_sample = x_prompt, x_sample
    ckv_p, kpe_p, conv_p, ckv_s, kpe_s, conv_s = [], [], [], [], [], []
    for l in range(DEPTH):
        lw = tuple(w[l] for w in weights)
        y_prompt, c_kv, k_pe, conv_state = _layer(y_prompt, pos_prompt, None, None, None, lw)
        ckv_p.append(c_kv)
        kpe_p.append(k_pe)
        conv_p.append(conv_state)
        y_sample, c_kv, k_pe, conv_state = _layer(y_sample, pos_sample, cache_ckv[l],
                                                  cache_kpe[l], state_conv[l], lw)
        ckv_s.append(c_kv)
        kpe_s.append(k_pe)
        conv_s.append(conv_state)
    return (y_prompt, y_sample, jnp.stack(ckv_p), jnp.stack(kpe_p), jnp.stack(conv_p),
            jnp.stack(ckv_s), jnp.stack(kpe_s), jnp.stack(conv_s))
```

```python
import numpy as np
import concourse.bass as bass
import concourse.mybir as mybir
from concourse.bass_utils import run_bass_kernel_spmd

F32 = mybir.dt.float32
BF16 = mybir.dt.bfloat16
AF = mybir.ActivationFunctionType
ALU = mybir.AluOpType

NCORE = 8
D = 2048
S = 16384
TOWN = 2048
NH = 16
IND = 7232
DFF = 5632
CONV = 1024
CW = 31
PAST = 4096
TS = 32
EPS = 1e-6
NKT = 128 + 2 * 33
NK = NKT * 128
MAGIC = 12582912.0
C1 = 6.28125
C2 = 0.0019353071795864769
PI = 3.1415925

VO = {}
_o = 0
for _n, _w in (("g_mix", 16), ("g_q_a", 4), ("g_kv_a", 4), ("b_glu", 16), ("b_gate", 32), ("w_dw", 248),
               ("b_dw", 8), ("g_ln", 8), ("b_ln", 8), ("b_co", 16), ("g_ffn", 16), ("gq", 2), ("gk", 2),
               ("invf", 1), ("c_eps", 1), ("c_eps192", 1)):
    VO[_n] = _o
    _o += _w
NV = _o

import os
PHASES = set(os.environ.get("KPHASES", "p1a,p1b,p2,p3").split(","))
SAMPLE_ONLY = False
DBG = {}


class Buf:
    __slots__ = ("w", "r", "x")

    def __init__(self, x=False):
        self.w = {}
        self.r = {}
        self.x = x


class Sched:
    def __init__(self, nc):
        self.nc = nc
        self.eng = {"pe": nc.tensor, "act": nc.scalar, "dve": nc.vector, "pool": nc.gpsimd, "sp": nc.sync}
        self.sems = {}
        self.cnt = {}
        self.seen = {e: {} for e in self.eng}
        for e in ("pe", "act", "dve", "pool"):
            self.sems[e] = nc.semaphore("s_" + e).__enter__()
            self.cnt[e] = 0
        self.ND = 8
        self.dq = {}
        for q in ("sp", "pool"):
            names = [f"d_{q}{i}" for i in range(self.ND)]
            for n in names:
                self.sems[n] = nc.semaphore(n).__enter__()
                self.cnt[n] = 0
            self.dq[q] = [names, 0]

    def _wait(self, e, key, val):
        if self.seen[e].get(key, 0) >= val:
            return
        self.eng[e].wait_ge(self.sems[key], val)
        self.seen[e][key] = val

    def _deps(self, e, reads, writes):
        need = {}

        def add(k, v, war=False):
            if k == e and e == "pe":
                return
            if need.get(k, 0) < v:
                need[k] = v

        for b in reads:
            for k, v in b.w.items():
                add(k, v)
        for b in writes:
            for k, v in b.w.items():
                add(k, v)
            for k, v in b.r.items():
                add(k, v, True)
        for k, v in need.items():
            self._wait(e, k, v)

    def _mark(self, tok, reads, writes):
        k, v = tok
        for b in reads:
            if b.r.get(k, 0) < v:
                b.r[k] = v
        for b in writes:
            if b.w.get(k, 0) < v:
                b.w[k] = v

    def op(self, e, fn, reads=(), writes=()):
        writes = list(writes) + [b for b in reads if b.x]
        reads = [b for b in reads if not b.x]
        self._deps(e, reads, writes)
        ins = fn(self.eng[e])
        self.cnt[e] += 1
        ins.then_inc(self.sems[e], 1)
        tok = (e, self.cnt[e])
        self._mark(tok, reads, writes)
        return tok

    def dma(self, q, out, in_, reads=(), writes=()):
        self._deps(q, reads, writes)
        names, i = self.dq[q]
        n = names[i]
        self.dq[q][1] = (i + 1) % self.ND
        if self.cnt[n] > 0:
            self._wait(q, n, self.cnt[n])
        ins = self.eng[q].dma_start(out=out, in_=in_)
        self.cnt[n] += 16
        ins.then_inc(self.sems[n], 16)
        tok = (n, self.cnt[n])
        self._mark(tok, reads, writes)
        return tok

    def barrier(self):
        for e in self.eng:
            for k, v in self.cnt.items():
                if k != e and v > 0:
                    self._wait(e, k, v)


class Tl:
    def __init__(self, t):
        self.t = t
        self.b = Buf()

    def __getitem__(self, k):
        return self.t[k]


class Ring:
    def __init__(self, tiles):
        self.tiles = tiles
        self.i = 0

    def next(self):
        t = self.tiles[self.i]
        self.i = (self.i + 1) % len(self.tiles)
        return t


def build_program():
    nc = bass.Bass("TRN2", target_bir_lowering=False)
    sc = Sched(nc)
    ctx = []

    def dram_in(name, shape):
        return nc.dram_tensor(name, list(shape), F32, kind="ExternalInput").ap()

    def dram_out(name, shape):
        return nc.dram_tensor(name, list(shape), F32, kind="ExternalOutput").ap()

    xT_all = dram_in("xT_all", [D, S])
    xT_own = dram_in("xT_own", [D, TOWN])
    xT_halo = dram_in("xT_halo", [D, 512])
    xT_s = dram_in("xT_s", [D, 64])
    pos_all = dram_in("pos_all", [1, S])
    pos_own = dram_in("pos_own", [1, TOWN])
    pos_s = dram_in("pos_s", [1, 64])
    hmask = dram_in("hmask", [1, 512])
    mb_in = dram_in("mb", [128, 1024])
    ident_in = dram_in("ident", [128, 128])
    cckvT = dram_in("cckvT", [2, 512, PAST])
    ckpeT = dram_in("ckpeT", [2, 64, PAST])
    sconvT = dram_in("sconvT", [2, CONV, 30])
    vecs_in = dram_in("vecs", [128, NV])
    w_in = dram_in("w_in", [D, IND])
    w_q_up = dram_in("w_q_up", [512, 3072])
    w_kv_up = dram_in("w_kv_up", [512, 4096])
    w_ao = dram_in("w_attn_out", [D, D])
    w_co = dram_in("w_conv_out", [CONV, D])
    w_o = dram_in("w_out", [D, D])
    w_fg = dram_in("w_ffn_gate", [D, DFF])
    w_fu = dram_in("w_ffn_up", [D, DFF])
    w_fd = dram_in("w_ffn_down", [DFF, D])

    yT_own = dram_out("yT_own", [D, TOWN])
    ckvT_own = dram_out("ckvT_own", [512, TOWN])
    kpeT_own = dram_out("kpeT_own", [64, TOWN])
    uT_last = dram_out("uT_last", [CONV, 32])
    yT_s = dram_out("yT_s", [D, 64])
    ckvT_s = dram_out("ckvT_s", [512, 64])
    kpeT_s = dram_out("kpeT_s", [64, 64])
    uT_s = dram_out("uT_s", [CONV, 64])

    KT = nc.dram_tensor("KT_scr", [NH, 128, NK], BF16).ap()
    KR = nc.dram_tensor("KR_scr", [64, NK], BF16).ap()
    V2 = nc.dram_tensor("V2_scr", [NH, 128, NKT, 128], BF16).ap()
    AT = nc.dram_tensor("AT_scr", [NH, 128, TOWN + 64], BF16).ap()
    WB = nc.dram_tensor("WB_scr", [62, 128, 8192], BF16).ap()

    cnt = [0]

    def sb(shape, dt, stack=None):
        cnt[0] += 1
        g = nc.sbuf_tensor(f"t{cnt[0]}", list(shape), dt)
        t = g.__enter__()
        (stack if stack is not None else ctx).append(g)
        return Tl(t)

    def free(stack):
        sc.barrier()
        while stack:
            stack.pop().__exit__(None, None, None)

    pp = []
    for i in range(4):
        g = nc.psum_tensor(f"pp{i}", [128, 1024], F32)
        pp.append(g.__enter__())
        ctx.append(g)
    bankb = [Buf(True) for _ in range(8)]

    class Bank:
        def __init__(self, i):
            self.i = i
            self.b = bankb[i]

        def ap(self, p0, p1, c0, c1):
            return pp[self.i // 2][p0:p1, (self.i % 2) * 512 + c0:(self.i % 2) * 512 + c1]

    banks = [Bank(i) for i in range(8)]
    rot7 = Ring(banks[0:7])
    rot2 = Ring(banks[6:8])

    def mm(out, lhsT, rhs, start, stop, R, W):
        return sc.op("pe", lambda e: e.matmul(out, lhsT=lhsT, rhs=rhs, start=start, stop=stop), reads=R, writes=W)

    vec = sb([128, NV], F32)
    ones = sb([128, 4, 128], BF16)
    onesf = sb([128, 128], F32)
    cos_s = sb([128, 64], F32)
    sin_s = sb([128, 64], F32)
    cqn_s = sb([128, 4, 64], BF16)
    ckvn_s = sb([128, 4, 64], BF16)
    kpe_s = sb([64, 64], BF16)
    sqr = Ring([sb([128, 512], BF16) for _ in range(3)])
    p12 = []
    scr_b = Buf()
    at_scr_b = Buf()
    ident = sb([128, 128], BF16, p12)
    mbt = sb([128, 8, 128], BF16, p12)
    gg = sb([128, 2], F32, p12)
    rstdk = sb([128, NKT, NH], F32, p12)
    cos_own = sb([128, TOWN], F32, p12)
    sin_own = sb([128, TOWN], F32, p12)
    cqn = sb([128, 4, TOWN], BF16, p12)

    sc.dma("sp", vec[:], vecs_in, writes=[vec.b])

    def V(name, i=0, p1=128):
        o = VO[name] + i
        return vec[0:p1, o:o + 1]

    for i, v in enumerate((1.0 / 2048, 1.0 / 512, 1.0 / 1024, 1.0)):
        sc.op("pool", lambda e, i=i, v=v: e.memset(ones[:, i, :], v), writes=[ones.b])
    sc.op("pool", lambda e: e.memset(onesf[:], 1.0), writes=[onesf.b])
    sc.dma("pool", ident[:], ident_in, writes=[ident.b])
    sc.dma("pool", mbt[:].rearrange("p a b -> p (a b)"), mb_in, writes=[mbt.b])
    sc.op("dve", lambda e: e.scalar_tensor_tensor(out=gg[:], in0=vec[:, VO["gq"]:VO["gq"] + 2], scalar=float(np.sqrt(192.0)),
                                                  in1=vec[:, VO["gk"]:VO["gk"] + 2], op0=ALU.mult, op1=ALU.mult),
          reads=[vec.b], writes=[gg.b])
    sc.op("pool", lambda e: e.memset(rstdk[:], 1.0), writes=[rstdk.b])

    def rsqrt_bc(ps_ap, epsname, out_tl, np_, T, tmp_tl):
        sc.op("act", lambda e: e.activation(out=tmp_tl[0:np_, 0:T], in_=ps_ap, func=AF.Sqrt, bias=V(epsname, 0, np_), scale=1.0),
              reads=[vec.b] + ps_ap_b[0], writes=[tmp_tl.b])
        sc.op("dve", lambda e: e.reciprocal(out=out_tl[0:np_, 0:T], in_=tmp_tl[0:np_, 0:T]), reads=[tmp_tl.b], writes=[out_tl.b])

    ps_ap_b = [[]]

    def sin_of(ang, out, T, t1, t2, shift, NP=64):
        src = ang
        if shift != 0.0:
            sc.op("dve", lambda e: e.tensor_scalar(out=t1[0:NP, 0:T], in0=ang[0:NP, 0:T], scalar1=float(shift), scalar2=None, op0=ALU.add),
                  reads=[ang.b], writes=[t1.b])
            src = t1
        sc.op("dve", lambda e: e.tensor_scalar(out=t2[0:NP, 0:T], in0=src[0:NP, 0:T], scalar1=float(1.0 / (2 * np.pi)), scalar2=MAGIC,
                                               op0=ALU.mult, op1=ALU.add), reads=[src.b], writes=[t2.b])
        sc.op("dve", lambda e: e.tensor_scalar(out=t2[0:NP, 0:T], in0=t2[0:NP, 0:T], scalar1=MAGIC, scalar2=None, op0=ALU.subtract),
              reads=[t2.b], writes=[t2.b])
        sc.op("dve", lambda e: e.scalar_tensor_tensor(out=t1[0:NP, 0:T], in0=t2[0:NP, 0:T], scalar=-C1, in1=src[0:NP, 0:T], op0=ALU.mult, op1=ALU.add),
              reads=[t2.b, src.b], writes=[t1.b])
        sc.op("dve", lambda e: e.scalar_tensor_tensor(out=t1[0:NP, 0:T], in0=t2[0:NP, 0:T], scalar=-C2, in1=t1[0:NP, 0:T], op0=ALU.mult, op1=ALU.add),
              reads=[t2.b, t1.b], writes=[t1.b])
        sc.op("dve", lambda e: e.tensor_scalar(out=t1[0:NP, 0:T], in0=t1[0:NP, 0:T], scalar1=PI, scalar2=-PI, op0=ALU.min, op1=ALU.max),
              reads=[t1.b], writes=[t1.b])
        sc.op("act", lambda e: e.activation(out=out, in_=t1[0:NP, 0:T], func=AF.Sin),
              reads=[t1.b], writes=[out.b if isinstance(out, Tl) else out_b[0]])

    out_b = [None]

    def rope_tables(pos_ap, T, cos_ap, sin_ap, dst_b, tmps, NP=64):
        pb, ang, t1, t2 = tmps
        sc.dma("sp", pb[0:NP, 0:T], pos_ap.partition_broadcast(NP), writes=[pb.b])
        sc.op("dve", lambda e: e.tensor_scalar(out=ang[0:NP, 0:T], in0=pb[0:NP, 0:T], scalar1=V("invf", 0, NP), scalar2=None, op0=ALU.mult),
              reads=[pb.b, vec.b], writes=[ang.b])
        out_b[0] = dst_b
        sin_of(ang, sin_ap, T, t1, t2, 0.0, NP)
        sin_of(ang, cos_ap, T, t1, t2, float(np.pi / 2), NP)

    def rms_h(xt, hT, T, st):
        bk = rot7.next()
        for kc in range(16):
            sq = sqr.next()
            sc.op("act", lambda e: e.activation(out=sq[:, 0:T], in_=xt[:, kc, 0:T], func=AF.Square), reads=[xt.b], writes=[sq.b])
            mm(bk.ap(0, 128, 0, T), ones[:, 0, :], sq[:, 0:T], kc == 0, kc == 15, [ones.b, sq.b], [bk.b])
        ps_ap_b[0] = [bk.b]
        rsqrt_bc(bk.ap(0, 128, 0, T), "c_eps", st["rs"], 128, T, st["rtmp"])
        rs = st["rs"]
        for kc in range(16):
            sc.op("dve", lambda e, kc=kc: e.scalar_tensor_tensor(out=hT[:, kc, 0:T], in0=xt[:, kc, 0:T], scalar=V("g_mix", kc),
                                                                 in1=rs[:, 0:T], op0=ALU.mult, op1=ALU.mult),
                  reads=[xt.b, rs.b, vec.b], writes=[hT.b])

    def proj16(bk, M, wt, c0, hT, T):
        for kc in range(16):
            mm(bk.ap(0, M, 0, T), wt[:, kc, c0:c0 + M], hT[:, kc, 0:T], kc == 0, kc == 15, [wt.b, hT.b], [bk.b])

    def lowrank_norm(wkq, c0, gname, hT, T, st, dst_fn, dst_bufs):
        raw = st["raw"]
        bk2 = rot7.next()
        pend = None
        for oc in range(4):
            bk = rot7.next()
            proj16(bk, 128, wkq, c0 + 128 * oc, hT, T)
            sc.op("dve", lambda e, oc=oc, bk=bk: e.tensor_copy(out=raw[:, oc, 0:T], in_=bk.ap(0, 128, 0, T)), reads=[bk.b], writes=[raw.b])
            sq = sqr.next()
            sc.op("act", lambda e, bk=bk, sq=sq: e.activation(out=sq[:, 0:T], in_=bk.ap(0, 128, 0, T), func=AF.Square), reads=[bk.b], writes=[sq.b])
            if pend is not None:
                mm(bk2.ap(0, 128, 0, T), ones[:, 1, :], pend[1][:, 0:T], pend[0] == 0, False, [ones.b, pend[1].b], [bk2.b])
            pend = (oc, sq)
        mm(bk2.ap(0, 128, 0, T), ones[:, 1, :], pend[1][:, 0:T], False, True, [ones.b, pend[1].b], [bk2.b])
        ps_ap_b[0] = [bk2.b]
        rsqrt_bc(bk2.ap(0, 128, 0, T), "c_eps", st["rs"], 128, T, st["rtmp"])
        rs = st["rs"]
        for oc in range(4):
            sc.op("dve", lambda e, oc=oc: e.scalar_tensor_tensor(out=dst_fn(oc), in0=raw[:, oc, 0:T], scalar=V(gname, oc), in1=rs[:, 0:T],
                                                                 op0=ALU.mult, op1=ALU.mult),
                  reads=[raw.b, rs.b, vec.b], writes=dst_bufs)

    def kpe_rope(wkq, hT, T, cos_ap, sin_ap, cs_bufs, st, dst_ap, dst_bufs, coff=0):
        bA = rot7.next()
        proj16(bA, 64, wkq, 1024 - coff, hT, T)
        bB = rot7.next()
        proj16(bB, 64, wkq, 1088 - coff, hT, T)
        t1, t2 = st["r1"], st["r2"]
        sc.op("dve", lambda e: e.tensor_tensor(out=t1[:, 0:T], in0=bA.ap(0, 64, 0, T), in1=cos_ap, op=ALU.mult), reads=[bA.b] + cs_bufs, writes=[t1.b])
        sc.op("dve", lambda e: e.tensor_tensor(out=t2[:, 0:T], in0=bB.ap(0, 64, 0, T), in1=sin_ap, op=ALU.mult), reads=[bB.b] + cs_bufs, writes=[t2.b])
        sc.op("dve", lambda e: e.tensor_tensor(out=dst_ap, in0=t1[:, 0:T], in1=t2[:, 0:T], op=ALU.add), reads=[t1.b, t2.b], writes=dst_bufs)

    p1 = []
    wkq = sb([128, 16, 1152], BF16, p1)
    for kc in range(16):
        sc.dma("pool", wkq[:, kc, 0:1088], w_in[kc * 128:(kc + 1) * 128, 0:1088], writes=[wkq.b])
    sc.op("pool", lambda e: e.tensor_scalar(out=wkq[:, :, 1088:1120], in0=wkq[:, :, 1056:1088], scalar1=-1.0, scalar2=None, op0=ALU.mult),
          reads=[wkq.b], writes=[wkq.b])
    sc.op("pool", lambda e: e.tensor_copy(out=wkq[:, :, 1120:1152], in_=wkq[:, :, 1024:1056]), reads=[wkq.b], writes=[wkq.b])

    st = {"rs": sb([128, 512], F32, p1), "rtmp": sb([128, 512], F32, p1), "raw": sb([128, 4, 512], F32, p1),
          "r1": sb([64, 512], F32, p1), "r2": sb([64, 512], F32, p1)}
    rtm = [sb([128, 512], F32, p1) for _ in range(4)]
    xts = Ring([sb([128, 16, 512], F32, p1) for _ in range(1)])
    hTs = Ring([sb([128, 16, 512], BF16, p1) for _ in range(1)])
    ckvf = Ring([sb([128, 4, 512], F32, p1) for _ in range(1)])
    kpef = Ring([sb([64, 512], F32, p1) for _ in range(1)])

    for t in range(4):
        out_b[0] = cos_own.b
        rope_tables(pos_own[:, t * 512:(t + 1) * 512], 512, cos_own[:, t * 512:(t + 1) * 512], sin_own[:, t * 512:(t + 1) * 512], cos_own.b, rtm, 128)
    rope_tables(pos_s[:, 0:64], 64, cos_s[:, 0:64], sin_s[:, 0:64], cos_s.b, rtm, 128)
    sin_own.b = cos_own.b
    sin_s.b = cos_s.b

    def own_tokens(x_ap, T, cos_ap, sin_ap, cs_b, cq_dst, cq_b, ckv_out, kpe_out, sample):
        xt = xts.next()
        hT = hTs.next()
        sc.dma("sp", xt[:, :, 0:T], x_ap.rearrange("(k p) t -> p k t", p=128), writes=[xt.b])
        rms_h(xt, hT, T, st)
        lowrank_norm(wkq, 0, "g_q_a", hT, T, st, lambda oc: cq_dst(oc), [cq_b])
        cf = ckvf.next()
        lowrank_norm(wkq, 512, "g_kv_a", hT, T, st, lambda oc: cf[:, oc, 0:T], [cf.b])
        sc.dma("sp", ckv_out.rearrange("(k p) t -> p k t", p=128), cf[:, :, 0:T], reads=[cf.b])
        kf = kpef.next()
        kpe_rope(wkq, hT, T, cos_ap, sin_ap, [cs_b], st, kf[:, 0:T], [kf.b])
        sc.dma("sp", kpe_out, kf[:, 0:T], reads=[kf.b])
        if sample:
            sc.op("dve", lambda e: e.tensor_copy(out=ckvn_s[:, :, 0:T], in_=cf[:, :, 0:T]), reads=[cf.b], writes=[ckvn_s.b])
            sc.op("dve", lambda e: e.tensor_copy(out=kpe_s[:, 0:T], in_=kf[:, 0:T]), reads=[kf.b], writes=[kpe_s.b])

    if "p1a" in PHASES:
        for t in range(0 if SAMPLE_ONLY else 4):
            c = slice(t * 512, (t + 1) * 512)
            own_tokens(xT_own[:, c], 512, cos_own[0:64, c], sin_own[0:64, c], cos_own.b,
                       lambda oc, c=c: cqn[:, oc, c], cqn.b, ckvT_own[:, c], kpeT_own[:, c], False)
        own_tokens(xT_s[:, 0:64], 64, cos_s[0:64, 0:64], sin_s[0:64, 0:64], cos_s.b,
                   lambda oc: cqn_s[:, oc, 0:64], cqn_s.b, ckvT_s[:, 0:64], kpeT_s[:, 0:64], True)

    free(p1)
    p1 = []
    if os.environ.get("KDEBUG"):
        print("sbuf remaining before P1b", nc.sbuf_bytes_remaining)
    wkq = sb([128, 16, 640], BF16, p1)
    for kc in range(16):
        sc.dma("pool", wkq[:, kc, 0:576], w_in[kc * 128:(kc + 1) * 128, 512:1088], writes=[wkq.b])
    sc.op("pool", lambda e: e.tensor_scalar(out=wkq[:, :, 576:608], in0=wkq[:, :, 544:576], scalar1=-1.0, scalar2=None, op0=ALU.mult),
          reads=[wkq.b], writes=[wkq.b])
    sc.op("pool", lambda e: e.tensor_copy(out=wkq[:, :, 608:640], in_=wkq[:, :, 512:544]), reads=[wkq.b], writes=[wkq.b])
    wkv = sb([128, 4, 4096], BF16, p1)
    for kc in range(4):
        sc.dma("pool", wkv[:, kc, :], w_kv_up[kc * 128:(kc + 1) * 128, :], writes=[wkv.b])
    rtm = [sb([64, 512], F32, p1) for _ in range(4)]
    st = {"rs": sb([128, 512], F32, p1), "rtmp": sb([128, 512], F32, p1), "raw": sb([128, 4, 512], F32, p1),
          "r1": rtm[2], "r2": rtm[3]}
    xts = Ring([sb([128, 16, 512], BF16, p1) for _ in range(2)])
    hTs = xts
    kbufs = Ring([sb([128, 16, 512], BF16, p1) for _ in range(1)])
    vbufs = Ring([sb([128, 4, 16, 128], BF16, p1) for _ in range(1)])
    sqrope = Ring([sb([64, 512], BF16, p1) for _ in range(2)])
    ssq_bank = banks[7]
    wkv_v = wkv[:].rearrange("p k (h c) -> p k h c", c=256)

    def gen_kv(ckvn, kpeb, T, kt0):
        nsub = (T + 127) // 128
        nk = min(128, T)
        k0 = kt0 * 128
        kpe_ap, kpe_b = kpeb
        sc.dma("sp", KR[:, k0:k0 + T], kpe_ap, reads=[kpe_b], writes=[scr_b])
        sr = sqrope.next()
        sc.op("pool", lambda e: e.tensor_tensor(out=sr[:, 0:T], in0=kpe_ap, in1=kpe_ap, op=ALU.mult), reads=[kpe_b], writes=[sr.b])
        kb = kbufs.next()

        def k_tiny(h, sq):
            for sub in range(nsub):
                col = sub * 16 + h
                mm(ssq_bank.ap(0, nk, col, col + 1), sq[:, sub * 128:sub * 128 + nk], ones[:, 3, 0:1], True, False, [sq.b, ones.b], [ssq_bank.b])
                mm(ssq_bank.ap(0, nk, col, col + 1), sr[0:64, sub * 128:sub * 128 + nk], ones[0:64, 3, 0:1], False, True, [sr.b, ones.b], [ssq_bank.b])
        pend = None
        for h in range(NH):
            bk = rot7.next()
            for kc in range(4):
                mm(bk.ap(0, 128, 0, T), wkv[:, kc, h * 256:h * 256 + 128], ckvn[:, kc, 0:T], kc == 0, kc == 3, [wkv.b, ckvn.b], [bk.b])
            sc.op("dve", lambda e, h=h, bk=bk: e.tensor_copy(out=kb[:, h, 0:T], in_=bk.ap(0, 128, 0, T)), reads=[bk.b], writes=[kb.b])
            sq = sqr.next()
            sc.op("act", lambda e, bk=bk, sq=sq: e.activation(out=sq[:, 0:T], in_=bk.ap(0, 128, 0, T), func=AF.Square), reads=[bk.b], writes=[sq.b])
            if pend is not None:
                k_tiny(*pend)
            pend = (h, sq)
        k_tiny(*pend)
        tmp = st["rtmp"]
        sc.op("act", lambda e: e.activation(out=tmp[0:nk, 0:nsub * 16], in_=ssq_bank.ap(0, nk, 0, nsub * 16), func=AF.Sqrt,
                                            bias=V("c_eps192", 0, nk), scale=1.0), reads=[ssq_bank.b, vec.b], writes=[tmp.b])
        sc.op("dve", lambda e: e.reciprocal(out=rstdk[0:nk, kt0:kt0 + nsub, :], in_=tmp[0:nk, 0:nsub * 16].rearrange("p (s h) -> p s h", h=16)),
              reads=[tmp.b], writes=[rstdk.b])
        sc.dma("sp", KT[:, :, k0:k0 + T].rearrange("h d t -> d h t"), kb[:, :, 0:T], reads=[kb.b], writes=[scr_b])
        vb = vbufs.next()
        if nk < 128:
            sc.op("pool", lambda e: e.memset(vb[:, 0, :, :], 0.0), writes=[vb.b])
        for sub in range(nsub):
            for hg in range(4):
                bk = rot7.next()
                for kc in range(4):
                    mm(bk.ap(0, nk, 0, 512).rearrange("p (h d) -> p h d", d=128), ckvn[:, kc, sub * 128:sub * 128 + nk],
                       wkv_v[:, kc, hg * 4:(hg + 1) * 4, 128:256], kc == 0, kc == 3, [wkv.b, ckvn.b], [bk.b])
                eng = "act" if (hg % 2 == 0) else "dve"
                if eng == "act":
                    sc.op("act", lambda e, bk=bk, sub=sub, hg=hg: e.copy(out=vb[0:nk, sub, hg * 4:(hg + 1) * 4, :],
                                                                         in_=bk.ap(0, nk, 0, 512).rearrange("p (h d) -> p h d", d=128)),
                          reads=[bk.b], writes=[vb.b])
                else:
                    sc.op("dve", lambda e, bk=bk, sub=sub, hg=hg: e.tensor_copy(out=vb[0:nk, sub, hg * 4:(hg + 1) * 4, :],
                                                                                in_=bk.ap(0, nk, 0, 512).rearrange("p (h d) -> p h d", d=128)),
                          reads=[bk.b], writes=[vb.b])
        for sub in range(nsub):
            sc.dma("sp", V2[:, :, kt0 + sub, :].rearrange("h p d -> p h d"), vb[:, sub, :, :], reads=[vb.b], writes=[scr_b])

    if "p1b" in PHASES:
        ckvb = Ring([sb([128, 4, 512], BF16, p1) for _ in range(2)])
        kpeb = Ring([sb([64, 512], BF16, p1) for _ in range(2)])
        cst = rtm
        cosb = sb([64, 512], F32, p1)
        sinb = sb([64, 512], F32, p1)
        sinb.b = cosb.b
        for t in range(0 if SAMPLE_ONLY else S // 512):
            c = slice(t * 512, (t + 1) * 512)
            xt = xts.next()
            hT = hTs.next()
            sc.dma("pool", xt[:, :, :], xT_all[:, c].rearrange("(k p) t -> p k t", p=128), writes=[xt.b])
            rms_h(xt, hT, 512, st)
            cb = ckvb.next()
            lowrank_norm(wkq, 0, "g_kv_a", hT, 512, st, lambda oc, cb=cb: cb[:, oc, :], [cb.b])
            out_b[0] = cosb.b
            rope_tables(pos_all[:, c], 512, cosb[:, :], sinb[:, :], cosb.b, cst)
            kb_ = kpeb.next()
            kpe_rope(wkq, hT, 512, cosb[:, :], sinb[:, :], [cosb.b], st, kb_[:, :], [kb_.b], coff=512)
            gen_kv(cb, (kb_[:, :], kb_.b), 512, t * 4)
        for b in range(2):
            for t in range(8):
                c = slice(t * 512, (t + 1) * 512)
                cb = ckvb.next()
                sc.dma("pool", cb[:, :, :], cckvT[b, :, c].rearrange("(k p) t -> p k t", p=128), writes=[cb.b])
                kb_ = kpeb.next()
                sc.dma("pool", kb_[:, :], ckpeT[b, :, c], writes=[kb_.b])
                gen_kv(cb, (kb_[:, :], kb_.b), 512, 128 + 33 * b + 4 * t)
            cb = ckvb.next()
            sc.op("dve", lambda e, cb=cb, b=b: e.tensor_copy(out=cb[:, :, 0:32], in_=ckvn_s[:, :, 32 * b:32 * b + 32]), reads=[ckvn_s.b], writes=[cb.b])
            kb_ = kpeb.next()
            sc.op("dve", lambda e, kb_=kb_, b=b: e.tensor_copy(out=kb_[:, 0:32], in_=kpe_s[:, 32 * b:32 * b + 32]), reads=[kpe_s.b], writes=[kb_.b])
            gen_kv(cb, (kb_[:, 0:32], kb_.b), 32, 128 + 33 * b + 32)
    free(p1)


    wblocks = []
    for g in range(2):
        wblocks.append([(0, w_in, 16, 1088 + 512 * g, 512)])
        wblocks.append([(0, w_in, 16, 2112 + 512 * g, 512)])
    for oc in range(16):
        wblocks.append([(0, w_ao, 16, 128 * oc, 128), (2048, w_co, 8, 128 * oc, 128),
                        (3072, w_in, 16, 3136 + 128 * oc, 128), (5120, w_in, 16, 3136 + 2048 + 128 * oc, 128)])
    for g in range(4):
        wblocks.append([(0, w_o, 16, 512 * g, 512)])
    for g in range(11):
        wblocks.append([(0, w_fg, 16, 512 * g, 512)])
        wblocks.append([(0, w_fu, 16, 512 * g, 512)])
    for oc in range(16):
        wblocks.append([(0, w_fd, 44, 128 * oc, 128)])
    assert len(wblocks) == 62
    wb_scr_b = Buf()

    def convert_block(bi):
        for (off, w, KC, c0, ncols) in wblocks[bi]:
            sc.dma("pool", WB[bi, :, off:off + KC * ncols].rearrange("p (k c) -> p k c", c=ncols),
                   w[0:KC * 128, c0:c0 + ncols].rearrange("(k p) c -> p k c", p=128), writes=[wb_scr_b])
    conv_done = [0]

    def convert_upto(n):
        while conv_done[0] < min(n, 62):
            convert_block(conv_done[0])
            conv_done[0] += 1

    if "p2" in PHASES:
        p2 = []
        wqs = Ring([sb([128, 4, 384], BF16, p2) for _ in range(2)])
        qns = Ring([sb([128, TOWN + 64], BF16, p2) for _ in range(2)])
        qrs = Ring([sb([128, TOWN + 64], BF16, p2) for _ in range(2)])
        kch = Ring([sb([128, 33 * 128], BF16, p2) for _ in range(2)])
        krch = Ring([sb([128, 33 * 128], BF16, p2) for _ in range(2)])
        vch = Ring([sb([128, 33, 128], BF16, p2) for _ in range(2)])
        pTs = Ring([sb([128, 1024], BF16, p2) for _ in range(3)])
        pTs_small = Ring([sb([128, 64], BF16, p2) for _ in range(6)])
        racc = sb([128, 1024], F32, p2)
        rr = sb([128, 1024], F32, p2)
        ato = Ring([sb([128, 1024], BF16, p2) for _ in range(2)])
        q32 = sb([128, 512], F32, p2)
        qt1 = sb([128, 512], F32, p2)
        qt2 = sb([128, 512], F32, p2)
        rq = sb([128, 512], F32, p2)
        rqt = sb([128, 512], F32, p2)
        sqq = Ring([sb([128, 512], BF16, p2) for _ in range(2)])
        Sb = [(banks[0], banks[1]), (banks[2], banks[3]), (banks[6], banks[7])]
        Ob = (banks[4], banks[5])
        zero_t = sb([128, 512], BF16, p2)
        sc.op("pool", lambda e: e.memset(zero_t[:, :], 0.0), writes=[zero_t.b])

        def compute_q(wq, src, src_b, c0, T, cos_ap, sin_ap, cs_b, qn, qr, d0):
            bA = rot2.next()
            for kc in range(4):
                mm(bA.ap(0, 128, 0, T), wq[:, kc, 0:128], src(kc, c0, T), kc == 0, kc == 3, [wq.b, src_b], [bA.b])
            sq = sqq.next()
            sc.op("act", lambda e: e.activation(out=sq[:, 0:T], in_=bA.ap(0, 128, 0, T), func=AF.Square), reads=[bA.b], writes=[sq.b])
            bB = rot2.next()
            for kc in range(4):
                mm(bB.ap(0, 128, 0, T), wq[:, kc, 128:256], src(kc, c0, T), kc == 0, kc == 3, [wq.b, src_b], [bB.b])
            sc.op("dve", lambda e: e.tensor_tensor(out=qt1[:, 0:T], in0=bB.ap(0, 128, 0, T), in1=cos_ap, op=ALU.mult), reads=[bB.b, cs_b], writes=[qt1.b])
            for kc in range(4):
                mm(bB.ap(0, 128, 0, T), wq[:, kc, 256:384], src(kc, c0, T), kc == 0, kc == 3, [wq.b, src_b], [bB.b])
            sc.op("dve", lambda e: e.tensor_tensor(out=qt2[:, 0:T], in0=bB.ap(0, 128, 0, T), in1=sin_ap, op=ALU.mult), reads=[bB.b, cs_b], writes=[qt2.b])
            sc.op("dve", lambda e: e.tensor_tensor(out=q32[:, 0:T], in0=qt1[:, 0:T], in1=qt2[:, 0:T], op=ALU.add), reads=[qt1.b, qt2.b], writes=[q32.b])
            sq2 = sqq.next()
            sc.op("act", lambda e: e.activation(out=sq2[0:64, 0:T], in_=q32[0:64, 0:T], func=AF.Square), reads=[q32.b], writes=[sq2.b])
            mm(bB.ap(0, 128, 0, T), ones[:, 3, :], sq[:, 0:T], True, False, [ones.b, sq.b], [bB.b])
            mm(bB.ap(0, 128, 0, T), ones[0:64, 3, :], sq2[0:64, 0:T], False, True, [ones.b, sq2.b], [bB.b])
            sc.op("act", lambda e: e.activation(out=rqt[:, 0:T], in_=bB.ap(0, 128, 0, T), func=AF.Sqrt, bias=V("c_eps192"), scale=1.0),
                  reads=[bB.b, vec.b], writes=[rqt.b])
            sc.op("dve", lambda e: e.reciprocal(out=rq[:, 0:T], in_=rqt[:, 0:T]), reads=[rqt.b], writes=[rq.b])
            sc.op("dve", lambda e: e.scalar_tensor_tensor(out=qn[:, d0:d0 + T], in0=bA.ap(0, 128, 0, T), scalar=gg[:, 0:1], in1=rq[:, 0:T],
                                                          op0=ALU.mult, op1=ALU.mult), reads=[bA.b, gg.b, rq.b], writes=[qn.b])
            sc.op("dve", lambda e: e.scalar_tensor_tensor(out=qr[:, d0:d0 + T], in0=q32[:, 0:T], scalar=gg[:, 1:2], in1=rq[:, 0:T],
                                                          op0=ALU.mult, op1=ALU.mult), reads=[q32.b, gg.b, rq.b], writes=[qr.b])

        def split512(a, b):
            out = []
            while a < b:
                e = min(b, (a // 512 + 1) * 512)
                out.append((a, e))
                a = e
            return out

        def attend(h, qn, qr, q0, NQ, steps, at_dst):
            n = len(steps)
            chunks = {}
            small = NQ <= 64
            slots = [(banks[0],), (banks[1],), (banks[2],), (banks[3],)] if small else Sb
            depth = 3 if small else 2

            def load_chunk(ci):
                i0 = ci * 33
                i1 = min(n, i0 + 33)
                ktA = steps[i0][0]
                nkeys = sum(s[1] for s in steps[i0:i1])
                kc_, kr_, vc_ = kch.next(), krch.next(), vch.next()
                sc.dma("sp", kc_[:, 0:nkeys], KT[h, :, ktA * 128:ktA * 128 + nkeys], reads=[scr_b], writes=[kc_.b])
                sc.dma("sp", kr_[0:64, 0:nkeys], KR[:, ktA * 128:ktA * 128 + nkeys], reads=[scr_b], writes=[kr_.b])
                sc.dma("sp", kr_[64:128, 0:nkeys], KR[:, ktA * 128:ktA * 128 + nkeys], reads=[scr_b], writes=[kr_.b])
                sc.dma("sp", vc_[:, 0:i1 - i0, :], V2[h, :, ktA:ktA + (i1 - i0), :], reads=[scr_b], writes=[vc_.b])
                chunks[ci] = (kc_, kr_, vc_)

            def qk(i):
                kt, nk, a, m = steps[i]
                ci, j = divmod(i, 33)
                if ci not in chunks:
                    load_chunk(ci)
                if j == 0 and (ci + 1) * 33 < n and (ci + 1) not in chunks:
                    load_chunk(ci + 1)
                kc_, kr_, _ = chunks[ci]
                S2 = slots[i % len(slots)]
                a2 = a
                if m is not None:
                    bk = S2[a // 512]
                    o = a % 512
                    mm(bk.ap(0, nk, o, o + 128), kc_[:, j * 128:j * 128 + nk], qn[:, q0 + a:q0 + a + 128], True, False, [kc_.b, qn.b], [bk.b])
                    mm(bk.ap(0, nk, o, o + 128), kr_[0:64, j * 128:j * 128 + nk], qr[0:64, q0 + a:q0 + a + 128], False, False, [kr_.b, qr.b], [bk.b])
                    mm(bk.ap(0, nk, o, o + 128), ident[:, 0:nk], mbt[:, m, :], False, True, [ident.b, mbt.b], [bk.b])
                    a2 = a + 128
                grs = split512(a2, NQ)
                for (g0, g1) in grs:
                    bk = S2[g0 // 512]
                    o0, o1 = g0 % 512, g0 % 512 + (g1 - g0)
                    mm(bk.ap(0, nk, o0, o1), kc_[:, j * 128:j * 128 + nk], qn[:, q0 + g0:q0 + g1], True, False, [kc_.b, qn.b], [bk.b])
                for gi, (g0, g1) in enumerate(grs):
                    bk = S2[g0 // 512]
                    o0, o1 = g0 % 512, g0 % 512 + (g1 - g0)
                    r0 = 64 * (gi % 2)
                    mm(bk.ap(0, nk, o0, o1), kr_[r0:r0 + 64, j * 128:j * 128 + nk], qr[r0:r0 + 64, q0 + g0:q0 + g1], False, True, [kr_.b, qr.b], [bk.b])

            pts = {}

            def expo(i):
                kt, nk, a, m = steps[i]
                S2 = slots[i % len(slots)]
                pt = pTs_small.next() if small else pTs.next()
                pts[i] = pt
                src = S2[0].ap(0, nk, a, NQ) if small else pp[S2[0].i // 2][0:nk, a:NQ]
                sc.op("act", lambda e: e.activation(out=pt[0:nk, a:NQ], in_=src, func=AF.Exp, scale=rstdk[0:nk, kt, h:h + 1]),
                      reads=[b_.b for b_ in S2] + [rstdk.b], writes=[pt.b])
                if i == 0:
                    sc.op("dve", lambda e: e.tensor_copy(out=racc[:, 0:NQ], in_=pt[:, 0:NQ]), reads=[pt.b], writes=[racc.b])
                else:
                    sc.op("dve", lambda e: e.tensor_tensor(out=racc[0:nk, a:NQ], in0=racc[0:nk, a:NQ], in1=pt[0:nk, a:NQ], op=ALU.add),
                          reads=[pt.b, racc.b], writes=[racc.b])

            def pv(i):
                kt, nk, a, m = steps[i]
                ci, j = divmod(i, 33)
                _, _, vc_ = chunks[ci]
                pt = pts.pop(i)
                for (g0, g1) in split512(a, NQ):
                    bk = Ob[g0 // 512]
                    o0, o1 = g0 % 512, g0 % 512 + (g1 - g0)
                    mm(bk.ap(0, 128, o0, o1), vc_[0:nk, j, :], pt[0:nk, g0:g1], i == 0, small and i == n - 1, [vc_.b, pt.b], [bk.b])

            for i in range(min(depth, n)):
                qk(i)
            for i in range(n):
                expo(i)
                if i + depth < n:
                    qk(i + depth)
                pv(i)
            if not small:
                for (g0, g1) in split512(0, NQ):
                    bk = Ob[g0 // 512]
                    mm(bk.ap(0, 128, 0, g1 - g0), zero_t[:, 0:128], zero_t[:, 0:g1 - g0], False, True, [zero_t.b], [bk.b])
            for (g0, g1) in split512(0, NQ):
                bk = Sb[0][g0 // 512]
                w = g1 - g0
                mm(bk.ap(0, 128, 0, w), onesf[:, :], racc[:, g0:g1], True, True, [onesf.b, racc.b], [bk.b])
                sc.op("dve", lambda e, bk=bk, g0=g0, g1=g1, w=w: e.reciprocal(out=rr[:, g0:g1], in_=bk.ap(0, 128, 0, w)), reads=[bk.b], writes=[rr.b])
            at = ato.next()
            for (g0, g1) in split512(0, NQ):
                bk = Ob[g0 // 512]
                w = g1 - g0
                sc.op("dve", lambda e, bk=bk, g0=g0, g1=g1, w=w: e.tensor_tensor(out=at[:, g0:g1], in0=bk.ap(0, 128, 0, w), in1=rr[:, g0:g1], op=ALU.mult),
                      reads=[bk.b, rr.b], writes=[at.b])
            sc.dma("sp", at_dst, at[:, 0:NQ], reads=[at.b], writes=[at_scr_b])

        for h in range(NH):
            convert_upto(4 * (h + 1))
            wq = wqs.next()
            for kc in range(4):
                sc.dma("pool", wq[:, kc, 0:192], w_q_up[kc * 128:(kc + 1) * 128, h * 192:(h + 1) * 192], writes=[wq.b])
            sc.op("pool", lambda e: e.tensor_copy(out=wq[:, :, 192:256], in_=wq[:, :, 128:192]), reads=[wq.b], writes=[wq.b])
            sc.op("pool", lambda e: e.tensor_scalar(out=wq[:, :, 256:288], in0=wq[:, :, 160:192], scalar1=-1.0, scalar2=None, op0=ALU.mult),
                  reads=[wq.b], writes=[wq.b])
            sc.op("pool", lambda e: e.tensor_copy(out=wq[:, :, 288:320], in_=wq[:, :, 128:160]), reads=[wq.b], writes=[wq.b])
            sc.op("pool", lambda e: e.tensor_copy(out=wq[:, :, 320:384], in_=wq[:, :, 256:320]), reads=[wq.b], writes=[wq.b])
            qn, qr = qns.next(), qrs.next()
            for t in range(0 if SAMPLE_ONLY else 4):
                c = slice(t * 512, (t + 1) * 512)
                compute_q(wq, lambda kc, c0, T: cqn[:, kc, c0:c0 + T], cqn.b, t * 512, 512, cos_own[:, c], sin_own[:, c], cos_own.b, qn, qr, t * 512)
            compute_q(wq, lambda kc, c0, T: cqn_s[:, kc, c0:c0 + T], cqn_s.b, 0, 64, cos_s[:, 0:64], sin_s[:, 0:64], cos_s.b, qn, qr, TOWN)
            for qh in range(0 if SAMPLE_ONLY else 2):
                steps = []
                for kt in range(64 * (qh + 1)):
                    j0 = max(kt // 8, 8 * qh)
                    m = (kt % 8) if (kt // 8 >= 8 * qh) else None
                    steps.append((kt, 128, (j0 - 8 * qh) * 128, m))
                attend(h, qn, qr, 1024 * qh, 1024, steps, AT[h, :, 1024 * qh:1024 * (qh + 1)])
            for b in range(2):
                steps = [(128 + 33 * b + i, 128 if i < 32 else 32, 0, None) for i in range(33)]
                attend(h, qn, qr, TOWN + 32 * b, 32, steps, AT[h, :, TOWN + 32 * b:TOWN + 32 * b + 32])
        free(p2)
    free(p12)

    if "p3" in PHASES:
        convert_upto(62)
        sc.barrier()
        p3 = []
        wb = Ring([sb([128, 8192], BF16, p3) for _ in range(3)])

        def wload(bi):
            t = wb.next()
            n = max(off + KC * ncols for (off, w, KC, c0, ncols) in wblocks[bi])
            sc.dma("pool" if (bi % 2) else "sp", t[:, 0:n], WB[bi, :, 0:n], reads=[wb_scr_b], writes=[t.b])
            views = [t[:, off:off + KC * ncols].rearrange("p (k c) -> p k c", c=ncols) for (off, w, KC, c0, ncols) in wblocks[bi]]
            return views, t.b

        xt = sb([128, 16, 512], F32, p3)
        hT = sb([128, 16, 512], BF16, p3)
        hm = sb([128, 128], F32, p3)
        arena = sb([128, 11264], F32, p3)
        upad = Tl(arena.t[:, 0:5120].rearrange("p (a b c) -> p a b c", a=8, b=4))
        ycv = Tl(arena.t[:, 5120:9216].rearrange("p (a c) -> p a c", a=8))
        ybf = sb([128, 8, 512], BF16, p3)
        zc = ybf
        att_raw = sb([128, 8192], BF16, p3)
        att = Tl(att_raw.t[:, :].rearrange("p (h t) -> p h t", t=512))
        xh = Tl(att_raw.t[:, 0:4096].bitcast(F32).rearrange("p (k t) -> p k t", t=128))
        hhT = Tl(att_raw.t[:, 4096:6144].rearrange("p (k t) -> p k t", t=128))
        att.b = xh.b = hhT.b = att_raw.b
        z2 = sb([128, 16, 512], BF16, p3)
        act_t = Tl(arena.t[:, :].bitcast(BF16).rearrange("p (a c) -> p a c", a=44))
        st3 = {"rs": sb([128, 512], F32, p3), "rtmp": sb([128, 512], F32, p3)}
        ident3 = sb([128, 128], BF16, p3)
        sc.dma("pool", ident3[:], ident_in, writes=[ident3.b])
        diags = Ring([sb([128, 128], BF16, p3) for _ in range(4)])
        upbs = Ring([sb([128, 4, 160], BF16, p3) for _ in range(2)])
        sg = Ring([sb([128, 512], F32, p3) for _ in range(2)])
        tA = Ring([sb([128, 512], F32, p3) for _ in range(1)])
        tB = Ring([sb([128, 512], F32, p3) for _ in range(1)])
        ln_m = st3["rs"]
        ln_r = st3["rtmp"]
        yst = Ring([sb([128, 512], F32, p3) for _ in range(1)])

        def post(x_ap, T, J, L, halo_ap, hm_ap, state, at_c0, y_out, u_out):
            sc.dma("sp", xt[:, :, 0:T], x_ap.rearrange("(k p) t -> p k t", p=128), writes=[xt.b])
            rms_h(xt, hT, T, st3)
            HT = J * 32
            if halo_ap is not None:
                sc.dma("sp", xh[:, :, 0:HT], halo_ap.rearrange("(k p) t -> p k t", p=128), writes=[xh.b])
                sc.dma("sp", hm[:, 0:HT], hm_ap.partition_broadcast(128), writes=[hm.b])
                rms_h(xh, hhT, HT, st3)
            else:
                for b in range(J):
                    sc.dma("sp", upad[:, :, b, 2:32], state[b].rearrange("(k p) r -> p k r", p=128), writes=[upad.b])
            for g in range(2):
                (wa,), wab = wload(2 * g)
                (wg,), wgb = wload(2 * g + 1)
                for o in range(4):
                    oc = g * 4 + o
                    bA, bG = rot7.next(), rot7.next()
                    for kc in range(16):
                        mm(bA.ap(0, 128, 0, T), wa[:, kc, o * 128:(o + 1) * 128], hT[:, kc, 0:T], kc == 0, kc == 15, [wab, hT.b], [bA.b])
                    for kc in range(16):
                        mm(bG.ap(0, 128, 0, T), wg[:, kc, o * 128:(o + 1) * 128], hT[:, kc, 0:T], kc == 0, kc == 15, [wgb, hT.b], [bG.b])
                    s_ = sg.next()
                    sc.op("act", lambda e, bG=bG, s_=s_, oc=oc: e.activation(out=s_[:, 0:T], in_=bG.ap(0, 128, 0, T), func=AF.Sigmoid,
                                                                            bias=V("b_glu", 8 + oc), scale=1.0), reads=[bG.b, vec.b], writes=[s_.b])
                    sc.op("dve", lambda e, bA=bA, s_=s_, oc=oc: e.scalar_tensor_tensor(
                        out=upad[:, oc, 0:J, 32:32 + L], in0=bA.ap(0, 128, 0, T).rearrange("p (j l) -> p j l", l=L), scalar=V("b_glu", oc),
                        in1=s_[:, 0:T].rearrange("p (j l) -> p j l", l=L), op0=ALU.add, op1=ALU.mult), reads=[bA.b, s_.b, vec.b], writes=[upad.b])
                    if halo_ap is not None:
                        bA, bG = rot7.next(), rot7.next()
                        for kc in range(16):
                            mm(bA.ap(0, 128, 0, HT), wa[:, kc, o * 128:(o + 1) * 128], hhT[:, kc, 0:HT], kc == 0, kc == 15, [wab, hhT.b], [bA.b])
                        for kc in range(16):
                            mm(bG.ap(0, 128, 0, HT), wg[:, kc, o * 128:(o + 1) * 128], hhT[:, kc, 0:HT], kc == 0, kc == 15, [wgb, hhT.b], [bG.b])
                        s_ = sg.next()
                        sc.op("act", lambda e, bG=bG, s_=s_, oc=oc: e.activation(out=s_[:, 0:HT], in_=bG.ap(0, 128, 0, HT), func=AF.Sigmoid,
                                                                                bias=V("b_glu", 8 + oc), scale=1.0), reads=[bG.b, vec.b], writes=[s_.b])
                        t_ = tA.next()
                        sc.op("dve", lambda e, bA=bA, s_=s_, oc=oc, t_=t_: e.scalar_tensor_tensor(
                            out=t_[:, 0:HT], in0=bA.ap(0, 128, 0, HT), scalar=V("b_glu", oc), in1=s_[:, 0:HT], op0=ALU.add, op1=ALU.mult),
                            reads=[bA.b, s_.b, vec.b], writes=[t_.b])
                        sc.op("dve", lambda e, oc=oc, t_=t_: e.tensor_tensor(
                            out=upad[:, oc, 0:J, 0:32], in0=t_[:, 0:HT].rearrange("p (j l) -> p j l", l=32),
                            in1=hm[:, 0:HT].rearrange("p (j l) -> p j l", l=32), op=ALU.mult), reads=[t_.b, hm.b], writes=[upad.b])
            if u_out is not None:
                u_out()
            sc.dma("sp", att[:, :, 0:T], AT[:, :, at_c0:at_c0 + T].rearrange("h d t -> d h t"), reads=[at_scr_b], writes=[att.b])
            for oc in range(8):
                upb = upbs.next()
                sc.op("pool", lambda e, oc=oc, upb=upb: e.tensor_copy(out=upb[:, 0:J, 2:32 + L], in_=upad[:, oc, 0:J, 2:32 + L]), reads=[upad.b], writes=[upb.b])
                bk = rot7.next()
                for k in range(CW):
                    dg = diags.next()
                    wcol = vec[:, VO["w_dw"] + oc * 31 + k:VO["w_dw"] + oc * 31 + k + 1]
                    sc.op("dve", lambda e, dg=dg, wcol=wcol: e.tensor_scalar(out=dg[:, :], in0=ident3[:, :], scalar1=wcol, scalar2=None, op0=ALU.mult),
                          reads=[ident3.b, vec.b], writes=[dg.b])
                    mm(bk.ap(0, 128, 0, T).rearrange("p (j l) -> p j l", l=L), dg[:, :], upb[:, 0:J, 2 + k:2 + k + L], k == 0, k == CW - 1,
                       [dg.b, upb.b], [bk.b])
                sc.op("act", lambda e, bk=bk, oc=oc: e.activation(out=ycv[:, oc, 0:T], in_=bk.ap(0, 128, 0, T), func=AF.Identity, bias=V("b_dw", oc), scale=1.0),
                      reads=[bk.b, vec.b], writes=[ycvb[oc]])
            b1, b2 = rot7.next(), rot7.next()
            for oc in range(8):
                sc.op("pool", lambda e, oc=oc: e.tensor_copy(out=ybf[:, oc, 0:T], in_=ycv[:, oc, 0:T]), reads=[ycvb[oc]], writes=[ybf.b])
                sq = sqr.next()
                sc.op("act", lambda e, oc=oc, sq=sq: e.activation(out=sq[:, 0:T], in_=ycv[:, oc, 0:T], func=AF.Square), reads=[ycvb[oc]], writes=[sq.b])
                mm(b1.ap(0, 128, 0, T), ones[:, 2, :], ybf[:, oc, 0:T], oc == 0, oc == 7, [ones.b, ybf.b], [b1.b])
                mm(b2.ap(0, 128, 0, T), ones[:, 2, :], sq[:, 0:T], oc == 0, oc == 7, [ones.b, sq.b], [b2.b])
            sc.op("dve", lambda e: e.tensor_copy(out=ln_m[:, 0:T], in_=b1.ap(0, 128, 0, T)), reads=[b1.b], writes=[ln_m.b])
            t_ = tA.next()
            sc.op("dve", lambda e: e.tensor_tensor(out=t_[:, 0:T], in0=ln_m[:, 0:T], in1=ln_m[:, 0:T], op=ALU.mult), reads=[ln_m.b], writes=[t_.b])
            t2_ = tB.next()
            sc.op("dve", lambda e: e.tensor_tensor(out=t2_[:, 0:T], in0=b2.ap(0, 128, 0, T), in1=t_[:, 0:T], op=ALU.subtract), reads=[b2.b, t_.b], writes=[t2_.b])
            sc.op("dve", lambda e: e.tensor_scalar(out=t2_[:, 0:T], in0=t2_[:, 0:T], scalar1=0.0, scalar2=None, op0=ALU.max), reads=[t2_.b], writes=[t2_.b])
            sc.op("act", lambda e: e.activation(out=t_[:, 0:T], in_=t2_[:, 0:T], func=AF.Sqrt, bias=V("c_eps"), scale=1.0), reads=[t2_.b, vec.b], writes=[t_.b])
            sc.op("dve", lambda e: e.reciprocal(out=ln_r[:, 0:T], in_=t_[:, 0:T]), reads=[t_.b], writes=[ln_r.b])
            for oc in range(8):
                t_ = tA.next()
                sc.op("dve", lambda e, oc=oc, t_=t_: e.tensor_tensor(out=t_[:, 0:T], in0=ycv[:, oc, 0:T], in1=ln_m[:, 0:T], op=ALU.subtract),
                      reads=[ycvb[oc], ln_m.b], writes=[t_.b])
                t2_ = tB.next()
                sc.op("dve", lambda e, t_=t_, t2_=t2_: e.tensor_tensor(out=t2_[:, 0:T], in0=t_[:, 0:T], in1=ln_r[:, 0:T], op=ALU.mult),
                      reads=[t_.b, ln_r.b], writes=[t2_.b])
                sc.op("act", lambda e, oc=oc, t2_=t2_: e.activation(out=zc[:, oc, 0:T], in_=t2_[:, 0:T], func=AF.Silu, bias=V("b_ln", oc), scale=V("g_ln", oc)),
                      reads=[t2_.b, vec.b], writes=[zc.b])
            for oc in range(16):
                (wao, wco, wga, wgb_), wtb = wload(4 + oc)
                bYa, bYb, bGa, bGb = rot7.next(), rot7.next(), rot7.next(), rot7.next()
                for kc in range(16):
                    mm(bYa.ap(0, 128, 0, T), wao[:, kc, :], att[:, kc, 0:T], kc == 0, kc == 15, [wtb, att.b], [bYa.b])
                for kc in range(8):
                    mm(bYb.ap(0, 128, 0, T), wco[:, kc, :], zc[:, kc, 0:T], kc == 0, kc == 7, [wtb, zc.b], [bYb.b])
                for kc in range(16):
                    mm(bGa.ap(0, 128, 0, T), wga[:, kc, :], hT[:, kc, 0:T], kc == 0, kc == 15, [wtb, hT.b], [bGa.b])
                for kc in range(16):
                    mm(bGb.ap(0, 128, 0, T), wgb_[:, kc, :], hT[:, kc, 0:T], kc == 0, kc == 15, [wtb, hT.b], [bGb.b])
                sa, sb_ = sg.next(), sg.next()
                sc.op("act", lambda e, bGa=bGa, sa=sa, oc=oc: e.activation(out=sa[:, 0:T], in_=bGa.ap(0, 128, 0, T), func=AF.Sigmoid,
                                                                          bias=V("b_gate", oc), scale=1.0), reads=[bGa.b, vec.b], writes=[sa.b])
                sc.op("act", lambda e, bGb=bGb, sb_=sb_, oc=oc: e.activation(out=sb_[:, 0:T], in_=bGb.ap(0, 128, 0, T), func=AF.Sigmoid,
                                                                            bias=V("b_gate", 16 + oc), scale=1.0), reads=[bGb.b, vec.b], writes=[sb_.b])
                t_, t2_ = tA.next(), tB.next()
                sc.op("dve", lambda e, bYa=bYa, sa=sa, t_=t_: e.tensor_tensor(out=t_[:, 0:T], in0=bYa.ap(0, 128, 0, T), in1=sa[:, 0:T], op=ALU.mult),
                      reads=[bYa.b, sa.b], writes=[t_.b])
                sc.op("dve", lambda e, bYb=bYb, sb_=sb_, t2_=t2_, oc=oc: e.scalar_tensor_tensor(
                    out=t2_[:, 0:T], in0=bYb.ap(0, 128, 0, T), scalar=V("b_co", oc), in1=sb_[:, 0:T], op0=ALU.add, op1=ALU.mult),
                    reads=[bYb.b, sb_.b, vec.b], writes=[t2_.b])
                sc.op("dve", lambda e, t_=t_, t2_=t2_, oc=oc: e.tensor_tensor(out=z2[:, oc, 0:T], in0=t_[:, 0:T], in1=t2_[:, 0:T], op=ALU.add),
                      reads=[t_.b, t2_.b], writes=[z2.b])
            for g in range(4):
                (wo,), wob = wload(20 + g)
                for o in range(4):
                    oc = g * 4 + o
                    bk = rot7.next()
                    for kc in range(16):
                        mm(bk.ap(0, 128, 0, T), wo[:, kc, o * 128:(o + 1) * 128], z2[:, kc, 0:T], kc == 0, kc == 15, [wob, z2.b], [bk.b])
                    sc.op("dve", lambda e, bk=bk, oc=oc: e.tensor_tensor(out=xt[:, oc, 0:T], in0=xt[:, oc, 0:T], in1=bk.ap(0, 128, 0, T), op=ALU.add),
                          reads=[bk.b, xt.b], writes=[xt.b])
            sc.barrier()
            bk = rot7.next()
            for kc in range(16):
                sq = sqr.next()
                sc.op("act", lambda e, kc=kc, sq=sq: e.activation(out=sq[:, 0:T], in_=xt[:, kc, 0:T], func=AF.Square), reads=[xt.b], writes=[sq.b])
                mm(bk.ap(0, 128, 0, T), ones[:, 0, :], sq[:, 0:T], kc == 0, kc == 15, [ones.b, sq.b], [bk.b])
            ps_ap_b[0] = [bk.b]
            rsqrt_bc(bk.ap(0, 128, 0, T), "c_eps", st3["rs"], 128, T, st3["rtmp"])
            rs = st3["rs"]
            for kc in range(16):
                sc.op("dve", lambda e, kc=kc: e.scalar_tensor_tensor(out=hT[:, kc, 0:T], in0=xt[:, kc, 0:T], scalar=V("g_ffn", kc), in1=rs[:, 0:T],
                                                                     op0=ALU.mult, op1=ALU.mult), reads=[xt.b, rs.b, vec.b], writes=[hT.b])
            for g in range(11):
                (wg_,), wgb2 = wload(24 + 2 * g)
                (wu_,), wub2 = wload(25 + 2 * g)
                for o in range(4):
                    fc = g * 4 + o
                    bG, bU = rot7.next(), rot7.next()
                    for kc in range(16):
                        mm(bG.ap(0, 128, 0, T), wg_[:, kc, o * 128:(o + 1) * 128], hT[:, kc, 0:T], kc == 0, kc == 15, [wgb2, hT.b], [bG.b])
                    for kc in range(16):
                        mm(bU.ap(0, 128, 0, T), wu_[:, kc, o * 128:(o + 1) * 128], hT[:, kc, 0:T], kc == 0, kc == 15, [wub2, hT.b], [bU.b])
                    s_ = sg.next()
                    sc.op("act", lambda e, bG=bG, s_=s_: e.activation(out=s_[:, 0:T], in_=bG.ap(0, 128, 0, T), func=AF.Silu), reads=[bG.b], writes=[s_.b])
                    sc.op("dve", lambda e, bU=bU, s_=s_, fc=fc: e.tensor_tensor(out=act_t[:, fc, 0:T], in0=bU.ap(0, 128, 0, T), in1=s_[:, 0:T], op=ALU.mult),
                          reads=[bU.b, s_.b], writes=[act_t.b])
            for oc in range(16):
                (wd,), wdb = wload(46 + oc)
                bk = rot7.next()
                for fc in range(44):
                    mm(bk.ap(0, 128, 0, T), wd[:, fc, :], act_t[:, fc, 0:T], fc == 0, fc == 43, [wdb, act_t.b], [bk.b])
                ys = yst.next()
                sc.op("dve", lambda e, bk=bk, oc=oc, ys=ys: e.tensor_tensor(out=ys[:, 0:T], in0=xt[:, oc, 0:T], in1=bk.ap(0, 128, 0, T), op=ALU.add),
                      reads=[bk.b, xt.b], writes=[ys.b])
                sc.dma("sp", y_out[oc * 128:(oc + 1) * 128, :], ys[:, 0:T], reads=[ys.b])
            sc.barrier()

        ycvb = [Buf() for _ in range(8)]
        DBG.update(z2=z2, zc=zc, att=att, xt=xt, hT=hT, arena=arena)
        for t in range(0 if SAMPLE_ONLY else 4):
            c = slice(t * 512, (t + 1) * 512)
            hc = slice(t * 128, (t + 1) * 128)
            uo = None
            if t == 3:
                def uo():
                    sc.dma("sp", uT_last.rearrange("(k p) t -> p k t", p=128), upad[:, :, 3, 128:160], reads=[upad.b])
            post(xT_own[:, c], 512, 4, 128, xT_halo[:, hc], hmask[:, hc], None, t * 512, yT_own[:, c], uo)

        def uo_s():
            for b in range(2):
                sc.dma("sp", uT_s[:, 32 * b:32 * b + 32].rearrange("(k p) t -> p k t", p=128), upad[:, :, b, 32:64], reads=[upad.b])
        post(xT_s[:, 0:64], 64, 2, 32, None, None, [sconvT[0], sconvT[1]], TOWN, yT_s[:, 0:64], uo_s)
        free(p3)

    sc.barrier()
    return nc


_CACHE = {}


def _prep_inputs(inp):
    f = np.float32
    xp = np.asarray(inp["x_prompt"], f)[0]
    xs = np.asarray(inp["x_sample"], f)
    xT_all = np.ascontiguousarray(xp.T)
    xt = xp.reshape(128, 128, D)
    vecs = np.zeros((128, NV), f)

    def put(name, arr):
        arr = np.asarray(arr, f)
        vecs[:arr.shape[0], VO[name]:VO[name] + arr.shape[1]] = arr

    col = lambda v: np.ascontiguousarray(np.asarray(v, f).reshape(-1, 128).T)
    put("g_mix", col(inp["g_mix_norm"][0]))
    put("g_q_a", col(inp["g_q_a"][0]))
    put("g_kv_a", col(inp["g_kv_a"][0]))
    put("b_glu", col(inp["b_glu"][0]))
    put("b_gate", col(inp["b_gate"][0]))
    wdw = np.asarray(inp["w_dw"], f)[0]
    put("w_dw", np.ascontiguousarray(wdw.T.reshape(8, 128, 31).transpose(1, 0, 2).reshape(128, 248)))
    put("b_dw", col(inp["b_dw"][0]))
    put("g_ln", col(inp["g_conv_ln"][0]))
    put("b_ln", col(inp["b_conv_ln"][0]))
    put("b_co", col(inp["b_conv_out"][0]))
    put("g_ffn", col(inp["g_ffn_norm"][0]))
    for nm, key in (("gq", "g_q_norm"), ("gk", "g_k_norm")):
        g = np.asarray(inp[key], f)[0]
        a = np.ones((128, 2), f)
        a[:, 0] = g[0:128]
        a[0:64, 1] = g[128:192]
        a[64:128, 1] = g[128:192]
        put(nm, a)
    invf = (1.0 / (np.float32(10000.0) ** (np.arange(0, 64, 2, dtype=np.float32) / np.float32(64)))).astype(f)
    iv = np.zeros((128, 1), f)
    iv[0:64, 0] = np.concatenate([invf, invf])
    iv[64:128, 0] = np.concatenate([invf, invf])
    put("invf", iv)
    put("c_eps", np.full((128, 1), EPS, f))
    put("c_eps192", np.full((128, 1), 192 * EPS, f))

    ident = np.eye(128, dtype=f)
    common = {
        "xT_all": xT_all, "vecs": vecs, "ident": ident,
        "pos_all": np.arange(S, dtype=f)[None, :],
        "pos_s": np.tile(np.arange(PAST, PAST + TS, dtype=f), 2)[None, :],
        "w_in": np.ascontiguousarray(inp["w_in"][0], f), "w_q_up": np.ascontiguousarray(inp["w_q_up"][0], f),
        "w_kv_up": np.ascontiguousarray(inp["w_kv_up"][0], f), "w_attn_out": np.ascontiguousarray(inp["w_attn_out"][0], f),
        "w_conv_out": np.ascontiguousarray(inp["w_conv_out"][0], f), "w_out": np.ascontiguousarray(inp["w_out"][0], f),
        "w_ffn_gate": np.ascontiguousarray(inp["w_ffn_gate"][0], f), "w_ffn_up": np.ascontiguousarray(inp["w_ffn_up"][0], f),
        "w_ffn_down": np.ascontiguousarray(inp["w_ffn_down"][0], f),
    }
    maps = []
    for c in range(NCORE):
        gt = np.arange(16) * 8 + c
        own = xt[gt].reshape(TOWN, D)
        halo = np.zeros((16, 32, D), f)
        hmask = np.ones((16, 32), f)
        for j, g in enumerate(gt):
            if g == 0:
                hmask[j] = 0.0
            else:
                halo[j] = xt[g - 1, 96:128]
        pos_own = (gt[:, None] * 128 + np.arange(128)[None, :]).reshape(1, TOWN).astype(f)
        mb = np.zeros((128, 8, 128), f)
        for m in range(8):
            if m > c:
                mb[:, m, :] = -30000.0
            elif m == c:
                mb[64:128, m, 0:64] = -30000.0
        d = dict(common)
        d.update({
            "xT_own": np.ascontiguousarray(own.T), "xT_halo": np.ascontiguousarray(halo.reshape(512, D).T),
            "xT_s": np.ascontiguousarray(xs[2 * c:2 * c + 2].reshape(64, D).T),
            "pos_own": pos_own, "hmask": hmask.reshape(1, 512), "mb": mb.reshape(128, 1024),
            "cckvT": np.ascontiguousarray(np.asarray(inp["cache_ckv"], f)[0, 2 * c:2 * c + 2].transpose(0, 2, 1)),
            "ckpeT": np.ascontiguousarray(np.asarray(inp["cache_kpe"], f)[0, 2 * c:2 * c + 2].transpose(0, 2, 1)),
            "sconvT": np.ascontiguousarray(np.asarray(inp["state_conv"], f)[0, 2 * c:2 * c + 2].transpose(0, 2, 1)),
        })
        maps.append(d)
    return maps


def kernel(**inp):
    if "nc" not in _CACHE:
        _CACHE["nc"] = build_program()
    nc = _CACHE["nc"]
    maps = _prep_inputs(inp)
    res = run_bass_kernel_spmd(nc, maps, core_ids=list(range(NCORE)))
    R = res.results
    f = np.float32
    y_p = np.zeros((1, S, D), f)
    ckv_p = np.zeros((1, 1, S, 512), f)
    kpe_p = np.zeros((1, 1, S, 64), f)
    y_s = np.zeros((16, TS, D), f)
    ckv_s = np.zeros((1, 16, TS, 512), f)
    kpe_s = np.zeros((1, 16, TS, 64), f)
    conv_s = np.zeros((1, 16, 30, CONV), f)
    for c in range(NCORE):
        r = R[c]
        gt = np.arange(16) * 8 + c
        idx = (gt[:, None] * 128 + np.arange(128)[None, :]).reshape(-1)
        y_p[0, idx] = np.asarray(r["yT_own"]).T
        ckv_p[0, 0, idx] = np.asarray(r["ckvT_own"]).T
        kpe_p[0, 0, idx] = np.asarray(r["kpeT_own"]).T
        y_s[2 * c:2 * c + 2] = np.asarray(r["yT_s"]).T.reshape(2, TS, D)
        ckv_s[0, 2 * c:2 * c + 2] = np.asarray(r["ckvT_s"]).T.reshape(2, TS, 512)
        kpe_s[0, 2 * c:2 * c + 2] = np.asarray(r["kpeT_s"]).T.reshape(2, TS, 64)
        us = np.asarray(r["uT_s"]).T.reshape(2, TS, CONV)
        conv_s[0, 2 * c:2 * c + 2] = us[:, 2:32]
    conv_p = np.asarray(R[7]["uT_last"]).T[2:32][None, None]
    return (y_p, y_s, ckv_p, kpe_p, conv_p.astype(f), ckv_s, kpe_s, conv_s)
```

```python
import numpy as np
import concourse.bass as bass
import concourse.mybir as mybir
from concourse.bass_utils import run_bass_kernel_spmd

F32 = mybir.dt.float32
BF16 = mybir.dt.bfloat16
AF = mybir.ActivationFunctionType
ALU = mybir.AluOpType

NCORE = 8
D = 2048
S = 16384
TOWN = 2048
NH = 16
IND = 7232
DFF = 5632
CONV = 1024
CW = 31
PAST = 4096
TS = 32
EPS = 1e-6
NKT = 128 + 2 * 33
NK = NKT * 128
MAGIC = 12582912.0
C1 = 6.28125
C2 = 0.0019353071795864769
PI = 3.1415925

VO = {}
_o = 0
for _n, _w in (("g_mix", 16), ("g_q_a", 4), ("g_kv_a", 4), ("b_glu", 16), ("b_gate", 32), ("w_dw", 248),
               ("b_dw", 8), ("g_ln", 8), ("b_ln", 8), ("b_co", 16), ("g_ffn", 16), ("gq", 2), ("gk", 2),
               ("invf", 1), ("c_eps", 1), ("c_eps192", 1)):
    VO[_n] = _o
    _o += _w
NV = _o

import os
PHASES = set(os.environ.get("KPHASES", "p1a,p1b,p2,p3").split(","))
SAMPLE_ONLY = False
DBG = {}


class Buf:
    __slots__ = ("w", "r", "x")

    def __init__(self, x=False):
        self.w = {}
        self.r = {}
        self.x = x


class Sched:
    def __init__(self, nc):
        self.nc = nc
        self.eng = {"pe": nc.tensor, "act": nc.scalar, "dve": nc.vector, "pool": nc.gpsimd, "sp": nc.sync}
        self.sems = {}
        self.cnt = {}
        self.seen = {e: {} for e in self.eng}
        for e in ("pe", "act", "dve", "pool"):
            self.sems[e] = nc.semaphore("s_" + e).__enter__()
            self.cnt[e] = 0
        self.ND = 8
        self.dq = {}
        for q in ("sp", "pool"):
            names = [f"d_{q}{i}" for i in range(self.ND)]
            for n in names:
                self.sems[n] = nc.semaphore(n).__enter__()
                self.cnt[n] = 0
            self.dq[q] = [names, 0]

    def _wait(self, e, key, val):
        if self.seen[e].get(key, 0) >= val:
            return
        self.eng[e].wait_ge(self.sems[key], val)
        self.seen[e][key] = val

    def _deps(self, e, reads, writes):
        need = {}

        def add(k, v, war=False):
            if k == e and e == "pe":
                return
            if need.get(k, 0) < v:
                need[k] = v

        for b in reads:
            for k, v in b.w.items():
                add(k, v)
        for b in writes:
            for k, v in b.w.items():
                add(k, v)
            for k, v in b.r.items():
                add(k, v, True)
        for k, v in need.items():
            self._wait(e, k, v)

    def _mark(self, tok, reads, writes):
        k, v = tok
        for b in reads:
            if b.r.get(k, 0) < v:
                b.r[k] = v
        for b in writes:
            if b.w.get(k, 0) < v:
                b.w[k] = v

    def op(self, e, fn, reads=(), writes=()):
        writes = list(writes) + [b for b in reads if b.x]
        reads = [b for b in reads if not b.x]
        self._deps(e, reads, writes)
        ins = fn(self.eng[e])
        self.cnt[e] += 1
        ins.then_inc(self.sems[e], 1)
        tok = (e, self.cnt[e])
        self._mark(tok, reads, writes)
        return tok

    def dma(self, q, out, in_, reads=(), writes=()):
        self._deps(q, reads, writes)
        names, i = self.dq[q]
        n = names[i]
        self.dq[q][1] = (i + 1) % self.ND
        if self.cnt[n] > 0:
            self._wait(q, n, self.cnt[n])
        ins = self.eng[q].dma_start(out=out, in_=in_)
        self.cnt[n] += 16
        ins.then_inc(self.sems[n], 16)
        tok = (n, self.cnt[n])
        self._mark(tok, reads, writes)
        return tok

    def barrier(self):
        for e in self.eng:
            for k, v in self.cnt.items():
                if k != e and v > 0:
                    self._wait(e, k, v)


class Tl:
    def __init__(self, t):
        self.t = t
        self.b = Buf()

    def __getitem__(self, k):
        return self.t[k]


class Ring:
    def __init__(self, tiles):
        self.tiles = tiles
        self.i = 0

    def next(self):
        t = self.tiles[self.i]
        self.i = (self.i + 1) % len(self.tiles)
        return t


def build_program():
    nc = bass.Bass("TRN2", target_bir_lowering=False)
    sc = Sched(nc)
    ctx = []

    def dram_in(name, shape):
        return nc.dram_tensor(name, list(shape), F32, kind="ExternalInput").ap()

    def dram_out(name, shape):
        return nc.dram_tensor(name, list(shape), F32, kind="ExternalOutput").ap()

    xT_all = dram_in("xT_all", [D, S])
    xT_own = dram_in("xT_own", [D, TOWN])
    xT_halo = dram_in("xT_halo", [D, 512])
    xT_s = dram_in("xT_s", [D, 64])
    pos_all = dram_in("pos_all", [1, S])
    pos_own = dram_in("pos_own", [1, TOWN])
    pos_s = dram_in("pos_s", [1, 64])
    hmask = dram_in("hmask", [1, 512])
    mb_in = dram_in("mb", [128, 1024])
    ident_in = dram_in("ident", [128, 128])
    cckvT = dram_in("cckvT", [2, 512, PAST])
    ckpeT = dram_in("ckpeT", [2, 64, PAST])
    sconvT = dram_in("sconvT", [2, CONV, 30])
    vecs_in = dram_in("vecs", [128, NV])
    w_in = dram_in("w_in", [D, IND])
    w_q_up = dram_in("w_q_up", [512, 3072])
    w_kv_up = dram_in("w_kv_up", [512, 4096])
    w_ao = dram_in("w_attn_out", [D, D])
    w_co = dram_in("w_conv_out", [CONV, D])
    w_o = dram_in("w_out", [D, D])
    w_fg = dram_in("w_ffn_gate", [D, DFF])
    w_fu = dram_in("w_ffn_up", [D, DFF])
    w_fd = dram_in("w_ffn_down", [DFF, D])

    yT_own = dram_out("yT_own", [D, TOWN])
    ckvT_own = dram_out("ckvT_own", [512, TOWN])
    kpeT_own = dram_out("kpeT_own", [64, TOWN])
    uT_last = dram_out("uT_last", [CONV, 32])
    yT_s = dram_out("yT_s", [D, 64])
    ckvT_s = dram_out("ckvT_s", [512, 64])
    kpeT_s = dram_out("kpeT_s", [64, 64])
    uT_s = dram_out("uT_s", [CONV, 64])

    KT = nc.dram_tensor("KT_scr", [NH, 128, NK], BF16).ap()
    KR = nc.dram_tensor("KR_scr", [64, NK], BF16).ap()
    V2 = nc.dram_tensor("V2_scr", [NH, 128, NKT, 128], BF16).ap()
    AT = nc.dram_tensor("AT_scr", [NH, 128, TOWN + 64], BF16).ap()
    WB = nc.dram_tensor("WB_scr", [62, 128, 8192], BF16).ap()

    cnt = [0]

    def sb(shape, dt, stack=None):
        cnt[0] += 1
        g = nc.sbuf_tensor(f"t{cnt[0]}", list(shape), dt)
        t = g.__enter__()
        (stack if stack is not None else ctx).append(g)
        return Tl(t)

    def free(stack):
        sc.barrier()
        while stack:
            stack.pop().__exit__(None, None, None)

    pp = []
    for i in range(4):
        g = nc.psum_tensor(f"pp{i}", [128, 1024], F32)
        pp.append(g.__enter__())
        ctx.append(g)
    bankb = [Buf(True) for _ in range(8)]

    class Bank:
        def __init__(self, i):
            self.i = i
            self.b = bankb[i]

        def ap(self, p0, p1, c0, c1):
            return pp[self.i // 2][p0:p1, (self.i % 2) * 512 + c0:(self.i % 2) * 512 + c1]

    banks = [Bank(i) for i in range(8)]
    rot7 = Ring(banks[0:7])
    rot2 = Ring(banks[6:8])

    def mm(out, lhsT, rhs, start, stop, R, W):
        return sc.op("pe", lambda e: e.matmul(out, lhsT=lhsT, rhs=rhs, start=start, stop=stop), reads=R, writes=W)

    vec = sb([128, NV], F32)
    ones = sb([128, 4, 128], BF16)
    onesf = sb([128, 128], F32)
    cos_s = sb([128, 64], F32)
    sin_s = sb([128, 64], F32)
    cqn_s = sb([128, 4, 64], BF16)
    ckvn_s = sb([128, 4, 64], BF16)
    kpe_s = sb([64, 64], BF16)
    sqr = Ring([sb([128, 512], BF16) for _ in range(3)])
    p12 = []
    scr_b = Buf()
    at_scr_b = Buf()
    ident = sb([128, 128], BF16, p12)
    mbt = sb([128, 8, 128], BF16, p12)
    gg = sb([128, 2], F32, p12)
    rstdk = sb([128, NKT, NH], F32, p12)
    cos_own = sb([128, TOWN], F32, p12)
    sin_own = sb([128, TOWN], F32, p12)
    cqn = sb([128, 4, TOWN], BF16, p12)

    sc.dma("sp", vec[:], vecs_in, writes=[vec.b])

    def V(name, i=0, p1=128):
        o = VO[name] + i
        return vec[0:p1, o:o + 1]

    for i, v in enumerate((1.0 / 2048, 1.0 / 512, 1.0 / 1024, 1.0)):
        sc.op("pool", lambda e, i=i, v=v: e.memset(ones[:, i, :], v), writes=[ones.b])
    sc.op("pool", lambda e: e.memset(onesf[:], 1.0), writes=[onesf.b])
    sc.dma("pool", ident[:], ident_in, writes=[ident.b])
    sc.dma("pool", mbt[:].rearrange("p a b -> p (a b)"), mb_in, writes=[mbt.b])
    sc.op("dve", lambda e: e.scalar_tensor_tensor(out=gg[:], in0=vec[:, VO["gq"]:VO["gq"] + 2], scalar=float(np.sqrt(192.0)),
                                                  in1=vec[:, VO["gk"]:VO["gk"] + 2], op0=ALU.mult, op1=ALU.mult),
          reads=[vec.b], writes=[gg.b])
    sc.op("pool", lambda e: e.memset(rstdk[:], 1.0), writes=[rstdk.b])

    def rsqrt_bc(ps_ap, epsname, out_tl, np_, T, tmp_tl):
        sc.op("act", lambda e: e.activation(out=tmp_tl[0:np_, 0:T], in_=ps_ap, func=AF.Sqrt, bias=V(epsname, 0, np_), scale=1.0),
              reads=[vec.b] + ps_ap_b[0], writes=[tmp_tl.b])
        sc.op("dve", lambda e: e.reciprocal(out=out_tl[0:np_, 0:T], in_=tmp_tl[0:np_, 0:T]), reads=[tmp_tl.b], writes=[out_tl.b])

    ps_ap_b = [[]]

    def sin_of(ang, out, T, t1, t2, shift, NP=64):
        src = ang
        if shift != 0.0:
            sc.op("dve", lambda e: e.tensor_scalar(out=t1[0:NP, 0:T], in0=ang[0:NP, 0:T], scalar1=float(shift), scalar2=None, op0=ALU.add),
                  reads=[ang.b], writes=[t1.b])
            src = t1
        sc.op("dve", lambda e: e.tensor_scalar(out=t2[0:NP, 0:T], in0=src[0:NP, 0:T], scalar1=float(1.0 / (2 * np.pi)), scalar2=MAGIC,
                                               op0=ALU.mult, op1=ALU.add), reads=[src.b], writes=[t2.b])
        sc.op("dve", lambda e: e.tensor_scalar(out=t2[0:NP, 0:T], in0=t2[0:NP, 0:T], scalar1=MAGIC, scalar2=None, op0=ALU.subtract),
              reads=[t2.b], writes=[t2.b])
        sc.op("dve", lambda e: e.scalar_tensor_tensor(out=t1[0:NP, 0:T], in0=t2[0:NP, 0:T], scalar=-C1, in1=src[0:NP, 0:T], op0=ALU.mult, op1=ALU.add),
              reads=[t2.b, src.b], writes=[t1.b])
        sc.op("dve", lambda e: e.scalar_tensor_tensor(out=t1[0:NP, 0:T], in0=t2[0:NP, 0:T], scalar=-C2, in1=t1[0:NP, 0:T], op0=ALU.mult, op1=ALU.add),
              reads=[t2.b, t1.b], writes=[t1.b])
        sc.op("dve", lambda e: e.tensor_scalar(out=t1[0:NP, 0:T], in0=t1[0:NP, 0:T], scalar1=PI, scalar2=-PI, op0=ALU.min, op1=ALU.max),
              reads=[t1.b], writes=[t1.b])
        sc.op("act", lambda e: e.activation(out=out, in_=t1[0:NP, 0:T], func=AF.Sin),
              reads=[t1.b], writes=[out.b if isinstance(out, Tl) else out_b[0]])

    out_b = [None]

    def rope_tables(pos_ap, T, cos_ap, sin_ap, dst_b, tmps, NP=64):
        pb, ang, t1, t2 = tmps
        sc.dma("sp", pb[0:NP, 0:T], pos_ap.partition_broadcast(NP), writes=[pb.b])
        sc.op("dve", lambda e: e.tensor_scalar(out=ang[0:NP, 0:T], in0=pb[0:NP, 0:T], scalar1=V("invf", 0, NP), scalar2=None, op0=ALU.mult),
              reads=[pb.b, vec.b], writes=[ang.b])
        out_b[0] = dst_b
        sin_of(ang, sin_ap, T, t1, t2, 0.0, NP)
        sin_of(ang, cos_ap, T, t1, t2, float(np.pi / 2), NP)

    def rms_h(xt, hT, T, st):
        bk = rot7.next()
        for kc in range(16):
            sq = sqr.next()
            sc.op("act", lambda e: e.activation(out=sq[:, 0:T], in_=xt[:, kc, 0:T], func=AF.Square), reads=[xt.b], writes=[sq.b])
            mm(bk.ap(0, 128, 0, T), ones[:, 0, :], sq[:, 0:T], kc == 0, kc == 15, [ones.b, sq.b], [bk.b])
        ps_ap_b[0] = [bk.b]
        rsqrt_bc(bk.ap(0, 128, 0, T), "c_eps", st["rs"], 128, T, st["rtmp"])
        rs = st["rs"]
        for kc in range(16):
            sc.op("dve", lambda e, kc=kc: e.scalar_tensor_tensor(out=hT[:, kc, 0:T], in0=xt[:, kc, 0:T], scalar=V("g_mix", kc),
                                                                 in1=rs[:, 0:T], op0=ALU.mult, op1=ALU.mult),
                  reads=[xt.b, rs.b, vec.b], writes=[hT.b])

    def proj16(bk, M, wt, c0, hT, T):
        for kc in range(16):
            mm(bk.ap(0, M, 0, T), wt[:, kc, c0:c0 + M], hT[:, kc, 0:T], kc == 0, kc == 15, [wt.b, hT.b], [bk.b])

    def lowrank_norm(wkq, c0, gname, hT, T, st, dst_fn, dst_bufs):
        raw = st["raw"]
        bk2 = rot7.next()
        pend = None
        for oc in range(4):
            bk = rot7.next()
            proj16(bk, 128, wkq, c0 + 128 * oc, hT, T)
            sc.op("dve", lambda e, oc=oc, bk=bk: e.tensor_copy(out=raw[:, oc, 0:T], in_=bk.ap(0, 128, 0, T)), reads=[bk.b], writes=[raw.b])
            sq = sqr.next()
            sc.op("act", lambda e, bk=bk, sq=sq: e.activation(out=sq[:, 0:T], in_=bk.ap(0, 128, 0, T), func=AF.Square), reads=[bk.b], writes=[sq.b])
            if pend is not None:
                mm(bk2.ap(0, 128, 0, T), ones[:, 1, :], pend[1][:, 0:T], pend[0] == 0, False, [ones.b, pend[1].b], [bk2.b])
            pend = (oc, sq)
        mm(bk2.ap(0, 128, 0, T), ones[:, 1, :], pend[1][:, 0:T], False, True, [ones.b, pend[1].b], [bk2.b])
        ps_ap_b[0] = [bk2.b]
        rsqrt_bc(bk2.ap(0, 128, 0, T), "c_eps", st["rs"], 128, T, st["rtmp"])
        rs = st["rs"]
        for oc in range(4):
            sc.op("dve", lambda e, oc=oc: e.scalar_tensor_tensor(out=dst_fn(oc), in0=raw[:, oc, 0:T], scalar=V(gname, oc), in1=rs[:, 0:T],
                                                                 op0=ALU.mult, op1=ALU.mult),
                  reads=[raw.b, rs.b, vec.b], writes=dst_bufs)

    def kpe_rope(wkq, hT, T, cos_ap, sin_ap, cs_bufs, st, dst_ap, dst_bufs, coff=0):
        bA = rot7.next()
        proj16(bA, 64, wkq, 1024 - coff, hT, T)
        bB = rot7.next()
        proj16(bB, 64, wkq, 1088 - coff, hT, T)
        t1, t2 = st["r1"], st["r2"]
        sc.op("dve", lambda e: e.tensor_tensor(out=t1[:, 0:T], in0=bA.ap(0, 64, 0, T), in1=cos_ap, op=ALU.mult), reads=[bA.b] + cs_bufs, writes=[t1.b])
        sc.op("dve", lambda e: e.tensor_tensor(out=t2[:, 0:T], in0=bB.ap(0, 64, 0, T), in1=sin_ap, op=ALU.mult), reads=[bB.b] + cs_bufs, writes=[t2.b])
        sc.op("dve", lambda e: e.tensor_tensor(out=dst_ap, in0=t1[:, 0:T], in1=t2[:, 0:T], op=ALU.add), reads=[t1.b, t2.b], writes=dst_bufs)

    p1 = []
    wkq = sb([128, 16, 1152], BF16, p1)
    for kc in range(16):
        sc.dma("pool", wkq[:, kc, 0:1088], w_in[kc * 128:(kc + 1) * 128, 0:1088], writes=[wkq.b])
    sc.op("pool", lambda e: e.tensor_scalar(out=wkq[:, :, 1088:1120], in0=wkq[:, :, 1056:1088], scalar1=-1.0, scalar2=None, op0=ALU.mult),
          reads=[wkq.b], writes=[wkq.b])
    sc.op("pool", lambda e: e.tensor_copy(out=wkq[:, :, 1120:1152], in_=wkq[:, :, 1024:1056]), reads=[wkq.b], writes=[wkq.b])

    st = {"rs": sb([128, 512], F32, p1), "rtmp": sb([128, 512], F32, p1), "raw": sb([128, 4, 512], F32, p1),
          "r1": sb([64, 512], F32, p1), "r2": sb([64, 512], F32, p1)}
    rtm = [sb([128, 512], F32, p1) for _ in range(4)]
    xts = Ring([sb([128, 16, 512], F32, p1) for _ in range(1)])
    hTs = Ring([sb([128, 16, 512], BF16, p1) for _ in range(1)])
    ckvf = Ring([sb([128, 4, 512], F32, p1) for _ in range(1)])
    kpef = Ring([sb([64, 512], F32, p1) for _ in range(1)])

    for t in range(4):
        out_b[0] = cos_own.b
        rope_tables(pos_own[:, t * 512:(t + 1) * 512], 512, cos_own[:, t * 512:(t + 1) * 512], sin_own[:, t * 512:(t + 1) * 512], cos_own.b, rtm, 128)
    rope_tables(pos_s[:, 0:64], 64, cos_s[:, 0:64], sin_s[:, 0:64], cos_s.b, rtm, 128)
    sin_own.b = cos_own.b
    sin_s.b = cos_s.b

    def own_tokens(x_ap, T, cos_ap, sin_ap, cs_b, cq_dst, cq_b, ckv_out, kpe_out, sample):
        xt = xts.next()
        hT = hTs.next()
        sc.dma("sp", xt[:, :, 0:T], x_ap.rearrange("(k p) t -> p k t", p=128), writes=[xt.b])
        rms_h(xt, hT, T, st)
        lowrank_norm(wkq, 0, "g_q_a", hT, T, st, lambda oc: cq_dst(oc), [cq_b])
        cf = ckvf.next()
        lowrank_norm(wkq, 512, "g_kv_a", hT, T, st, lambda oc: cf[:, oc, 0:T], [cf.b])
        sc.dma("sp", ckv_out.rearrange("(k p) t -> p k t", p=128), cf[:, :, 0:T], reads=[cf.b])
        kf = kpef.next()
        kpe_rope(wkq, hT, T, cos_ap, sin_ap, [cs_b], st, kf[:, 0:T], [kf.b])
        sc.dma("sp", kpe_out, kf[:, 0:T], reads=[kf.b])
        if sample:
            sc.op("dve", lambda e: e.tensor_copy(out=ckvn_s[:, :, 0:T], in_=cf[:, :, 0:T]), reads=[cf.b], writes=[ckvn_s.b])
            sc.op("dve", lambda e: e.tensor_copy(out=kpe_s[:, 0:T], in_=kf[:, 0:T]), reads=[kf.b], writes=[kpe_s.b])

    if "p1a" in PHASES:
        for t in range(0 if SAMPLE_ONLY else 4):
            c = slice(t * 512, (t + 1) * 512)
            own_tokens(xT_own[:, c], 512, cos_own[0:64, c], sin_own[0:64, c], cos_own.b,
                       lambda oc, c=c: cqn[:, oc, c], cqn.b, ckvT_own[:, c], kpeT_own[:, c], False)
        own_tokens(xT_s[:, 0:64], 64, cos_s[0:64, 0:64], sin_s[0:64, 0:64], cos_s.b,
                   lambda oc: cqn_s[:, oc, 0:64], cqn_s.b, ckvT_s[:, 0:64], kpeT_s[:, 0:64], True)

    free(p1)
    p1 = []
    if os.environ.get("KDEBUG"):
        print("sbuf remaining before P1b", nc.sbuf_bytes_remaining)
    wkq = sb([128, 16, 640], BF16, p1)
    for kc in range(16):
        sc.dma("pool", wkq[:, kc, 0:576], w_in[kc * 128:(kc + 1) * 128, 512:1088], writes=[wkq.b])
    sc.op("pool", lambda e: e.tensor_scalar(out=wkq[:, :, 576:608], in0=wkq[:, :, 544:576], scalar1=-1.0, scalar2=None, op0=ALU.mult),
          reads=[wkq.b], writes=[wkq.b])
    sc.op("pool", lambda e: e.tensor_copy(out=wkq[:, :, 608:640], in_=wkq[:, :, 512:544]), reads=[wkq.b], writes=[wkq.b])
    wkv = sb([128, 4, 4096], BF16, p1)
    for kc in range(4):
        sc.dma("pool", wkv[:, kc, :], w_kv_up[kc * 128:(kc + 1) * 128, :], writes=[wkv.b])
    rtm = [sb([64, 512], F32, p1) for _ in range(4)]
    st = {"rs": sb([128, 512], F32, p1), "rtmp": sb([128, 512], F32, p1), "raw": sb([128, 4, 512], F32, p1),
          "r1": rtm[2], "r2": rtm[3]}
    xts = Ring([sb([128, 16, 512], BF16, p1) for _ in range(2)])
    hTs = xts
    kbufs = Ring([sb([128, 16, 512], BF16, p1) for _ in range(1)])
    vbufs = Ring([sb([128, 4, 16, 128], BF16, p1) for _ in range(1)])
    sqrope = Ring([sb([64, 512], BF16, p1) for _ in range(2)])
    ssq_bank = banks[7]
    wkv_v = wkv[:].rearrange("p k (h c) -> p k h c", c=256)

    def gen_kv(ckvn, kpeb, T, kt0):
        nsub = (T + 127) // 128
        nk = min(128, T)
        k0 = kt0 * 128
        kpe_ap, kpe_b = kpeb
        sc.dma("sp", KR[:, k0:k0 + T], kpe_ap, reads=[kpe_b], writes=[scr_b])
        sr = sqrope.next()
        sc.op("pool", lambda e: e.tensor_tensor(out=sr[:, 0:T], in0=kpe_ap, in1=kpe_ap, op=ALU.mult), reads=[kpe_b], writes=[sr.b])
        kb = kbufs.next()

        def k_tiny(h, sq):
            for sub in range(nsub):
                col = sub * 16 + h
                mm(ssq_bank.ap(0, nk, col, col + 1), sq[:, sub * 128:sub * 128 + nk], ones[:, 3, 0:1], True, False, [sq.b, ones.b], [ssq_bank.b])
                mm(ssq_bank.ap(0, nk, col, col + 1), sr[0:64, sub * 128:sub * 128 + nk], ones[0:64, 3, 0:1], False, True, [sr.b, ones.b], [ssq_bank.b])
        pend = None
        for h in range(NH):
            bk = rot7.next()
            for kc in range(4):
                mm(bk.ap(0, 128, 0, T), wkv[:, kc, h * 256:h * 256 + 128], ckvn[:, kc, 0:T], kc == 0, kc == 3, [wkv.b, ckvn.b], [bk.b])
            sc.op("dve", lambda e, h=h, bk=bk: e.tensor_copy(out=kb[:, h, 0:T], in_=bk.ap(0, 128, 0, T)), reads=[bk.b], writes=[kb.b])
            sq = sqr.next()
            sc.op("act", lambda e, bk=bk, sq=sq: e.activation(out=sq[:, 0:T], in_=bk.ap(0, 128, 0, T), func=AF.Square), reads=[bk.b], writes=[sq.b])
            if pend is not None:
                k_tiny(*pend)
            pend = (h, sq)
        k_tiny(*pend)
        tmp = st["rtmp"]
        sc.op("act", lambda e: e.activation(out=tmp[0:nk, 0:nsub * 16], in_=ssq_bank.ap(0, nk, 0, nsub * 16), func=AF.Sqrt,
                                            bias=V("c_eps192", 0, nk), scale=1.0), reads=[ssq_bank.b, vec.b], writes=[tmp.b])
        sc.op("dve", lambda e: e.reciprocal(out=rstdk[0:nk, kt0:kt0 + nsub, :], in_=tmp[0:nk, 0:nsub * 16].rearrange("p (s h) -> p s h", h=16)),
              reads=[tmp.b], writes=[rstdk.b])
        sc.dma("sp", KT[:, :, k0:k0 + T].rearrange("h d t -> d h t"), kb[:, :, 0:T], reads=[kb.b], writes=[scr_b])
        vb = vbufs.next()
        if nk < 128:
            sc.op("pool", lambda e: e.memset(vb[:, 0, :, :], 0.0), writes=[vb.b])
        for sub in range(nsub):
            for hg in range(4):
                bk = rot7.next()
                for kc in range(4):
                    mm(bk.ap(0, nk, 0, 512).rearrange("p (h d) -> p h d", d=128), ckvn[:, kc, sub * 128:sub * 128 + nk],
                       wkv_v[:, kc, hg * 4:(hg + 1) * 4, 128:256], kc == 0, kc == 3, [wkv.b, ckvn.b], [bk.b])
                eng = "act" if (hg % 2 == 0) else "dve"
                if eng == "act":
                    sc.op("act", lambda e, bk=bk, sub=sub, hg=hg: e.copy(out=vb[0:nk, sub, hg * 4:(hg + 1) * 4, :],
                                                                         in_=bk.ap(0, nk, 0, 512).rearrange("p (h d) -> p h d", d=128)),
                          reads=[bk.b], writes=[vb.b])
                else:
                    sc.op("dve", lambda e, bk=bk, sub=sub, hg=hg: e.tensor_copy(out=vb[0:nk, sub, hg * 4:(hg + 1) * 4, :],
                                                                                in_=bk.ap(0, nk, 0, 512).rearrange("p (h d) -> p h d", d=128)),
                          reads=[bk.b], writes=[vb.b])
        for sub in range(nsub):
            sc.dma("sp", V2[:, :, kt0 + sub, :].rearrange("h p d -> p h d"), vb[:, sub, :, :], reads=[vb.b], writes=[scr_b])

    if "p1b" in PHASES:
        ckvb = Ring([sb([128, 4, 512], BF16, p1) for _ in range(2)])
        kpeb = Ring([sb([64, 512], BF16, p1) for _ in range(2)])
        cst = rtm
        cosb = sb([64, 512], F32, p1)
        sinb = sb([64, 512], F32, p1)
        sinb.b = cosb.b
        for t in range(0 if SAMPLE_ONLY else S // 512):
            c = slice(t * 512, (t + 1) * 512)
            xt = xts.next()
            hT = hTs.next()
            sc.dma("pool", xt[:, :, :], xT_all[:, c].rearrange("(k p) t -> p k t", p=128), writes=[xt.b])
            rms_h(xt, hT, 512, st)
            cb = ckvb.next()
            lowrank_norm(wkq, 0, "g_kv_a", hT, 512, st, lambda oc, cb=cb: cb[:, oc, :], [cb.b])
            out_b[0] = cosb.b
            rope_tables(pos_all[:, c], 512, cosb[:, :], sinb[:, :], cosb.b, cst)
            kb_ = kpeb.next()
            kpe_rope(wkq, hT, 512, cosb[:, :], sinb[:, :], [cosb.b], st, kb_[:, :], [kb_.b], coff=512)
            gen_kv(cb, (kb_[:, :], kb_.b), 512, t * 4)
        for b in range(2):
            for t in range(8):
                c = slice(t * 512, (t + 1) * 512)
                cb = ckvb.next()
                sc.dma("pool", cb[:, :, :], cckvT[b, :, c].rearrange("(k p) t -> p k t", p=128), writes=[cb.b])
                kb_ = kpeb.next()
                sc.dma("pool", kb_[:, :], ckpeT[b, :, c], writes=[kb_.b])
                gen_kv(cb, (kb_[:, :], kb_.b), 512, 128 + 33 * b + 4 * t)
            cb = ckvb.next()
            sc.op("dve", lambda e, cb=cb, b=b: e.tensor_copy(out=cb[:, :, 0:32], in_=ckvn_s[:, :, 32 * b:32 * b + 32]), reads=[ckvn_s.b], writes=[cb.b])
            kb_ = kpeb.next()
            sc.op("dve", lambda e, kb_=kb_, b=b: e.tensor_copy(out=kb_[:, 0:32], in_=kpe_s[:, 32 * b:32 * b + 32]), reads=[kpe_s.b], writes=[kb_.b])
            gen_kv(cb, (kb_[:, 0:32], kb_.b), 32, 128 + 33 * b + 32)
    free(p1)


    wblocks = []
    for g in range(2):
        wblocks.append([(0, w_in, 16, 1088 + 512 * g, 512)])
        wblocks.append([(0, w_in, 16, 2112 + 512 * g, 512)])
    for oc in range(16):
        wblocks.append([(0, w_ao, 16, 128 * oc, 128), (2048, w_co, 8, 128 * oc, 128),
                        (3072, w_in, 16, 3136 + 128 * oc, 128), (5120, w_in, 16, 3136 + 2048 + 128 * oc, 128)])
    for g in range(4):
        wblocks.append([(0, w_o, 16, 512 * g, 512)])
    for g in range(11):
        wblocks.append([(0, w_fg, 16, 512 * g, 512)])
        wblocks.append([(0, w_fu, 16, 512 * g, 512)])
    for oc in range(16):
        wblocks.append([(0, w_fd, 44, 128 * oc, 128)])
    assert len(wblocks) == 62
    wb_scr_b = Buf()

    def convert_block(bi):
        for (off, w, KC, c0, ncols) in wblocks[bi]:
            sc.dma("pool", WB[bi, :, off:off + KC * ncols].rearrange("p (k c) -> p k c", c=ncols),
                   w[0:KC * 128, c0:c0 + ncols].rearrange("(k p) c -> p k c", p=128), writes=[wb_scr_b])
    conv_done = [0]

    def convert_upto(n):
        while conv_done[0] < min(n, 62):
            convert_block(conv_done[0])
            conv_done[0] += 1

    if "p2" in PHASES:
        p2 = []
        wqs = Ring([sb([128, 4, 384], BF16, p2) for _ in range(2)])
        qns = Ring([sb([128, TOWN + 64], BF16, p2) for _ in range(2)])
        qrs = Ring([sb([128, TOWN + 64], BF16, p2) for _ in range(2)])
        kch = Ring([sb([128, 33 * 128], BF16, p2) for _ in range(2)])
        krch = Ring([sb([128, 33 * 128], BF16, p2) for _ in range(2)])
        vch = Ring([sb([128, 33, 128], BF16, p2) for _ in range(2)])
        pTs = Ring([sb([128, 1024], BF16, p2) for _ in range(3)])
        pTs_small = Ring([sb([128, 64], BF16, p2) for _ in range(6)])
        raccs = [sb([128, 1024], F32, p2), sb([128, 1024], F32, p2)]
        rr = sb([128, 1024], F32, p2)
        ato = Ring([sb([128, 1024], BF16, p2) for _ in range(2)])
        q32 = sb([128, 512], F32, p2)
        qt1 = sb([128, 512], F32, p2)
        qt2 = sb([128, 512], F32, p2)
        rq = sb([128, 512], F32, p2)
        rqt = sb([128, 512], F32, p2)
        sqq = Ring([sb([128, 512], BF16, p2) for _ in range(2)])
        Sb = [(banks[0], banks[1]), (banks[2], banks[3]), (banks[6], banks[7])]
        Ob = (banks[4], banks[5])
        zero_t = sb([128, 512], BF16, p2)
        sc.op("pool", lambda e: e.memset(zero_t[:, :], 0.0), writes=[zero_t.b])

        def compute_q(wq, src, src_b, c0, T, cos_ap, sin_ap, cs_b, qn, qr, d0):
            bA = rot2.next()
            for kc in range(4):
                mm(bA.ap(0, 128, 0, T), wq[:, kc, 0:128], src(kc, c0, T), kc == 0, kc == 3, [wq.b, src_b], [bA.b])
            sq = sqq.next()
            sc.op("act", lambda e: e.activation(out=sq[:, 0:T], in_=bA.ap(0, 128, 0, T), func=AF.Square), reads=[bA.b], writes=[sq.b])
            bB = rot2.next()
            for kc in range(4):
                mm(bB.ap(0, 128, 0, T), wq[:, kc, 128:256], src(kc, c0, T), kc == 0, kc == 3, [wq.b, src_b], [bB.b])
            sc.op("dve", lambda e: e.tensor_tensor(out=qt1[:, 0:T], in0=bB.ap(0, 128, 0, T), in1=cos_ap, op=ALU.mult), reads=[bB.b, cs_b], writes=[qt1.b])
            for kc in range(4):
                mm(bB.ap(0, 128, 0, T), wq[:, kc, 256:384], src(kc, c0, T), kc == 0, kc == 3, [wq.b, src_b], [bB.b])
            sc.op("dve", lambda e: e.tensor_tensor(out=qt2[:, 0:T], in0=bB.ap(0, 128, 0, T), in1=sin_ap, op=ALU.mult), reads=[bB.b, cs_b], writes=[qt2.b])
            sc.op("dve", lambda e: e.tensor_tensor(out=q32[:, 0:T], in0=qt1[:, 0:T], in1=qt2[:, 0:T], op=ALU.add), reads=[qt1.b, qt2.b], writes=[q32.b])
            sq2 = sqq.next()
            sc.op("act", lambda e: e.activation(out=sq2[0:64, 0:T], in_=q32[0:64, 0:T], func=AF.Square), reads=[q32.b], writes=[sq2.b])
            mm(bB.ap(0, 128, 0, T), ones[:, 3, :], sq[:, 0:T], True, False, [ones.b, sq.b], [bB.b])
            mm(bB.ap(0, 128, 0, T), ones[0:64, 3, :], sq2[0:64, 0:T], False, True, [ones.b, sq2.b], [bB.b])
            sc.op("act", lambda e: e.activation(out=rqt[:, 0:T], in_=bB.ap(0, 128, 0, T), func=AF.Sqrt, bias=V("c_eps192"), scale=1.0),
                  reads=[bB.b, vec.b], writes=[rqt.b])
            sc.op("dve", lambda e: e.reciprocal(out=rq[:, 0:T], in_=rqt[:, 0:T]), reads=[rqt.b], writes=[rq.b])
            sc.op("dve", lambda e: e.scalar_tensor_tensor(out=qn[:, d0:d0 + T], in0=bA.ap(0, 128, 0, T), scalar=gg[:, 0:1], in1=rq[:, 0:T],
                                                          op0=ALU.mult, op1=ALU.mult), reads=[bA.b, gg.b, rq.b], writes=[qn.b])
            sc.op("dve", lambda e: e.scalar_tensor_tensor(out=qr[:, d0:d0 + T], in0=q32[:, 0:T], scalar=gg[:, 1:2], in1=rq[:, 0:T],
                                                          op0=ALU.mult, op1=ALU.mult), reads=[q32.b, gg.b, rq.b], writes=[qr.b])

        def split512(a, b):
            out = []
            while a < b:
                e = min(b, (a // 512 + 1) * 512)
                out.append((a, e))
                a = e
            return out

        def attend(h, qn, qr, q0, NQ, steps, at_dst):
            n = len(steps)
            chunks = {}
            small = NQ <= 64
            slots = [(banks[0],), (banks[1],), (banks[2],), (banks[3],)] if small else Sb
            depth = 3 if small else 2

            def load_chunk(ci):
                i0 = ci * 33
                i1 = min(n, i0 + 33)
                ktA = steps[i0][0]
                nkeys = sum(s[1] for s in steps[i0:i1])
                kc_, kr_, vc_ = kch.next(), krch.next(), vch.next()
                sc.dma("sp", kc_[:, 0:nkeys], KT[h, :, ktA * 128:ktA * 128 + nkeys], reads=[scr_b], writes=[kc_.b])
                sc.dma("sp", kr_[0:64, 0:nkeys], KR[:, ktA * 128:ktA * 128 + nkeys], reads=[scr_b], writes=[kr_.b])
                sc.dma("sp", kr_[64:128, 0:nkeys], KR[:, ktA * 128:ktA * 128 + nkeys], reads=[scr_b], writes=[kr_.b])
                sc.dma("sp", vc_[:, 0:i1 - i0, :], V2[h, :, ktA:ktA + (i1 - i0), :], reads=[scr_b], writes=[vc_.b])
                chunks[ci] = (kc_, kr_, vc_)

            def qk(i):
                kt, nk, a, m = steps[i]
                ci, j = divmod(i, 33)
                if ci not in chunks:
                    load_chunk(ci)
                if j == 0 and (ci + 1) * 33 < n and (ci + 1) not in chunks:
                    load_chunk(ci + 1)
                kc_, kr_, _ = chunks[ci]
                S2 = slots[i % len(slots)]
                a2 = a
                if m is not None:
                    bk = S2[a // 512]
                    o = a % 512
                    mm(bk.ap(0, nk, o, o + 128), kc_[:, j * 128:j * 128 + nk], qn[:, q0 + a:q0 + a + 128], True, False, [kc_.b, qn.b], [bk.b])
                    mm(bk.ap(0, nk, o, o + 128), kr_[0:64, j * 128:j * 128 + nk], qr[0:64, q0 + a:q0 + a + 128], False, False, [kr_.b, qr.b], [bk.b])
                    mm(bk.ap(0, nk, o, o + 128), ident[:, 0:nk], mbt[:, m, :], False, True, [ident.b, mbt.b], [bk.b])
                    a2 = a + 128
                grs = split512(a2, NQ)
                for (g0, g1) in grs:
                    bk = S2[g0 // 512]
                    o0, o1 = g0 % 512, g0 % 512 + (g1 - g0)
                    mm(bk.ap(0, nk, o0, o1), kc_[:, j * 128:j * 128 + nk], qn[:, q0 + g0:q0 + g1], True, False, [kc_.b, qn.b], [bk.b])
                for gi, (g0, g1) in enumerate(grs):
                    bk = S2[g0 // 512]
                    o0, o1 = g0 % 512, g0 % 512 + (g1 - g0)
                    r0 = 64 * (gi % 2)
                    mm(bk.ap(0, nk, o0, o1), kr_[r0:r0 + 64, j * 128:j * 128 + nk], qr[r0:r0 + 64, q0 + g0:q0 + g1], False, True, [kr_.b, qr.b], [bk.b])

            pts = {}

            def expo(i):
                kt, nk, a, m = steps[i]
                S2 = slots[i % len(slots)]
                pt = pTs_small.next() if small else pTs.next()
                pts[i] = pt
                src = S2[0].ap(0, nk, a, NQ) if small else pp[S2[0].i // 2][0:nk, a:NQ]
                sc.op("act", lambda e: e.activation(out=pt[0:nk, a:NQ], in_=src, func=AF.Exp, scale=rstdk[0:nk, kt, h:h + 1]),
                      reads=[b_.b for b_ in S2] + [rstdk.b], writes=[pt.b])
                racc = raccs[i % 2]
                if i < 2:
                    assert a == 0 and nk == 128
                    sc.op("dve", lambda e: e.tensor_copy(out=racc[:, 0:NQ], in_=pt[:, 0:NQ]), reads=[pt.b], writes=[racc.b])
                else:
                    sc.op("dve", lambda e: e.tensor_tensor(out=racc[0:nk, a:NQ], in0=racc[0:nk, a:NQ], in1=pt[0:nk, a:NQ], op=ALU.add),
                          reads=[pt.b, racc.b], writes=[racc.b])

            def pv(i):
                kt, nk, a, m = steps[i]
                ci, j = divmod(i, 33)
                _, _, vc_ = chunks[ci]
                pt = pts.pop(i)
                for (g0, g1) in split512(a, NQ):
                    bk = Ob[g0 // 512]
                    o0, o1 = g0 % 512, g0 % 512 + (g1 - g0)
                    mm(bk.ap(0, 128, o0, o1), vc_[0:nk, j, :], pt[0:nk, g0:g1], i == 0, small and i == n - 1, [vc_.b, pt.b], [bk.b])

            for i in range(min(depth, n)):
                qk(i)
            for i in range(n):
                expo(i)
                if i + depth < n:
                    qk(i + depth)
                pv(i)
            if not small:
                for (g0, g1) in split512(0, NQ):
                    bk = Ob[g0 // 512]
                    mm(bk.ap(0, 128, 0, g1 - g0), zero_t[:, 0:128], zero_t[:, 0:g1 - g0], False, True, [zero_t.b], [bk.b])
            racc = raccs[0]
            sc.op("dve", lambda e: e.tensor_tensor(out=racc[:, 0:NQ], in0=racc[:, 0:NQ], in1=raccs[1][:, 0:NQ], op=ALU.add),
                  reads=[racc.b, raccs[1].b], writes=[racc.b])
            for (g0, g1) in split512(0, NQ):
                bk = Sb[0][g0 // 512]
                w = g1 - g0
                mm(bk.ap(0, 128, 0, w), onesf[:, :], racc[:, g0:g1], True, True, [onesf.b, racc.b], [bk.b])
                sc.op("dve", lambda e, bk=bk, g0=g0, g1=g1, w=w: e.reciprocal(out=rr[:, g0:g1], in_=bk.ap(0, 128, 0, w)), reads=[bk.b], writes=[rr.b])
            at = ato.next()
            for (g0, g1) in split512(0, NQ):
                bk = Ob[g0 // 512]
                w = g1 - g0
                sc.op("dve", lambda e, bk=bk, g0=g0, g1=g1, w=w: e.tensor_tensor(out=at[:, g0:g1], in0=bk.ap(0, 128, 0, w), in1=rr[:, g0:g1], op=ALU.mult),
                      reads=[bk.b, rr.b], writes=[at.b])
            sc.dma("sp", at_dst, at[:, 0:NQ], reads=[at.b], writes=[at_scr_b])

        for h in range(NH):
            convert_upto(4 * (h + 1))
            wq = wqs.next()
            for kc in range(4):
                sc.dma("pool", wq[:, kc, 0:192], w_q_up[kc * 128:(kc + 1) * 128, h * 192:(h + 1) * 192], writes=[wq.b])
            sc.op("pool", lambda e: e.tensor_copy(out=wq[:, :, 192:256], in_=wq[:, :, 128:192]), reads=[wq.b], writes=[wq.b])
            sc.op("pool", lambda e: e.tensor_scalar(out=wq[:, :, 256:288], in0=wq[:, :, 160:192], scalar1=-1.0, scalar2=None, op0=ALU.mult),
                  reads=[wq.b], writes=[wq.b])
            sc.op("pool", lambda e: e.tensor_copy(out=wq[:, :, 288:320], in_=wq[:, :, 128:160]), reads=[wq.b], writes=[wq.b])
            sc.op("pool", lambda e: e.tensor_copy(out=wq[:, :, 320:384], in_=wq[:, :, 256:320]), reads=[wq.b], writes=[wq.b])
            qn, qr = qns.next(), qrs.next()
            for t in range(0 if SAMPLE_ONLY else 4):
                c = slice(t * 512, (t + 1) * 512)
                compute_q(wq, lambda kc, c0, T: cqn[:, kc, c0:c0 + T], cqn.b, t * 512, 512, cos_own[:, c], sin_own[:, c], cos_own.b, qn, qr, t * 512)
            compute_q(wq, lambda kc, c0, T: cqn_s[:, kc, c0:c0 + T], cqn_s.b, 0, 64, cos_s[:, 0:64], sin_s[:, 0:64], cos_s.b, qn, qr, TOWN)
            for qh in range(0 if SAMPLE_ONLY else 2):
                steps = []
                for kt in range(64 * (qh + 1)):
                    j0 = max(kt // 8, 8 * qh)
                    m = (kt % 8) if (kt // 8 >= 8 * qh) else None
                    steps.append((kt, 128, (j0 - 8 * qh) * 128, m))
                attend(h, qn, qr, 1024 * qh, 1024, steps, AT[h, :, 1024 * qh:1024 * (qh + 1)])
            for b in range(2):
                steps = [(128 + 33 * b + i, 128 if i < 32 else 32, 0, None) for i in range(33)]
                attend(h, qn, qr, TOWN + 32 * b, 32, steps, AT[h, :, TOWN + 32 * b:TOWN + 32 * b + 32])
        free(p2)
    free(p12)

    if "p3" in PHASES:
        convert_upto(62)
        sc.barrier()
        p3 = []
        wb = Ring([sb([128, 8192], BF16, p3) for _ in range(3)])

        def wload(bi):
            t = wb.next()
            n = max(off + KC * ncols for (off, w, KC, c0, ncols) in wblocks[bi])
            sc.dma("pool" if (bi % 2) else "sp", t[:, 0:n], WB[bi, :, 0:n], reads=[wb_scr_b], writes=[t.b])
            views = [t[:, off:off + KC * ncols].rearrange("p (k c) -> p k c", c=ncols) for (off, w, KC, c0, ncols) in wblocks[bi]]
            return views, t.b

        xt = sb([128, 16, 512], F32, p3)
        hT = sb([128, 16, 512], BF16, p3)
        hm = sb([128, 128], F32, p3)
        arena = sb([128, 11264], F32, p3)
        upad = Tl(arena.t[:, 0:5120].rearrange("p (a b c) -> p a b c", a=8, b=4))
        ycv = Tl(arena.t[:, 5120:9216].rearrange("p (a c) -> p a c", a=8))
        ybf = sb([128, 8, 512], BF16, p3)
        zc = ybf
        att_raw = sb([128, 8192], BF16, p3)
        att = Tl(att_raw.t[:, :].rearrange("p (h t) -> p h t", t=512))
        xh = Tl(att_raw.t[:, 0:4096].bitcast(F32).rearrange("p (k t) -> p k t", t=128))
        hhT = Tl(att_raw.t[:, 4096:6144].rearrange("p (k t) -> p k t", t=128))
        att.b = xh.b = hhT.b = att_raw.b
        z2 = sb([128, 16, 512], BF16, p3)
        act_t = Tl(arena.t[:, :].bitcast(BF16).rearrange("p (a c) -> p a c", a=44))
        st3 = {"rs": sb([128, 512], F32, p3), "rtmp": sb([128, 512], F32, p3)}
        ident3 = sb([128, 128], BF16, p3)
        sc.dma("pool", ident3[:], ident_in, writes=[ident3.b])
        diags = Ring([sb([128, 128], BF16, p3) for _ in range(4)])
        upbs = Ring([sb([128, 4, 160], BF16, p3) for _ in range(2)])
        sg = Ring([sb([128, 512], F32, p3) for _ in range(2)])
        tA = Ring([sb([128, 512], F32, p3) for _ in range(1)])
        tB = Ring([sb([128, 512], F32, p3) for _ in range(1)])
        ln_m = st3["rs"]
        ln_r = st3["rtmp"]
        yst = Ring([sb([128, 512], F32, p3) for _ in range(1)])

        def post(x_ap, T, J, L, halo_ap, hm_ap, state, at_c0, y_out, u_out):
            sc.dma("sp", xt[:, :, 0:T], x_ap.rearrange("(k p) t -> p k t", p=128), writes=[xt.b])
            rms_h(xt, hT, T, st3)
            HT = J * 32
            if halo_ap is not None:
                sc.dma("sp", xh[:, :, 0:HT], halo_ap.rearrange("(k p) t -> p k t", p=128), writes=[xh.b])
                sc.dma("sp", hm[:, 0:HT], hm_ap.partition_broadcast(128), writes=[hm.b])
                rms_h(xh, hhT, HT, st3)
            else:
                for b in range(J):
                    sc.dma("sp", upad[:, :, b, 2:32], state[b].rearrange("(k p) r -> p k r", p=128), writes=[upad.b])
            for g in range(2):
                (wa,), wab = wload(2 * g)
                (wg,), wgb = wload(2 * g + 1)
                for o in range(4):
                    oc = g * 4 + o
                    bA, bG = rot7.next(), rot7.next()
                    for kc in range(16):
                        mm(bA.ap(0, 128, 0, T), wa[:, kc, o * 128:(o + 1) * 128], hT[:, kc, 0:T], kc == 0, kc == 15, [wab, hT.b], [bA.b])
                    for kc in range(16):
                        mm(bG.ap(0, 128, 0, T), wg[:, kc, o * 128:(o + 1) * 128], hT[:, kc, 0:T], kc == 0, kc == 15, [wgb, hT.b], [bG.b])
                    s_ = sg.next()
                    sc.op("act", lambda e, bG=bG, s_=s_, oc=oc: e.activation(out=s_[:, 0:T], in_=bG.ap(0, 128, 0, T), func=AF.Sigmoid,
                                                                            bias=V("b_glu", 8 + oc), scale=1.0), reads=[bG.b, vec.b], writes=[s_.b])
                    sc.op("dve", lambda e, bA=bA, s_=s_, oc=oc: e.scalar_tensor_tensor(
                        out=upad[:, oc, 0:J, 32:32 + L], in0=bA.ap(0, 128, 0, T).rearrange("p (j l) -> p j l", l=L), scalar=V("b_glu", oc),
                        in1=s_[:, 0:T].rearrange("p (j l) -> p j l", l=L), op0=ALU.add, op1=ALU.mult), reads=[bA.b, s_.b, vec.b], writes=[upad.b])
                    if halo_ap is not None:
                        bA, bG = rot7.next(), rot7.next()
                        for kc in range(16):
                            mm(bA.ap(0, 128, 0, HT), wa[:, kc, o * 128:(o + 1) * 128], hhT[:, kc, 0:HT], kc == 0, kc == 15, [wab, hhT.b], [bA.b])
                        for kc in range(16):
                            mm(bG.ap(0, 128, 0, HT), wg[:, kc, o * 128:(o + 1) * 128], hhT[:, kc, 0:HT], kc == 0, kc == 15, [wgb, hhT.b], [bG.b])
                        s_ = sg.next()
                        sc.op("act", lambda e, bG=bG, s_=s_, oc=oc: e.activation(out=s_[:, 0:HT], in_=bG.ap(0, 128, 0, HT), func=AF.Sigmoid,
                                                                                bias=V("b_glu", 8 + oc), scale=1.0), reads=[bG.b, vec.b], writes=[s_.b])
                        t_ = tA.next()
                        sc.op("dve", lambda e, bA=bA, s_=s_, oc=oc, t_=t_: e.scalar_tensor_tensor(
                            out=t_[:, 0:HT], in0=bA.ap(0, 128, 0, HT), scalar=V("b_glu", oc), in1=s_[:, 0:HT], op0=ALU.add, op1=ALU.mult),
                            reads=[bA.b, s_.b, vec.b], writes=[t_.b])
                        sc.op("dve", lambda e, oc=oc, t_=t_: e.tensor_tensor(
                            out=upad[:, oc, 0:J, 0:32], in0=t_[:, 0:HT].rearrange("p (j l) -> p j l", l=32),
                            in1=hm[:, 0:HT].rearrange("p (j l) -> p j l", l=32), op=ALU.mult), reads=[t_.b, hm.b], writes=[upad.b])
            if u_out is not None:
                u_out()
            sc.dma("sp", att[:, :, 0:T], AT[:, :, at_c0:at_c0 + T].rearrange("h d t -> d h t"), reads=[at_scr_b], writes=[att.b])
            for oc in range(8):
                upb = upbs.next()
                sc.op("pool", lambda e, oc=oc, upb=upb: e.tensor_copy(out=upb[:, 0:J, 2:32 + L], in_=upad[:, oc, 0:J, 2:32 + L]), reads=[upad.b], writes=[upb.b])
                bk = rot7.next()
                for k in range(CW):
                    dg = diags.next()
                    wcol = vec[:, VO["w_dw"] + oc * 31 + k:VO["w_dw"] + oc * 31 + k + 1]
                    sc.op("dve", lambda e, dg=dg, wcol=wcol: e.tensor_scalar(out=dg[:, :], in0=ident3[:, :], scalar1=wcol, scalar2=None, op0=ALU.mult),
                          reads=[ident3.b, vec.b], writes=[dg.b])
                    mm(bk.ap(0, 128, 0, T).rearrange("p (j l) -> p j l", l=L), dg[:, :], upb[:, 0:J, 2 + k:2 + k + L], k == 0, k == CW - 1,
                       [dg.b, upb.b], [bk.b])
                sc.op("act", lambda e, bk=bk, oc=oc: e.activation(out=ycv[:, oc, 0:T], in_=bk.ap(0, 128, 0, T), func=AF.Identity, bias=V("b_dw", oc), scale=1.0),
                      reads=[bk.b, vec.b], writes=[ycvb[oc]])
            b1, b2 = rot7.next(), rot7.next()
            for oc in range(8):
                sc.op("pool", lambda e, oc=oc: e.tensor_copy(out=ybf[:, oc, 0:T], in_=ycv[:, oc, 0:T]), reads=[ycvb[oc]], writes=[ybf.b])
                sq = sqr.next()
                sc.op("act", lambda e, oc=oc, sq=sq: e.activation(out=sq[:, 0:T], in_=ycv[:, oc, 0:T], func=AF.Square), reads=[ycvb[oc]], writes=[sq.b])
                mm(b1.ap(0, 128, 0, T), ones[:, 2, :], ybf[:, oc, 0:T], oc == 0, oc == 7, [ones.b, ybf.b], [b1.b])
                mm(b2.ap(0, 128, 0, T), ones[:, 2, :], sq[:, 0:T], oc == 0, oc == 7, [ones.b, sq.b], [b2.b])
            sc.op("dve", lambda e: e.tensor_copy(out=ln_m[:, 0:T], in_=b1.ap(0, 128, 0, T)), reads=[b1.b], writes=[ln_m.b])
            t_ = tA.next()
            sc.op("dve", lambda e: e.tensor_tensor(out=t_[:, 0:T], in0=ln_m[:, 0:T], in1=ln_m[:, 0:T], op=ALU.mult), reads=[ln_m.b], writes=[t_.b])
            t2_ = tB.next()
            sc.op("dve", lambda e: e.tensor_tensor(out=t2_[:, 0:T], in0=b2.ap(0, 128, 0, T), in1=t_[:, 0:T], op=ALU.subtract), reads=[b2.b, t_.b], writes=[t2_.b])
            sc.op("dve", lambda e: e.tensor_scalar(out=t2_[:, 0:T], in0=t2_[:, 0:T], scalar1=0.0, scalar2=None, op0=ALU.max), reads=[t2_.b], writes=[t2_.b])
            sc.op("act", lambda e: e.activation(out=t_[:, 0:T], in_=t2_[:, 0:T], func=AF.Sqrt, bias=V("c_eps"), scale=1.0), reads=[t2_.b, vec.b], writes=[t_.b])
            sc.op("dve", lambda e: e.reciprocal(out=ln_r[:, 0:T], in_=t_[:, 0:T]), reads=[t_.b], writes=[ln_r.b])
            for oc in range(8):
                t_ = tA.next()
                sc.op("dve", lambda e, oc=oc, t_=t_: e.tensor_tensor(out=t_[:, 0:T], in0=ycv[:, oc, 0:T], in1=ln_m[:, 0:T], op=ALU.subtract),
                      reads=[ycvb[oc], ln_m.b], writes=[t_.b])
                t2_ = tB.next()
                sc.op("dve", lambda e, t_=t_, t2_=t2_: e.tensor_tensor(out=t2_[:, 0:T], in0=t_[:, 0:T], in1=ln_r[:, 0:T], op=ALU.mult),
                      reads=[t_.b, ln_r.b], writes=[t2_.b])
                sc.op("act", lambda e, oc=oc, t2_=t2_: e.activation(out=zc[:, oc, 0:T], in_=t2_[:, 0:T], func=AF.Silu, bias=V("b_ln", oc), scale=V("g_ln", oc)),
                      reads=[t2_.b, vec.b], writes=[zc.b])
            for oc in range(16):
                (wao, wco, wga, wgb_), wtb = wload(4 + oc)
                bYa, bYb, bGa, bGb = rot7.next(), rot7.next(), rot7.next(), rot7.next()
                for kc in range(16):
                    mm(bYa.ap(0, 128, 0, T), wao[:, kc, :], att[:, kc, 0:T], kc == 0, kc == 15, [wtb, att.b], [bYa.b])
                for kc in range(8):
                    mm(bYb.ap(0, 128, 0, T), wco[:, kc, :], zc[:, kc, 0:T], kc == 0, kc == 7, [wtb, zc.b], [bYb.b])
                for kc in range(16):
                    mm(bGa.ap(0, 128, 0, T), wga[:, kc, :], hT[:, kc, 0:T], kc == 0, kc == 15, [wtb, hT.b], [bGa.b])
                for kc in range(16):
                    mm(bGb.ap(0, 128, 0, T), wgb_[:, kc, :], hT[:, kc, 0:T], kc == 0, kc == 15, [wtb, hT.b], [bGb.b])
                sa, sb_ = sg.next(), sg.next()
                sc.op("act", lambda e, bGa=bGa, sa=sa, oc=oc: e.activation(out=sa[:, 0:T], in_=bGa.ap(0, 128, 0, T), func=AF.Sigmoid,
                                                                          bias=V("b_gate", oc), scale=1.0), reads=[bGa.b, vec.b], writes=[sa.b])
                sc.op("act", lambda e, bGb=bGb, sb_=sb_, oc=oc: e.activation(out=sb_[:, 0:T], in_=bGb.ap(0, 128, 0, T), func=AF.Sigmoid,
                                                                            bias=V("b_gate", 16 + oc), scale=1.0), reads=[bGb.b, vec.b], writes=[sb_.b])
                t_, t2_ = tA.next(), tB.next()
                sc.op("dve", lambda e, bYa=bYa, sa=sa, t_=t_: e.tensor_tensor(out=t_[:, 0:T], in0=bYa.ap(0, 128, 0, T), in1=sa[:, 0:T], op=ALU.mult),
                      reads=[bYa.b, sa.b], writes=[t_.b])
                sc.op("dve", lambda e, bYb=bYb, sb_=sb_, t2_=t2_, oc=oc: e.scalar_tensor_tensor(
                    out=t2_[:, 0:T], in0=bYb.ap(0, 128, 0, T), scalar=V("b_co", oc), in1=sb_[:, 0:T], op0=ALU.add, op1=ALU.mult),
                    reads=[bYb.b, sb_.b, vec.b], writes=[t2_.b])
                sc.op("dve", lambda e, t_=t_, t2_=t2_, oc=oc: e.tensor_tensor(out=z2[:, oc, 0:T], in0=t_[:, 0:T], in1=t2_[:, 0:T], op=ALU.add),
                      reads=[t_.b, t2_.b], writes=[z2.b])
            for g in range(4):
                (wo,), wob = wload(20 + g)
                for o in range(4):
                    oc = g * 4 + o
                    bk = rot7.next()
                    for kc in range(16):
                        mm(bk.ap(0, 128, 0, T), wo[:, kc, o * 128:(o + 1) * 128], z2[:, kc, 0:T], kc == 0, kc == 15, [wob, z2.b], [bk.b])
                    sc.op("dve", lambda e, bk=bk, oc=oc: e.tensor_tensor(out=xt[:, oc, 0:T], in0=xt[:, oc, 0:T], in1=bk.ap(0, 128, 0, T), op=ALU.add),
                          reads=[bk.b, xt.b], writes=[xt.b])
            sc.barrier()
            bk = rot7.next()
            for kc in range(16):
                sq = sqr.next()
                sc.op("act", lambda e, kc=kc, sq=sq: e.activation(out=sq[:, 0:T], in_=xt[:, kc, 0:T], func=AF.Square), reads=[xt.b], writes=[sq.b])
                mm(bk.ap(0, 128, 0, T), ones[:, 0, :], sq[:, 0:T], kc == 0, kc == 15, [ones.b, sq.b], [bk.b])
            ps_ap_b[0] = [bk.b]
            rsqrt_bc(bk.ap(0, 128, 0, T), "c_eps", st3["rs"], 128, T, st3["rtmp"])
            rs = st3["rs"]
            for kc in range(16):
                sc.op("dve", lambda e, kc=kc: e.scalar_tensor_tensor(out=hT[:, kc, 0:T], in0=xt[:, kc, 0:T], scalar=V("g_ffn", kc), in1=rs[:, 0:T],
                                                                     op0=ALU.mult, op1=ALU.mult), reads=[xt.b, rs.b, vec.b], writes=[hT.b])
            for g in range(11):
                (wg_,), wgb2 = wload(24 + 2 * g)
                (wu_,), wub2 = wload(25 + 2 * g)
                for o in range(4):
                    fc = g * 4 + o
                    bG, bU = rot7.next(), rot7.next()
                    for kc in range(16):
                        mm(bG.ap(0, 128, 0, T), wg_[:, kc, o * 128:(o + 1) * 128], hT[:, kc, 0:T], kc == 0, kc == 15, [wgb2, hT.b], [bG.b])
                    for kc in range(16):
                        mm(bU.ap(0, 128, 0, T), wu_[:, kc, o * 128:(o + 1) * 128], hT[:, kc, 0:T], kc == 0, kc == 15, [wub2, hT.b], [bU.b])
                    s_ = sg.next()
                    sc.op("act", lambda e, bG=bG, s_=s_: e.activation(out=s_[:, 0:T], in_=bG.ap(0, 128, 0, T), func=AF.Silu), reads=[bG.b], writes=[s_.b])
                    sc.op("dve", lambda e, bU=bU, s_=s_, fc=fc: e.tensor_tensor(out=act_t[:, fc, 0:T], in0=bU.ap(0, 128, 0, T), in1=s_[:, 0:T], op=ALU.mult),
                          reads=[bU.b, s_.b], writes=[act_t.b])
            for oc in range(16):
                (wd,), wdb = wload(46 + oc)
                bk = rot7.next()
                for fc in range(44):
                    mm(bk.ap(0, 128, 0, T), wd[:, fc, :], act_t[:, fc, 0:T], fc == 0, fc == 43, [wdb, act_t.b], [bk.b])
                ys = yst.next()
                sc.op("dve", lambda e, bk=bk, oc=oc, ys=ys: e.tensor_tensor(out=ys[:, 0:T], in0=xt[:, oc, 0:T], in1=bk.ap(0, 128, 0, T), op=ALU.add),
                      reads=[bk.b, xt.b], writes=[ys.b])
                sc.dma("sp", y_out[oc * 128:(oc + 1) * 128, :], ys[:, 0:T], reads=[ys.b])
            sc.barrier()

        ycvb = [Buf() for _ in range(8)]
        DBG.update(z2=z2, zc=zc, att=att, xt=xt, hT=hT, arena=arena)
        for t in range(0 if SAMPLE_ONLY else 4):
            c = slice(t * 512, (t + 1) * 512)
            hc = slice(t * 128, (t + 1) * 128)
            uo = None
            if t == 3:
                def uo():
                    sc.dma("sp", uT_last.rearrange("(k p) t -> p k t", p=128), upad[:, :, 3, 128:160], reads=[upad.b])
            post(xT_own[:, c], 512, 4, 128, xT_halo[:, hc], hmask[:, hc], None, t * 512, yT_own[:, c], uo)

        def uo_s():
            for b in range(2):
                sc.dma("sp", uT_s[:, 32 * b:32 * b + 32].rearrange("(k p) t -> p k t", p=128), upad[:, :, b, 32:64], reads=[upad.b])
        post(xT_s[:, 0:64], 64, 2, 32, None, None, [sconvT[0], sconvT[1]], TOWN, yT_s[:, 0:64], uo_s)
        free(p3)

    sc.barrier()
    return nc


_CACHE = {}


def _prep_inputs(inp):
    f = np.float32
    xp = np.asarray(inp["x_prompt"], f)[0]
    xs = np.asarray(inp["x_sample"], f)
    xT_all = np.ascontiguousarray(xp.T)
    xt = xp.reshape(128, 128, D)
    vecs = np.zeros((128, NV), f)

    def put(name, arr):
        arr = np.asarray(arr, f)
        vecs[:arr.shape[0], VO[name]:VO[name] + arr.shape[1]] = arr

    col = lambda v: np.ascontiguousarray(np.asarray(v, f).reshape(-1, 128).T)
    put("g_mix", col(inp["g_mix_norm"][0]))
    put("g_q_a", col(inp["g_q_a"][0]))
    put("g_kv_a", col(inp["g_kv_a"][0]))
    put("b_glu", col(inp["b_glu"][0]))
    put("b_gate", col(inp["b_gate"][0]))
    wdw = np.asarray(inp["w_dw"], f)[0]
    put("w_dw", np.ascontiguousarray(wdw.T.reshape(8, 128, 31).transpose(1, 0, 2).reshape(128, 248)))
    put("b_dw", col(inp["b_dw"][0]))
    put("g_ln", col(inp["g_conv_ln"][0]))
    put("b_ln", col(inp["b_conv_ln"][0]))
    put("b_co", col(inp["b_conv_out"][0]))
    put("g_ffn", col(inp["g_ffn_norm"][0]))
    for nm, key in (("gq", "g_q_norm"), ("gk", "g_k_norm")):
        g = np.asarray(inp[key], f)[0]
        a = np.ones((128, 2), f)
        a[:, 0] = g[0:128]
        a[0:64, 1] = g[128:192]
        a[64:128, 1] = g[128:192]
        put(nm, a)
    invf = (1.0 / (np.float32(10000.0) ** (np.arange(0, 64, 2, dtype=np.float32) / np.float32(64)))).astype(f)
    iv = np.zeros((128, 1), f)
    iv[0:64, 0] = np.concatenate([invf, invf])
    iv[64:128, 0] = np.concatenate([invf, invf])
    put("invf", iv)
    put("c_eps", np.full((128, 1), EPS, f))
    put("c_eps192", np.full((128, 1), 192 * EPS, f))

    ident = np.eye(128, dtype=f)
    common = {
        "xT_all": xT_all, "vecs": vecs, "ident": ident,
        "pos_all": np.arange(S, dtype=f)[None, :],
        "pos_s": np.tile(np.arange(PAST, PAST + TS, dtype=f), 2)[None, :],
        "w_in": np.ascontiguousarray(inp["w_in"][0], f), "w_q_up": np.ascontiguousarray(inp["w_q_up"][0], f),
        "w_kv_up": np.ascontiguousarray(inp["w_kv_up"][0], f), "w_attn_out": np.ascontiguousarray(inp["w_attn_out"][0], f),
        "w_conv_out": np.ascontiguousarray(inp["w_conv_out"][0], f), "w_out": np.ascontiguousarray(inp["w_out"][0], f),
        "w_ffn_gate": np.ascontiguousarray(inp["w_ffn_gate"][0], f), "w_ffn_up": np.ascontiguousarray(inp["w_ffn_up"][0], f),
        "w_ffn_down": np.ascontiguousarray(inp["w_ffn_down"][0], f),
    }
    maps = []
    for c in range(NCORE):
        gt = np.arange(16) * 8 + c
        own = xt[gt].reshape(TOWN, D)
        halo = np.zeros((16, 32, D), f)
        hmask = np.ones((16, 32), f)
        for j, g in enumerate(gt):
            if g == 0:
                hmask[j] = 0.0
            else:
                halo[j] = xt[g - 1, 96:128]
        pos_own = (gt[:, None] * 128 + np.arange(128)[None, :]).reshape(1, TOWN).astype(f)
        mb = np.zeros((128, 8, 128), f)
        for m in range(8):
            if m > c:
                mb[:, m, :] = -30000.0
            elif m == c:
                mb[64:128, m, 0:64] = -30000.0
        d = dict(common)
        d.update({
            "xT_own": np.ascontiguousarray(own.T), "xT_halo": np.ascontiguousarray(halo.reshape(512, D).T),
            "xT_s": np.ascontiguousarray(xs[2 * c:2 * c + 2].reshape(64, D).T),
            "pos_own": pos_own, "hmask": hmask.reshape(1, 512), "mb": mb.reshape(128, 1024),
            "cckvT": np.ascontiguousarray(np.asarray(inp["cache_ckv"], f)[0, 2 * c:2 * c + 2].transpose(0, 2, 1)),
            "ckpeT": np.ascontiguousarray(np.asarray(inp["cache_kpe"], f)[0, 2 * c:2 * c + 2].transpose(0, 2, 1)),
            "sconvT": np.ascontiguousarray(np.asarray(inp["state_conv"], f)[0, 2 * c:2 * c + 2].transpose(0, 2, 1)),
        })
        maps.append(d)
    return maps


def kernel(**inp):
    if "nc" not in _CACHE:
        _CACHE["nc"] = build_program()
    nc = _CACHE["nc"]
    maps = _prep_inputs(inp)
    res = run_bass_kernel_spmd(nc, maps, core_ids=list(range(NCORE)))
    R = res.results
    f = np.float32
    y_p = np.zeros((1, S, D), f)
    ckv_p = np.zeros((1, 1, S, 512), f)
    kpe_p = np.zeros((1, 1, S, 64), f)
    y_s = np.zeros((16, TS, D), f)
    ckv_s = np.zeros((1, 16, TS, 512), f)
    kpe_s = np.zeros((1, 16, TS, 64), f)
    conv_s = np.zeros((1, 16, 30, CONV), f)
    for c in range(NCORE):
        r = R[c]
        gt = np.arange(16) * 8 + c
        idx = (gt[:, None] * 128 + np.arange(128)[None, :]).reshape(-1)
        y_p[0, idx] = np.asarray(r["yT_own"]).T
        ckv_p[0, 0, idx] = np.asarray(r["ckvT_own"]).T
        kpe_p[0, 0, idx] = np.asarray(r["kpeT_own"]).T
        y_s[2 * c:2 * c + 2] = np.asarray(r["yT_s"]).T.reshape(2, TS, D)
        ckv_s[0, 2 * c:2 * c + 2] = np.asarray(r["ckvT_s"]).T.reshape(2, TS, 512)
        kpe_s[0, 2 * c:2 * c + 2] = np.asarray(r["kpeT_s"]).T.reshape(2, TS, 64)
        us = np.asarray(r["uT_s"]).T.reshape(2, TS, CONV)
        conv_s[0, 2 * c:2 * c + 2] = us[:, 2:32]
    conv_p = np.asarray(R[7]["uT_last"]).T[2:32][None, None]
    return (y_p, y_s, ckv_p, kpe_p, conv_p.astype(f), ckv_s, kpe_s, conv_s)
```

```python
import numpy as np
import concourse.bass as bass
import concourse.mybir as mybir
from concourse.bass_utils import run_bass_kernel_spmd

F32 = mybir.dt.float32
BF16 = mybir.dt.bfloat16
AF = mybir.ActivationFunctionType
ALU = mybir.AluOpType

NCORE = 8
D = 2048
S = 16384
TOWN = 2048
NH = 16
IND = 7232
DFF = 5632
CONV = 1024
CW = 31
PAST = 4096
TS = 32
EPS = 1e-6
NKT = 128 + 2 * 33
NK = NKT * 128
MAGIC = 12582912.0
C1 = 6.28125
C2 = 0.0019353071795864769
PI = 3.1415925

VO = {}
_o = 0
for _n, _w in (("g_mix", 16), ("g_q_a", 4), ("g_kv_a", 4), ("b_glu", 16), ("b_gate", 32), ("w_dw", 248),
               ("b_dw", 8), ("g_ln", 8), ("b_ln", 8), ("b_co", 16), ("g_ffn", 16), ("gq", 2), ("gk", 2),
               ("invf", 1), ("c_eps", 1), ("c_eps192", 1)):
    VO[_n] = _o
    _o += _w
NV = _o

import os
PHASES = set(os.environ.get("KPHASES", "p1a,p1b,p2,p3").split(","))
SAMPLE_ONLY = False
DBG = {}


class Buf:
    __slots__ = ("w", "r", "x")

    def __init__(self, x=False):
        self.w = {}
        self.r = {}
        self.x = x


class Sched:
    def __init__(self, nc):
        self.nc = nc
        self.eng = {"pe": nc.tensor, "act": nc.scalar, "dve": nc.vector, "pool": nc.gpsimd, "sp": nc.sync}
        self.sems = {}
        self.cnt = {}
        self.seen = {e: {} for e in self.eng}
        for e in ("pe", "act", "dve", "pool"):
            self.sems[e] = nc.semaphore("s_" + e).__enter__()
            self.cnt[e] = 0
        self.ND = 8
        self.dq = {}
        for q in ("sp", "pool"):
            names = [f"d_{q}{i}" for i in range(self.ND)]
            for n in names:
                self.sems[n] = nc.semaphore(n).__enter__()
                self.cnt[n] = 0
            self.dq[q] = [names, 0]

    def _wait(self, e, key, val):
        if self.seen[e].get(key, 0) >= val:
            return
        self.eng[e].wait_ge(self.sems[key], val)
        self.seen[e][key] = val

    def _deps(self, e, reads, writes):
        need = {}

        def add(k, v, war=False):
            if k == e and e == "pe":
                return
            if need.get(k, 0) < v:
                need[k] = v

        for b in reads:
            for k, v in b.w.items():
                add(k, v)
        for b in writes:
            for k, v in b.w.items():
                add(k, v)
            for k, v in b.r.items():
                add(k, v, True)
        for k, v in need.items():
            self._wait(e, k, v)

    def _mark(self, tok, reads, writes):
        k, v = tok
        for b in reads:
            if b.r.get(k, 0) < v:
                b.r[k] = v
        for b in writes:
            if b.w.get(k, 0) < v:
                b.w[k] = v

    def op(self, e, fn, reads=(), writes=()):
        writes = list(writes) + [b for b in reads if b.x]
        reads = [b for b in reads if not b.x]
        self._deps(e, reads, writes)
        ins = fn(self.eng[e])
        self.cnt[e] += 1
        ins.then_inc(self.sems[e], 1)
        tok = (e, self.cnt[e])
        self._mark(tok, reads, writes)
        return tok

    def dma(self, q, out, in_, reads=(), writes=()):
        self._deps(q, reads, writes)
        names, i = self.dq[q]
        n = names[i]
        self.dq[q][1] = (i + 1) % self.ND
        if self.cnt[n] > 0:
            self._wait(q, n, self.cnt[n])
        ins = self.eng[q].dma_start(out=out, in_=in_)
        self.cnt[n] += 16
        ins.then_inc(self.sems[n], 16)
        tok = (n, self.cnt[n])
        self._mark(tok, reads, writes)
        return tok

    def barrier(self):
        for e in self.eng:
            for k, v in self.cnt.items():
                if k != e and v > 0:
                    self._wait(e, k, v)


class Tl:
    def __init__(self, t):
        self.t = t
        self.b = Buf()

    def __getitem__(self, k):
        return self.t[k]


class Ring:
    def __init__(self, tiles):
        self.tiles = tiles
        self.i = 0

    def next(self):
        t = self.tiles[self.i]
        self.i = (self.i + 1) % len(self.tiles)
        return t


def build_program():
    nc = bass.Bass("TRN2", target_bir_lowering=False)
    sc = Sched(nc)
    ctx = []

    def dram_in(name, shape):
        return nc.dram_tensor(name, list(shape), F32, kind="ExternalInput").ap()

    def dram_out(name, shape):
        return nc.dram_tensor(name, list(shape), F32, kind="ExternalOutput").ap()

    xT_all = dram_in("xT_all", [D, S])
    xT_own = dram_in("xT_own", [D, TOWN])
    xT_halo = dram_in("xT_halo", [D, 512])
    xT_s = dram_in("xT_s", [D, 64])
    pos_all = dram_in("pos_all", [1, S])
    pos_own = dram_in("pos_own", [1, TOWN])
    pos_s = dram_in("pos_s", [1, 64])
    hmask = dram_in("hmask", [1, 512])
    mb_in = dram_in("mb", [128, 1024])
    ident_in = dram_in("ident", [128, 128])
    cckvT = dram_in("cckvT", [2, 512, PAST])
    ckpeT = dram_in("ckpeT", [2, 64, PAST])
    sconvT = dram_in("sconvT", [2, CONV, 30])
    vecs_in = dram_in("vecs", [128, NV])
    w_in = dram_in("w_in", [D, IND])
    w_q_up = dram_in("w_q_up", [512, 3072])
    w_kv_up = dram_in("w_kv_up", [512, 4096])
    w_ao = dram_in("w_attn_out", [D, D])
    w_co = dram_in("w_conv_out", [CONV, D])
    w_o = dram_in("w_out", [D, D])
    w_fg = dram_in("w_ffn_gate", [D, DFF])
    w_fu = dram_in("w_ffn_up", [D, DFF])
    w_fd = dram_in("w_ffn_down", [DFF, D])

    yT_own = dram_out("yT_own", [D, TOWN])
    ckvT_own = dram_out("ckvT_own", [512, TOWN])
    kpeT_own = dram_out("kpeT_own", [64, TOWN])
    uT_last = dram_out("uT_last", [CONV, 32])
    yT_s = dram_out("yT_s", [D, 64])
    ckvT_s = dram_out("ckvT_s", [512, 64])
    kpeT_s = dram_out("kpeT_s", [64, 64])
    uT_s = dram_out("uT_s", [CONV, 64])

    KT = nc.dram_tensor("KT_scr", [NH, 128, NK], BF16).ap()
    KR = nc.dram_tensor("KR_scr", [64, NK], BF16).ap()
    V2 = nc.dram_tensor("V2_scr", [NH, 128, NKT, 128], BF16).ap()
    AT = nc.dram_tensor("AT_scr", [NH, 128, TOWN + 64], BF16).ap()
    WB = nc.dram_tensor("WB_scr", [62, 128, 8192], BF16).ap()

    cnt = [0]

    def sb(shape, dt, stack=None):
        cnt[0] += 1
        g = nc.sbuf_tensor(f"t{cnt[0]}", list(shape), dt)
        t = g.__enter__()
        (stack if stack is not None else ctx).append(g)
        return Tl(t)

    def free(stack):
        sc.barrier()
        while stack:
            stack.pop().__exit__(None, None, None)

    pp = []
    for i in range(4):
        g = nc.psum_tensor(f"pp{i}", [128, 1024], F32)
        pp.append(g.__enter__())
        ctx.append(g)
    bankb = [Buf(True) for _ in range(8)]

    class Bank:
        def __init__(self, i):
            self.i = i
            self.b = bankb[i]

        def ap(self, p0, p1, c0, c1):
            return pp[self.i // 2][p0:p1, (self.i % 2) * 512 + c0:(self.i % 2) * 512 + c1]

    banks = [Bank(i) for i in range(8)]
    rot7 = Ring(banks[0:7])
    rot2 = Ring(banks[6:8])

    def mm(out, lhsT, rhs, start, stop, R, W):
        return sc.op("pe", lambda e: e.matmul(out, lhsT=lhsT, rhs=rhs, start=start, stop=stop), reads=R, writes=W)

    vec = sb([128, NV], F32)
    ones = sb([128, 4, 128], BF16)
    onesf = sb([128, 128], F32)
    cos_s = sb([128, 64], F32)
    sin_s = sb([128, 64], F32)
    cqn_s = sb([128, 4, 64], BF16)
    ckvn_s = sb([128, 4, 64], BF16)
    kpe_s = sb([64, 64], BF16)
    sqr = Ring([sb([128, 512], BF16) for _ in range(3)])
    p12 = []
    scr_b = Buf()
    at_scr_b = Buf()
    ident = sb([128, 128], BF16, p12)
    mbt = sb([128, 8, 128], BF16, p12)
    gg = sb([128, 2], F32, p12)
    rstdk = sb([128, NKT, NH], F32, p12)
    cos_own = sb([128, TOWN], F32, p12)
    sin_own = sb([128, TOWN], F32, p12)
    cqn = sb([128, 4, TOWN], BF16, p12)

    sc.dma("sp", vec[:], vecs_in, writes=[vec.b])

    def V(name, i=0, p1=128):
        o = VO[name] + i
        return vec[0:p1, o:o + 1]

    for i, v in enumerate((1.0 / 2048, 1.0 / 512, 1.0 / 1024, 1.0)):
        sc.op("pool", lambda e, i=i, v=v: e.memset(ones[:, i, :], v), writes=[ones.b])
    sc.op("pool", lambda e: e.memset(onesf[:], 1.0), writes=[onesf.b])
    sc.dma("pool", ident[:], ident_in, writes=[ident.b])
    sc.dma("pool", mbt[:].rearrange("p a b -> p (a b)"), mb_in, writes=[mbt.b])
    sc.op("dve", lambda e: e.scalar_tensor_tensor(out=gg[:], in0=vec[:, VO["gq"]:VO["gq"] + 2], scalar=float(np.sqrt(192.0)),
                                                  in1=vec[:, VO["gk"]:VO["gk"] + 2], op0=ALU.mult, op1=ALU.mult),
          reads=[vec.b], writes=[gg.b])
    sc.op("pool", lambda e: e.memset(rstdk[:], 1.0), writes=[rstdk.b])

    def rsqrt_bc(ps_ap, epsname, out_tl, np_, T, tmp_tl):
        sc.op("act", lambda e: e.activation(out=tmp_tl[0:np_, 0:T], in_=ps_ap, func=AF.Sqrt, bias=V(epsname, 0, np_), scale=1.0),
              reads=[vec.b] + ps_ap_b[0], writes=[tmp_tl.b])
        sc.op("dve", lambda e: e.reciprocal(out=out_tl[0:np_, 0:T], in_=tmp_tl[0:np_, 0:T]), reads=[tmp_tl.b], writes=[out_tl.b])

    ps_ap_b = [[]]

    def sin_of(ang, out, T, t1, t2, shift, NP=64):
        src = ang
        if shift != 0.0:
            sc.op("dve", lambda e: e.tensor_scalar(out=t1[0:NP, 0:T], in0=ang[0:NP, 0:T], scalar1=float(shift), scalar2=None, op0=ALU.add),
                  reads=[ang.b], writes=[t1.b])
            src = t1
        sc.op("dve", lambda e: e.tensor_scalar(out=t2[0:NP, 0:T], in0=src[0:NP, 0:T], scalar1=float(1.0 / (2 * np.pi)), scalar2=MAGIC,
                                               op0=ALU.mult, op1=ALU.add), reads=[src.b], writes=[t2.b])
        sc.op("dve", lambda e: e.tensor_scalar(out=t2[0:NP, 0:T], in0=t2[0:NP, 0:T], scalar1=MAGIC, scalar2=None, op0=ALU.subtract),
              reads=[t2.b], writes=[t2.b])
        sc.op("dve", lambda e: e.scalar_tensor_tensor(out=t1[0:NP, 0:T], in0=t2[0:NP, 0:T], scalar=-C1, in1=src[0:NP, 0:T], op0=ALU.mult, op1=ALU.add),
              reads=[t2.b, src.b], writes=[t1.b])
        sc.op("dve", lambda e: e.scalar_tensor_tensor(out=t1[0:NP, 0:T], in0=t2[0:NP, 0:T], scalar=-C2, in1=t1[0:NP, 0:T], op0=ALU.mult, op1=ALU.add),
              reads=[t2.b, t1.b], writes=[t1.b])
        sc.op("dve", lambda e: e.tensor_scalar(out=t1[0:NP, 0:T], in0=t1[0:NP, 0:T], scalar1=PI, scalar2=-PI, op0=ALU.min, op1=ALU.max),
              reads=[t1.b], writes=[t1.b])
        sc.op("act", lambda e: e.activation(out=out, in_=t1[0:NP, 0:T], func=AF.Sin),
              reads=[t1.b], writes=[out.b if isinstance(out, Tl) else out_b[0]])

    out_b = [None]

    def rope_tables(pos_ap, T, cos_ap, sin_ap, dst_b, tmps, NP=64):
        pb, ang, t1, t2 = tmps
        sc.dma("sp", pb[0:NP, 0:T], pos_ap.partition_broadcast(NP), writes=[pb.b])
        sc.op("dve", lambda e: e.tensor_scalar(out=ang[0:NP, 0:T], in0=pb[0:NP, 0:T], scalar1=V("invf", 0, NP), scalar2=None, op0=ALU.mult),
              reads=[pb.b, vec.b], writes=[ang.b])
        out_b[0] = dst_b
        sin_of(ang, sin_ap, T, t1, t2, 0.0, NP)
        sin_of(ang, cos_ap, T, t1, t2, float(np.pi / 2), NP)

    def rms_h(xt, hT, T, st):
        bk = rot7.next()
        for kc in range(16):
            sq = sqr.next()
            sc.op("act", lambda e: e.activation(out=sq[:, 0:T], in_=xt[:, kc, 0:T], func=AF.Square), reads=[xt.b], writes=[sq.b])
            mm(bk.ap(0, 128, 0, T), ones[:, 0, :], sq[:, 0:T], kc == 0, kc == 15, [ones.b, sq.b], [bk.b])
        ps_ap_b[0] = [bk.b]
        rsqrt_bc(bk.ap(0, 128, 0, T), "c_eps", st["rs"], 128, T, st["rtmp"])
        rs = st["rs"]
        for kc in range(16):
            sc.op("dve", lambda e, kc=kc: e.scalar_tensor_tensor(out=hT[:, kc, 0:T], in0=xt[:, kc, 0:T], scalar=V("g_mix", kc),
                                                                 in1=rs[:, 0:T], op0=ALU.mult, op1=ALU.mult),
                  reads=[xt.b, rs.b, vec.b], writes=[hT.b])

    def proj16(bk, M, wt, c0, hT, T):
        for kc in range(16):
            mm(bk.ap(0, M, 0, T), wt[:, kc, c0:c0 + M], hT[:, kc, 0:T], kc == 0, kc == 15, [wt.b, hT.b], [bk.b])

    def lowrank_norm(wkq, c0, gname, hT, T, st, dst_fn, dst_bufs):
        raw = st["raw"]
        bk2 = rot7.next()
        pend = None
        for oc in range(4):
            bk = rot7.next()
            proj16(bk, 128, wkq, c0 + 128 * oc, hT, T)
            sc.op("dve", lambda e, oc=oc, bk=bk: e.tensor_copy(out=raw[:, oc, 0:T], in_=bk.ap(0, 128, 0, T)), reads=[bk.b], writes=[raw.b])
            sq = sqr.next()
            sc.op("act", lambda e, bk=bk, sq=sq: e.activation(out=sq[:, 0:T], in_=bk.ap(0, 128, 0, T), func=AF.Square), reads=[bk.b], writes=[sq.b])
            if pend is not None:
                mm(bk2.ap(0, 128, 0, T), ones[:, 1, :], pend[1][:, 0:T], pend[0] == 0, False, [ones.b, pend[1].b], [bk2.b])
            pend = (oc, sq)
        mm(bk2.ap(0, 128, 0, T), ones[:, 1, :], pend[1][:, 0:T], False, True, [ones.b, pend[1].b], [bk2.b])
        ps_ap_b[0] = [bk2.b]
        rsqrt_bc(bk2.ap(0, 128, 0, T), "c_eps", st["rs"], 128, T, st["rtmp"])
        rs = st["rs"]
        for oc in range(4):
            sc.op("dve", lambda e, oc=oc: e.scalar_tensor_tensor(out=dst_fn(oc), in0=raw[:, oc, 0:T], scalar=V(gname, oc), in1=rs[:, 0:T],
                                                                 op0=ALU.mult, op1=ALU.mult),
                  reads=[raw.b, rs.b, vec.b], writes=dst_bufs)

    def kpe_rope(wkq, hT, T, cos_ap, sin_ap, cs_bufs, st, dst_ap, dst_bufs, coff=0):
        bA = rot7.next()
        proj16(bA, 64, wkq, 1024 - coff, hT, T)
        bB = rot7.next()
        proj16(bB, 64, wkq, 1088 - coff, hT, T)
        t1, t2 = st["r1"], st["r2"]
        sc.op("dve", lambda e: e.tensor_tensor(out=t1[:, 0:T], in0=bA.ap(0, 64, 0, T), in1=cos_ap, op=ALU.mult), reads=[bA.b] + cs_bufs, writes=[t1.b])
        sc.op("dve", lambda e: e.tensor_tensor(out=t2[:, 0:T], in0=bB.ap(0, 64, 0, T), in1=sin_ap, op=ALU.mult), reads=[bB.b] + cs_bufs, writes=[t2.b])
        sc.op("dve", lambda e: e.tensor_tensor(out=dst_ap, in0=t1[:, 0:T], in1=t2[:, 0:T], op=ALU.add), reads=[t1.b, t2.b], writes=dst_bufs)

    p1 = []
    wkq = sb([128, 16, 1152], BF16, p1)
    for kc in range(16):
        sc.dma("pool", wkq[:, kc, 0:1088], w_in[kc * 128:(kc + 1) * 128, 0:1088], writes=[wkq.b])
    sc.op("pool", lambda e: e.tensor_scalar(out=wkq[:, :, 1088:1120], in0=wkq[:, :, 1056:1088], scalar1=-1.0, scalar2=None, op0=ALU.mult),
          reads=[wkq.b], writes=[wkq.b])
    sc.op("pool", lambda e: e.tensor_copy(out=wkq[:, :, 1120:1152], in_=wkq[:, :, 1024:1056]), reads=[wkq.b], writes=[wkq.b])

    st = {"rs": sb([128, 512], F32, p1), "rtmp": sb([128, 512], F32, p1), "raw": sb([128, 4, 512], F32, p1),
          "r1": sb([64, 512], F32, p1), "r2": sb([64, 512], F32, p1)}
    rtm = [sb([128, 512], F32, p1) for _ in range(4)]
    xts = Ring([sb([128, 16, 512], F32, p1) for _ in range(1)])
    hTs = Ring([sb([128, 16, 512], BF16, p1) for _ in range(1)])
    ckvf = Ring([sb([128, 4, 512], F32, p1) for _ in range(1)])
    kpef = Ring([sb([64, 512], F32, p1) for _ in range(1)])

    for t in range(4):
        out_b[0] = cos_own.b
        rope_tables(pos_own[:, t * 512:(t + 1) * 512], 512, cos_own[:, t * 512:(t + 1) * 512], sin_own[:, t * 512:(t + 1) * 512], cos_own.b, rtm, 128)
    rope_tables(pos_s[:, 0:64], 64, cos_s[:, 0:64], sin_s[:, 0:64], cos_s.b, rtm, 128)
    sin_own.b = cos_own.b
    sin_s.b = cos_s.b

    def own_tokens(x_ap, T, cos_ap, sin_ap, cs_b, cq_dst, cq_b, ckv_out, kpe_out, sample):
        xt = xts.next()
        hT = hTs.next()
        sc.dma("sp", xt[:, :, 0:T], x_ap.rearrange("(k p) t -> p k t", p=128), writes=[xt.b])
        rms_h(xt, hT, T, st)
        lowrank_norm(wkq, 0, "g_q_a", hT, T, st, lambda oc: cq_dst(oc), [cq_b])
        cf = ckvf.next()
        lowrank_norm(wkq, 512, "g_kv_a", hT, T, st, lambda oc: cf[:, oc, 0:T], [cf.b])
        sc.dma("sp", ckv_out.rearrange("(k p) t -> p k t", p=128), cf[:, :, 0:T], reads=[cf.b])
        kf = kpef.next()
        kpe_rope(wkq, hT, T, cos_ap, sin_ap, [cs_b], st, kf[:, 0:T], [kf.b])
        sc.dma("sp", kpe_out, kf[:, 0:T], reads=[kf.b])
        if sample:
            sc.op("dve", lambda e: e.tensor_copy(out=ckvn_s[:, :, 0:T], in_=cf[:, :, 0:T]), reads=[cf.b], writes=[ckvn_s.b])
            sc.op("dve", lambda e: e.tensor_copy(out=kpe_s[:, 0:T], in_=kf[:, 0:T]), reads=[kf.b], writes=[kpe_s.b])

    if "p1a" in PHASES:
        for t in range(0 if SAMPLE_ONLY else 4):
            c = slice(t * 512, (t + 1) * 512)
            own_tokens(xT_own[:, c], 512, cos_own[0:64, c], sin_own[0:64, c], cos_own.b,
                       lambda oc, c=c: cqn[:, oc, c], cqn.b, ckvT_own[:, c], kpeT_own[:, c], False)
        own_tokens(xT_s[:, 0:64], 64, cos_s[0:64, 0:64], sin_s[0:64, 0:64], cos_s.b,
                   lambda oc: cqn_s[:, oc, 0:64], cqn_s.b, ckvT_s[:, 0:64], kpeT_s[:, 0:64], True)

    free(p1)
    p1 = []
    if os.environ.get("KDEBUG"):
        print("sbuf remaining before P1b", nc.sbuf_bytes_remaining)
    wkq = sb([128, 16, 640], BF16, p1)
    for kc in range(16):
        sc.dma("pool", wkq[:, kc, 0:576], w_in[kc * 128:(kc + 1) * 128, 512:1088], writes=[wkq.b])
    sc.op("pool", lambda e: e.tensor_scalar(out=wkq[:, :, 576:608], in0=wkq[:, :, 544:576], scalar1=-1.0, scalar2=None, op0=ALU.mult),
          reads=[wkq.b], writes=[wkq.b])
    sc.op("pool", lambda e: e.tensor_copy(out=wkq[:, :, 608:640], in_=wkq[:, :, 512:544]), reads=[wkq.b], writes=[wkq.b])
    wkv = sb([128, 4, 4096], BF16, p1)
    for kc in range(4):
        sc.dma("pool", wkv[:, kc, :], w_kv_up[kc * 128:(kc + 1) * 128, :], writes=[wkv.b])
    rtm = [sb([64, 512], F32, p1) for _ in range(4)]
    st = {"rs": sb([128, 512], F32, p1), "rtmp": sb([128, 512], F32, p1), "raw": sb([128, 4, 512], F32, p1),
          "r1": rtm[2], "r2": rtm[3]}
    xts = Ring([sb([128, 16, 512], BF16, p1) for _ in range(2)])
    hTs = xts
    kbufs = Ring([sb([128, 16, 512], BF16, p1) for _ in range(1)])
    vbufs = Ring([sb([128, 4, 16, 128], BF16, p1) for _ in range(1)])
    sqrope = Ring([sb([64, 512], BF16, p1) for _ in range(2)])
    ssq_bank = banks[7]
    wkv_v = wkv[:].rearrange("p k (h c) -> p k h c", c=256)

    def gen_kv(ckvn, kpeb, T, kt0):
        nsub = (T + 127) // 128
        nk = min(128, T)
        k0 = kt0 * 128
        kpe_ap, kpe_b = kpeb
        sc.dma("sp", KR[:, k0:k0 + T], kpe_ap, reads=[kpe_b], writes=[scr_b])
        sr = sqrope.next()
        sc.op("pool", lambda e: e.tensor_tensor(out=sr[:, 0:T], in0=kpe_ap, in1=kpe_ap, op=ALU.mult), reads=[kpe_b], writes=[sr.b])
        kb = kbufs.next()

        def k_tiny(h, sq):
            for sub in range(nsub):
                col = sub * 16 + h
                mm(ssq_bank.ap(0, nk, col, col + 1), sq[:, sub * 128:sub * 128 + nk], ones[:, 3, 0:1], True, False, [sq.b, ones.b], [ssq_bank.b])
                mm(ssq_bank.ap(0, nk, col, col + 1), sr[0:64, sub * 128:sub * 128 + nk], ones[0:64, 3, 0:1], False, True, [sr.b, ones.b], [ssq_bank.b])
        pend = None
        for h in range(NH):
            bk = rot7.next()
            for kc in range(4):
                mm(bk.ap(0, 128, 0, T), wkv[:, kc, h * 256:h * 256 + 128], ckvn[:, kc, 0:T], kc == 0, kc == 3, [wkv.b, ckvn.b], [bk.b])
            sc.op("dve", lambda e, h=h, bk=bk: e.tensor_copy(out=kb[:, h, 0:T], in_=bk.ap(0, 128, 0, T)), reads=[bk.b], writes=[kb.b])
            sq = sqr.next()
            sc.op("act", lambda e, bk=bk, sq=sq: e.activation(out=sq[:, 0:T], in_=bk.ap(0, 128, 0, T), func=AF.Square), reads=[bk.b], writes=[sq.b])
            if pend is not None:
                k_tiny(*pend)
            pend = (h, sq)
        k_tiny(*pend)
        tmp = st["rtmp"]
        sc.op("act", lambda e: e.activation(out=tmp[0:nk, 0:nsub * 16], in_=ssq_bank.ap(0, nk, 0, nsub * 16), func=AF.Sqrt,
                                            bias=V("c_eps192", 0, nk), scale=1.0), reads=[ssq_bank.b, vec.b], writes=[tmp.b])
        sc.op("dve", lambda e: e.reciprocal(out=rstdk[0:nk, kt0:kt0 + nsub, :], in_=tmp[0:nk, 0:nsub * 16].rearrange("p (s h) -> p s h", h=16)),
              reads=[tmp.b], writes=[rstdk.b])
        sc.dma("sp", KT[:, :, k0:k0 + T].rearrange("h d t -> d h t"), kb[:, :, 0:T], reads=[kb.b], writes=[scr_b])
        vb = vbufs.next()
        if nk < 128:
            sc.op("pool", lambda e: e.memset(vb[:, 0, :, :], 0.0), writes=[vb.b])
        for sub in range(nsub):
            for hg in range(4):
                bk = rot7.next()
                for kc in range(4):
                    mm(bk.ap(0, nk, 0, 512).rearrange("p (h d) -> p h d", d=128), ckvn[:, kc, sub * 128:sub * 128 + nk],
                       wkv_v[:, kc, hg * 4:(hg + 1) * 4, 128:256], kc == 0, kc == 3, [wkv.b, ckvn.b], [bk.b])
                eng = "act" if (hg % 2 == 0) else "dve"
                if eng == "act":
                    sc.op("act", lambda e, bk=bk, sub=sub, hg=hg: e.copy(out=vb[0:nk, sub, hg * 4:(hg + 1) * 4, :],
                                                                         in_=bk.ap(0, nk, 0, 512).rearrange("p (h d) -> p h d", d=128)),
                          reads=[bk.b], writes=[vb.b])
                else:
                    sc.op("dve", lambda e, bk=bk, sub=sub, hg=hg: e.tensor_copy(out=vb[0:nk, sub, hg * 4:(hg + 1) * 4, :],
                                                                                in_=bk.ap(0, nk, 0, 512).rearrange("p (h d) -> p h d", d=128)),
                          reads=[bk.b], writes=[vb.b])
        for sub in range(nsub):
            sc.dma("sp", V2[:, :, kt0 + sub, :].rearrange("h p d -> p h d"), vb[:, sub, :, :], reads=[vb.b], writes=[scr_b])

    if "p1b" in PHASES:
        ckvb = Ring([sb([128, 4, 512], BF16, p1) for _ in range(2)])
        kpeb = Ring([sb([64, 512], BF16, p1) for _ in range(2)])
        cst = rtm
        cosb = sb([64, 512], F32, p1)
        sinb = sb([64, 512], F32, p1)
        sinb.b = cosb.b
        for t in range(0 if SAMPLE_ONLY else S // 512):
            c = slice(t * 512, (t + 1) * 512)
            xt = xts.next()
            hT = hTs.next()
            sc.dma("pool", xt[:, :, :], xT_all[:, c].rearrange("(k p) t -> p k t", p=128), writes=[xt.b])
            rms_h(xt, hT, 512, st)
            cb = ckvb.next()
            lowrank_norm(wkq, 0, "g_kv_a", hT, 512, st, lambda oc, cb=cb: cb[:, oc, :], [cb.b])
            out_b[0] = cosb.b
            rope_tables(pos_all[:, c], 512, cosb[:, :], sinb[:, :], cosb.b, cst)
            kb_ = kpeb.next()
            kpe_rope(wkq, hT, 512, cosb[:, :], sinb[:, :], [cosb.b], st, kb_[:, :], [kb_.b], coff=512)
            gen_kv(cb, (kb_[:, :], kb_.b), 512, t * 4)
        for b in range(2):
            for t in range(8):
                c = slice(t * 512, (t + 1) * 512)
                cb = ckvb.next()
                sc.dma("pool", cb[:, :, :], cckvT[b, :, c].rearrange("(k p) t -> p k t", p=128), writes=[cb.b])
                kb_ = kpeb.next()
                sc.dma("pool", kb_[:, :], ckpeT[b, :, c], writes=[kb_.b])
                gen_kv(cb, (kb_[:, :], kb_.b), 512, 128 + 33 * b + 4 * t)
            cb = ckvb.next()
            sc.op("dve", lambda e, cb=cb, b=b: e.tensor_copy(out=cb[:, :, 0:32], in_=ckvn_s[:, :, 32 * b:32 * b + 32]), reads=[ckvn_s.b], writes=[cb.b])
            kb_ = kpeb.next()
            sc.op("dve", lambda e, kb_=kb_, b=b: e.tensor_copy(out=kb_[:, 0:32], in_=kpe_s[:, 32 * b:32 * b + 32]), reads=[kpe_s.b], writes=[kb_.b])
            gen_kv(cb, (kb_[:, 0:32], kb_.b), 32, 128 + 33 * b + 32)
    free(p1)


    wblocks = []
    for g in range(2):
        wblocks.append([(0, w_in, 16, 1088 + 512 * g, 512)])
        wblocks.append([(0, w_in, 16, 2112 + 512 * g, 512)])
    for oc in range(16):
        wblocks.append([(0, w_ao, 16, 128 * oc, 128), (2048, w_co, 8, 128 * oc, 128),
                        (3072, w_in, 16, 3136 + 128 * oc, 128), (5120, w_in, 16, 3136 + 2048 + 128 * oc, 128)])
    for g in range(4):
        wblocks.append([(0, w_o, 16, 512 * g, 512)])
    for g in range(11):
        wblocks.append([(0, w_fg, 16, 512 * g, 512)])
        wblocks.append([(0, w_fu, 16, 512 * g, 512)])
    for oc in range(16):
        wblocks.append([(0, w_fd, 44, 128 * oc, 128)])
    assert len(wblocks) == 62
    wb_scr_b = Buf()

    def convert_block(bi):
        for (off, w, KC, c0, ncols) in wblocks[bi]:
            sc.dma("pool", WB[bi, :, off:off + KC * ncols].rearrange("p (k c) -> p k c", c=ncols),
                   w[0:KC * 128, c0:c0 + ncols].rearrange("(k p) c -> p k c", p=128), writes=[wb_scr_b])
    conv_done = [0]

    def convert_upto(n):
        while conv_done[0] < min(n, 62):
            convert_block(conv_done[0])
            conv_done[0] += 1

    if "p2" in PHASES:
        p2 = []
        wqs = Ring([sb([128, 4, 384], BF16, p2) for _ in range(2)])
        qns = Ring([sb([128, TOWN + 64], BF16, p2) for _ in range(2)])
        qrs = Ring([sb([128, TOWN + 64], BF16, p2) for _ in range(2)])
        kch = Ring([sb([128, 33 * 128], BF16, p2) for _ in range(2)])
        krch = Ring([sb([128, 33 * 128], BF16, p2) for _ in range(2)])
        vch = Ring([sb([128, 33, 128], BF16, p2) for _ in range(2)])
        pTs = Ring([sb([128, 1024], BF16, p2) for _ in range(3)])
        pTs_small = Ring([sb([128, 64], BF16, p2) for _ in range(6)])
        racc = sb([128, 1024], F32, p2)
        rr = sb([128, 1024], F32, p2)
        ato = Ring([sb([128, 1024], BF16, p2) for _ in range(2)])
        q32 = sb([128, 512], F32, p2)
        qt1 = sb([128, 512], F32, p2)
        qt2 = sb([128, 512], F32, p2)
        rq = sb([128, 512], F32, p2)
        rqt = sb([128, 512], F32, p2)
        sqq = Ring([sb([128, 512], BF16, p2) for _ in range(2)])
        Sb = [(banks[0], banks[1]), (banks[2], banks[3]), (banks[6], banks[7])]
        Ob = (banks[4], banks[5])
        zero_t = sb([128, 512], BF16, p2)
        sc.op("pool", lambda e: e.memset(zero_t[:, :], 0.0), writes=[zero_t.b])

        def compute_q(wq, src, src_b, c0, T, cos_ap, sin_ap, cs_b, qn, qr, d0):
            bA = rot2.next()
            for kc in range(4):
                mm(bA.ap(0, 128, 0, T), wq[:, kc, 0:128], src(kc, c0, T), kc == 0, kc == 3, [wq.b, src_b], [bA.b])
            sq = sqq.next()
            sc.op("act", lambda e: e.activation(out=sq[:, 0:T], in_=bA.ap(0, 128, 0, T), func=AF.Square), reads=[bA.b], writes=[sq.b])
            bB = rot2.next()
            for kc in range(4):
                mm(bB.ap(0, 128, 0, T), wq[:, kc, 128:256], src(kc, c0, T), kc == 0, kc == 3, [wq.b, src_b], [bB.b])
            sc.op("dve", lambda e: e.tensor_tensor(out=qt1[:, 0:T], in0=bB.ap(0, 128, 0, T), in1=cos_ap, op=ALU.mult), reads=[bB.b, cs_b], writes=[qt1.b])
            for kc in range(4):
                mm(bB.ap(0, 128, 0, T), wq[:, kc, 256:384], src(kc, c0, T), kc == 0, kc == 3, [wq.b, src_b], [bB.b])
            sc.op("dve", lambda e: e.tensor_tensor(out=qt2[:, 0:T], in0=bB.ap(0, 128, 0, T), in1=sin_ap, op=ALU.mult), reads=[bB.b, cs_b], writes=[qt2.b])
            sc.op("dve", lambda e: e.tensor_tensor(out=q32[:, 0:T], in0=qt1[:, 0:T], in1=qt2[:, 0:T], op=ALU.add), reads=[qt1.b, qt2.b], writes=[q32.b])
            sq2 = sqq.next()
            sc.op("act", lambda e: e.activation(out=sq2[0:64, 0:T], in_=q32[0:64, 0:T], func=AF.Square), reads=[q32.b], writes=[sq2.b])
            mm(bB.ap(0, 128, 0, T), ones[:, 3, :], sq[:, 0:T], True, False, [ones.b, sq.b], [bB.b])
            mm(bB.ap(0, 128, 0, T), ones[0:64, 3, :], sq2[0:64, 0:T], False, True, [ones.b, sq2.b], [bB.b])
            sc.op("act", lambda e: e.activation(out=rqt[:, 0:T], in_=bB.ap(0, 128, 0, T), func=AF.Sqrt, bias=V("c_eps192"), scale=1.0),
                  reads=[bB.b, vec.b], writes=[rqt.b])
            sc.op("dve", lambda e: e.reciprocal(out=rq[:, 0:T], in_=rqt[:, 0:T]), reads=[rqt.b], writes=[rq.b])
            sc.op("dve", lambda e: e.scalar_tensor_tensor(out=qn[:, d0:d0 + T], in0=bA.ap(0, 128, 0, T), scalar=gg[:, 0:1], in1=rq[:, 0:T],
                                                          op0=ALU.mult, op1=ALU.mult), reads=[bA.b, gg.b, rq.b], writes=[qn.b])
            sc.op("dve", lambda e: e.scalar_tensor_tensor(out=qr[:, d0:d0 + T], in0=q32[:, 0:T], scalar=gg[:, 1:2], in1=rq[:, 0:T],
                                                          op0=ALU.mult, op1=ALU.mult), reads=[q32.b, gg.b, rq.b], writes=[qr.b])

        def split512(a, b):
            out = []
            while a < b:
                e = min(b, (a // 512 + 1) * 512)
                out.append((a, e))
                a = e
            return out

        def attend(h, qn, qr, q0, NQ, steps, at_dst):
            n = len(steps)
            chunks = {}
            small = NQ <= 64
            slots = [(banks[0],), (banks[1],), (banks[2],), (banks[3],)] if small else Sb
            depth = 3 if small else 2

            def load_chunk(ci):
                i0 = ci * 33
                i1 = min(n, i0 + 33)
                ktA = steps[i0][0]
                nkeys = sum(s[1] for s in steps[i0:i1])
                kc_, kr_, vc_ = kch.next(), krch.next(), vch.next()
                sc.dma("sp", kc_[:, 0:nkeys], KT[h, :, ktA * 128:ktA * 128 + nkeys], reads=[scr_b], writes=[kc_.b])
                sc.dma("sp", kr_[0:64, 0:nkeys], KR[:, ktA * 128:ktA * 128 + nkeys], reads=[scr_b], writes=[kr_.b])
                sc.dma("sp", kr_[64:128, 0:nkeys], KR[:, ktA * 128:ktA * 128 + nkeys], reads=[scr_b], writes=[kr_.b])
                sc.dma("sp", vc_[:, 0:i1 - i0, :], V2[h, :, ktA:ktA + (i1 - i0), :], reads=[scr_b], writes=[vc_.b])
                chunks[ci] = (kc_, kr_, vc_)

            def qk(i):
                kt, nk, a, m = steps[i]
                ci, j = divmod(i, 33)
                if ci not in chunks:
                    load_chunk(ci)
                if j == 0 and (ci + 1) * 33 < n and (ci + 1) not in chunks:
                    load_chunk(ci + 1)
                kc_, kr_, _ = chunks[ci]
                S2 = slots[i % len(slots)]
                grs = split512(a, NQ)
                for (g0, g1) in grs:
                    bk = S2[g0 // 512]
                    o0, o1 = g0 % 512, g0 % 512 + (g1 - g0)
                    mm(bk.ap(0, nk, o0, o1), kc_[:, j * 128:j * 128 + nk], qn[:, q0 + g0:q0 + g1], True, False, [kc_.b, qn.b], [bk.b])
                if m is not None:
                    bk = S2[a // 512]
                    o = a % 512
                    mm(bk.ap(0, nk, o, o + 128), ident[:, 0:nk], mbt[:, m, :], False, False, [ident.b, mbt.b], [bk.b])
                for gi, (g0, g1) in enumerate(grs):
                    bk = S2[g0 // 512]
                    o0, o1 = g0 % 512, g0 % 512 + (g1 - g0)
                    r0 = 64 * (gi % 2)
                    mm(bk.ap(0, nk, o0, o1), kr_[r0:r0 + 64, j * 128:j * 128 + nk], qr[r0:r0 + 64, q0 + g0:q0 + g1], False, True, [kr_.b, qr.b], [bk.b])

            pts = {}

            def expo(i):
                kt, nk, a, m = steps[i]
                S2 = slots[i % len(slots)]
                pt = pTs_small.next() if small else pTs.next()
                pts[i] = pt
                src = S2[0].ap(0, nk, a, NQ) if small else pp[S2[0].i // 2][0:nk, a:NQ]
                sc.op("act", lambda e: e.activation(out=pt[0:nk, a:NQ], in_=src, func=AF.Exp, scale=rstdk[0:nk, kt, h:h + 1]),
                      reads=[b_.b for b_ in S2] + [rstdk.b], writes=[pt.b])
                if i == 0:
                    sc.op("dve", lambda e: e.tensor_copy(out=racc[:, 0:NQ], in_=pt[:, 0:NQ]), reads=[pt.b], writes=[racc.b])
                else:
                    sc.op("dve", lambda e: e.tensor_tensor(out=racc[0:nk, a:NQ], in0=racc[0:nk, a:NQ], in1=pt[0:nk, a:NQ], op=ALU.add),
                          reads=[pt.b, racc.b], writes=[racc.b])

            def pv(i):
                kt, nk, a, m = steps[i]
                ci, j = divmod(i, 33)
                _, _, vc_ = chunks[ci]
                pt = pts.pop(i)
                for (g0, g1) in split512(a, NQ):
                    bk = Ob[g0 // 512]
                    o0, o1 = g0 % 512, g0 % 512 + (g1 - g0)
                    mm(bk.ap(0, 128, o0, o1), vc_[0:nk, j, :], pt[0:nk, g0:g1], i == 0, small and i == n - 1, [vc_.b, pt.b], [bk.b])

            for i in range(min(depth, n)):
                qk(i)
            for i in range(n):
                expo(i)
                if i + depth < n:
                    qk(i + depth)
                pv(i)
            if not small:
                for (g0, g1) in split512(0, NQ):
                    bk = Ob[g0 // 512]
                    mm(bk.ap(0, 128, 0, g1 - g0), zero_t[:, 0:128], zero_t[:, 0:g1 - g0], False, True, [zero_t.b], [bk.b])
            for (g0, g1) in split512(0, NQ):
                bk = Sb[0][g0 // 512]
                w = g1 - g0
                mm(bk.ap(0, 128, 0, w), onesf[:, :], racc[:, g0:g1], True, True, [onesf.b, racc.b], [bk.b])
                sc.op("dve", lambda e, bk=bk, g0=g0, g1=g1, w=w: e.reciprocal(out=rr[:, g0:g1], in_=bk.ap(0, 128, 0, w)), reads=[bk.b], writes=[rr.b])
            at = ato.next()
            for (g0, g1) in split512(0, NQ):
                bk = Ob[g0 // 512]
                w = g1 - g0
                sc.op("dve", lambda e, bk=bk, g0=g0, g1=g1, w=w: e.tensor_tensor(out=at[:, g0:g1], in0=bk.ap(0, 128, 0, w), in1=rr[:, g0:g1], op=ALU.mult),
                      reads=[bk.b, rr.b], writes=[at.b])
            sc.dma("sp", at_dst, at[:, 0:NQ], reads=[at.b], writes=[at_scr_b])

        for h in range(NH):
            convert_upto(4 * (h + 1))
            wq = wqs.next()
            for kc in range(4):
                sc.dma("pool", wq[:, kc, 0:192], w_q_up[kc * 128:(kc + 1) * 128, h * 192:(h + 1) * 192], writes=[wq.b])
            sc.op("pool", lambda e: e.tensor_copy(out=wq[:, :, 192:256], in_=wq[:, :, 128:192]), reads=[wq.b], writes=[wq.b])
            sc.op("pool", lambda e: e.tensor_scalar(out=wq[:, :, 256:288], in0=wq[:, :, 160:192], scalar1=-1.0, scalar2=None, op0=ALU.mult),
                  reads=[wq.b], writes=[wq.b])
            sc.op("pool", lambda e: e.tensor_copy(out=wq[:, :, 288:320], in_=wq[:, :, 128:160]), reads=[wq.b], writes=[wq.b])
            sc.op("pool", lambda e: e.tensor_copy(out=wq[:, :, 320:384], in_=wq[:, :, 256:320]), reads=[wq.b], writes=[wq.b])
            qn, qr = qns.next(), qrs.next()
            for t in range(0 if SAMPLE_ONLY else 4):
                c = slice(t * 512, (t + 1) * 512)
                compute_q(wq, lambda kc, c0, T: cqn[:, kc, c0:c0 + T], cqn.b, t * 512, 512, cos_own[:, c], sin_own[:, c], cos_own.b, qn, qr, t * 512)
            compute_q(wq, lambda kc, c0, T: cqn_s[:, kc, c0:c0 + T], cqn_s.b, 0, 64, cos_s[:, 0:64], sin_s[:, 0:64], cos_s.b, qn, qr, TOWN)
            for qh in range(0 if SAMPLE_ONLY else 2):
                steps = []
                for kt in range(64 * (qh + 1)):
                    j0 = max(kt // 8, 8 * qh)
                    m = (kt % 8) if (kt // 8 >= 8 * qh) else None
                    steps.append((kt, 128, (j0 - 8 * qh) * 128, m))
                attend(h, qn, qr, 1024 * qh, 1024, steps, AT[h, :, 1024 * qh:1024 * (qh + 1)])
            for b in range(2):
                steps = [(128 + 33 * b + i, 128 if i < 32 else 32, 0, None) for i in range(33)]
                attend(h, qn, qr, TOWN + 32 * b, 32, steps, AT[h, :, TOWN + 32 * b:TOWN + 32 * b + 32])
        free(p2)
    free(p12)

    if "p3" in PHASES:
        convert_upto(62)
        sc.barrier()
        p3 = []
        wb = Ring([sb([128, 8192], BF16, p3) for _ in range(3)])

        def wload(bi):
            t = wb.next()
            n = max(off + KC * ncols for (off, w, KC, c0, ncols) in wblocks[bi])
            sc.dma("pool" if (bi % 2) else "sp", t[:, 0:n], WB[bi, :, 0:n], reads=[wb_scr_b], writes=[t.b])
            views = [t[:, off:off + KC * ncols].rearrange("p (k c) -> p k c", c=ncols) for (off, w, KC, c0, ncols) in wblocks[bi]]
            return views, t.b

        xt = sb([128, 16, 512], F32, p3)
        hT = sb([128, 16, 512], BF16, p3)
        hm = sb([128, 128], F32, p3)
        arena = sb([128, 11264], F32, p3)
        upad = Tl(arena.t[:, 0:5120].rearrange("p (a b c) -> p a b c", a=8, b=4))
        ycv = Tl(arena.t[:, 5120:9216].rearrange("p (a c) -> p a c", a=8))
        ybf = sb([128, 8, 512], BF16, p3)
        zc = ybf
        att_raw = sb([128, 8192], BF16, p3)
        att = Tl(att_raw.t[:, :].rearrange("p (h t) -> p h t", t=512))
        xh = Tl(att_raw.t[:, 0:4096].bitcast(F32).rearrange("p (k t) -> p k t", t=128))
        hhT = Tl(att_raw.t[:, 4096:6144].rearrange("p (k t) -> p k t", t=128))
        att.b = xh.b = hhT.b = att_raw.b
        z2 = sb([128, 16, 512], BF16, p3)
        act_t = Tl(arena.t[:, :].bitcast(BF16).rearrange("p (a c) -> p a c", a=44))
        st3 = {"rs": sb([128, 512], F32, p3), "rtmp": sb([128, 512], F32, p3)}
        ident3 = sb([128, 128], BF16, p3)
        sc.dma("pool", ident3[:], ident_in, writes=[ident3.b])
        diags = Ring([sb([128, 128], BF16, p3) for _ in range(4)])
        upbs = Ring([sb([128, 4, 160], BF16, p3) for _ in range(2)])
        sg = Ring([sb([128, 512], F32, p3) for _ in range(2)])
        tA = Ring([sb([128, 512], F32, p3) for _ in range(1)])
        tB = Ring([sb([128, 512], F32, p3) for _ in range(1)])
        ln_m = st3["rs"]
        ln_r = st3["rtmp"]
        yst = Ring([sb([128, 512], F32, p3) for _ in range(1)])

        def post(x_ap, T, J, L, halo_ap, hm_ap, state, at_c0, y_out, u_out):
            sc.dma("sp", xt[:, :, 0:T], x_ap.rearrange("(k p) t -> p k t", p=128), writes=[xt.b])
            rms_h(xt, hT, T, st3)
            HT = J * 32
            if halo_ap is not None:
                sc.dma("sp", xh[:, :, 0:HT], halo_ap.rearrange("(k p) t -> p k t", p=128), writes=[xh.b])
                sc.dma("sp", hm[:, 0:HT], hm_ap.partition_broadcast(128), writes=[hm.b])
                rms_h(xh, hhT, HT, st3)
            else:
                for b in range(J):
                    sc.dma("sp", upad[:, :, b, 2:32], state[b].rearrange("(k p) r -> p k r", p=128), writes=[upad.b])
            for g in range(2):
                (wa,), wab = wload(2 * g)
                (wg,), wgb = wload(2 * g + 1)
                for o in range(4):
                    oc = g * 4 + o
                    bA, bG = rot7.next(), rot7.next()
                    for kc in range(16):
                        mm(bA.ap(0, 128, 0, T), wa[:, kc, o * 128:(o + 1) * 128], hT[:, kc, 0:T], kc == 0, kc == 15, [wab, hT.b], [bA.b])
                    for kc in range(16):
                        mm(bG.ap(0, 128, 0, T), wg[:, kc, o * 128:(o + 1) * 128], hT[:, kc, 0:T], kc == 0, kc == 15, [wgb, hT.b], [bG.b])
                    s_ = sg.next()
                    sc.op("act", lambda e, bG=bG, s_=s_, oc=oc: e.activation(out=s_[:, 0:T], in_=bG.ap(0, 128, 0, T), func=AF.Sigmoid,
                                                                            bias=V("b_glu", 8 + oc), scale=1.0), reads=[bG.b, vec.b], writes=[s_.b])
                    sc.op("dve", lambda e, bA=bA, s_=s_, oc=oc: e.scalar_tensor_tensor(
                        out=upad[:, oc, 0:J, 32:32 + L], in0=bA.ap(0, 128, 0, T).rearrange("p (j l) -> p j l", l=L), scalar=V("b_glu", oc),
                        in1=s_[:, 0:T].rearrange("p (j l) -> p j l", l=L), op0=ALU.add, op1=ALU.mult), reads=[bA.b, s_.b, vec.b], writes=[upad.b])
                    if halo_ap is not None:
                        bA, bG = rot7.next(), rot7.next()
                        for kc in range(16):
                            mm(bA.ap(0, 128, 0, HT), wa[:, kc, o * 128:(o + 1) * 128], hhT[:, kc, 0:HT], kc == 0, kc == 15, [wab, hhT.b], [bA.b])
                        for kc in range(16):
                            mm(bG.ap(0, 128, 0, HT), wg[:, kc, o * 128:(o + 1) * 128], hhT[:, kc, 0:HT], kc == 0, kc == 15, [wgb, hhT.b], [bG.b])
                        s_ = sg.next()
                        sc.op("act", lambda e, bG=bG, s_=s_, oc=oc: e.activation(out=s_[:, 0:HT], in_=bG.ap(0, 128, 0, HT), func=AF.Sigmoid,
                                                                                bias=V("b_glu", 8 + oc), scale=1.0), reads=[bG.b, vec.b], writes=[s_.b])
                        t_ = tA.next()
                        sc.op("dve", lambda e, bA=bA, s_=s_, oc=oc, t_=t_: e.scalar_tensor_tensor(
                            out=t_[:, 0:HT], in0=bA.ap(0, 128, 0, HT), scalar=V("b_glu", oc), in1=s_[:, 0:HT], op0=ALU.add, op1=ALU.mult),
                            reads=[bA.b, s_.b, vec.b], writes=[t_.b])
                        sc.op("dve", lambda e, oc=oc, t_=t_: e.tensor_tensor(
                            out=upad[:, oc, 0:J, 0:32], in0=t_[:, 0:HT].rearrange("p (j l) -> p j l", l=32),
                            in1=hm[:, 0:HT].rearrange("p (j l) -> p j l", l=32), op=ALU.mult), reads=[t_.b, hm.b], writes=[upad.b])
            if u_out is not None:
                u_out()
            sc.dma("sp", att[:, :, 0:T], AT[:, :, at_c0:at_c0 + T].rearrange("h d t -> d h t"), reads=[at_scr_b], writes=[att.b])
            for oc in range(8):
                upb = upbs.next()
                sc.op("pool", lambda e, oc=oc, upb=upb: e.tensor_copy(out=upb[:, 0:J, 2:32 + L], in_=upad[:, oc, 0:J, 2:32 + L]), reads=[upad.b], writes=[upb.b])
                bk = rot7.next()
                for k in range(CW):
                    dg = diags.next()
                    wcol = vec[:, VO["w_dw"] + oc * 31 + k:VO["w_dw"] + oc * 31 + k + 1]
                    sc.op("dve", lambda e, dg=dg, wcol=wcol: e.tensor_scalar(out=dg[:, :], in0=ident3[:, :], scalar1=wcol, scalar2=None, op0=ALU.mult),
                          reads=[ident3.b, vec.b], writes=[dg.b])
                    mm(bk.ap(0, 128, 0, T).rearrange("p (j l) -> p j l", l=L), dg[:, :], upb[:, 0:J, 2 + k:2 + k + L], k == 0, k == CW - 1,
                       [dg.b, upb.b], [bk.b])
                sc.op("act", lambda e, bk=bk, oc=oc: e.activation(out=ycv[:, oc, 0:T], in_=bk.ap(0, 128, 0, T), func=AF.Identity, bias=V("b_dw", oc), scale=1.0),
                      reads=[bk.b, vec.b], writes=[ycvb[oc]])
            b1, b2 = rot7.next(), rot7.next()
            for oc in range(8):
                sc.op("pool", lambda e, oc=oc: e.tensor_copy(out=ybf[:, oc, 0:T], in_=ycv[:, oc, 0:T]), reads=[ycvb[oc]], writes=[ybf.b])
                sq = sqr.next()
                sc.op("act", lambda e, oc=oc, sq=sq: e.activation(out=sq[:, 0:T], in_=ycv[:, oc, 0:T], func=AF.Square), reads=[ycvb[oc]], writes=[sq.b])
                mm(b1.ap(0, 128, 0, T), ones[:, 2, :], ybf[:, oc, 0:T], oc == 0, oc == 7, [ones.b, ybf.b], [b1.b])
                mm(b2.ap(0, 128, 0, T), ones[:, 2, :], sq[:, 0:T], oc == 0, oc == 7, [ones.b, sq.b], [b2.b])
            sc.op("dve", lambda e: e.tensor_copy(out=ln_m[:, 0:T], in_=b1.ap(0, 128, 0, T)), reads=[b1.b], writes=[ln_m.b])
            t_ = tA.next()
            sc.op("dve", lambda e: e.tensor_tensor(out=t_[:, 0:T], in0=ln_m[:, 0:T], in1=ln_m[:, 0:T], op=ALU.mult), reads=[ln_m.b], writes=[t_.b])
            t2_ = tB.next()
            sc.op("dve", lambda e: e.tensor_tensor(out=t2_[:, 0:T], in0=b2.ap(0, 128, 0, T), in1=t_[:, 0:T], op=ALU.subtract), reads=[b2.b, t_.b], writes=[t2_.b])
            sc.op("dve", lambda e: e.tensor_scalar(out=t2_[:, 0:T], in0=t2_[:, 0:T], scalar1=0.0, scalar2=None, op0=ALU.max), reads=[t2_.b], writes=[t2_.b])
            sc.op("act", lambda e: e.activation(out=t_[:, 0:T], in_=t2_[:, 0:T], func=AF.Sqrt, bias=V("c_eps"), scale=1.0), reads=[t2_.b, vec.b], writes=[t_.b])
            sc.op("dve", lambda e: e.reciprocal(out=ln_r[:, 0:T], in_=t_[:, 0:T]), reads=[t_.b], writes=[ln_r.b])
            for oc in range(8):
                t_ = tA.next()
                sc.op("dve", lambda e, oc=oc, t_=t_: e.tensor_tensor(out=t_[:, 0:T], in0=ycv[:, oc, 0:T], in1=ln_m[:, 0:T], op=ALU.subtract),
                      reads=[ycvb[oc], ln_m.b], writes=[t_.b])
                t2_ = tB.next()
                sc.op("dve", lambda e, t_=t_, t2_=t2_: e.tensor_tensor(out=t2_[:, 0:T], in0=t_[:, 0:T], in1=ln_r[:, 0:T], op=ALU.mult),
                      reads=[t_.b, ln_r.b], writes=[t2_.b])
                sc.op("act", lambda e, oc=oc, t2_=t2_: e.activation(out=zc[:, oc, 0:T], in_=t2_[:, 0:T], func=AF.Silu, bias=V("b_ln", oc), scale=V("g_ln", oc)),
                      reads=[t2_.b, vec.b], writes=[zc.b])
            for oc in range(16):
                (wao, wco, wga, wgb_), wtb = wload(4 + oc)
                bYa, bYb, bGa, bGb = rot7.next(), rot7.next(), rot7.next(), rot7.next()
                for kc in range(16):
                    mm(bYa.ap(0, 128, 0, T), wao[:, kc, :], att[:, kc, 0:T], kc == 0, kc == 15, [wtb, att.b], [bYa.b])
                for kc in range(8):
                    mm(bYb.ap(0, 128, 0, T), wco[:, kc, :], zc[:, kc, 0:T], kc == 0, kc == 7, [wtb, zc.b], [bYb.b])
                for kc in range(16):
                    mm(bGa.ap(0, 128, 0, T), wga[:, kc, :], hT[:, kc, 0:T], kc == 0, kc == 15, [wtb, hT.b], [bGa.b])
                for kc in range(16):
                    mm(bGb.ap(0, 128, 0, T), wgb_[:, kc, :], hT[:, kc, 0:T], kc == 0, kc == 15, [wtb, hT.b], [bGb.b])
                sa, sb_ = sg.next(), sg.next()
                sc.op("act", lambda e, bGa=bGa, sa=sa, oc=oc: e.activation(out=sa[:, 0:T], in_=bGa.ap(0, 128, 0, T), func=AF.Sigmoid,
                                                                          bias=V("b_gate", oc), scale=1.0), reads=[bGa.b, vec.b], writes=[sa.b])
                sc.op("act", lambda e, bGb=bGb, sb_=sb_, oc=oc: e.activation(out=sb_[:, 0:T], in_=bGb.ap(0, 128, 0, T), func=AF.Sigmoid,
                                                                            bias=V("b_gate", 16 + oc), scale=1.0), reads=[bGb.b, vec.b], writes=[sb_.b])
                t_, t2_ = tA.next(), tB.next()
                sc.op("dve", lambda e, bYa=bYa, sa=sa, t_=t_: e.tensor_tensor(out=t_[:, 0:T], in0=bYa.ap(0, 128, 0, T), in1=sa[:, 0:T], op=ALU.mult),
                      reads=[bYa.b, sa.b], writes=[t_.b])
                sc.op("dve", lambda e, bYb=bYb, sb_=sb_, t2_=t2_, oc=oc: e.scalar_tensor_tensor(
                    out=t2_[:, 0:T], in0=bYb.ap(0, 128, 0, T), scalar=V("b_co", oc), in1=sb_[:, 0:T], op0=ALU.add, op1=ALU.mult),
                    reads=[bYb.b, sb_.b, vec.b], writes=[t2_.b])
                sc.op("dve", lambda e, t_=t_, t2_=t2_, oc=oc: e.tensor_tensor(out=z2[:, oc, 0:T], in0=t_[:, 0:T], in1=t2_[:, 0:T], op=ALU.add),
                      reads=[t_.b, t2_.b], writes=[z2.b])
            for g in range(4):
                (wo,), wob = wload(20 + g)
                for o in range(4):
                    oc = g * 4 + o
                    bk = rot7.next()
                    for kc in range(16):
                        mm(bk.ap(0, 128, 0, T), wo[:, kc, o * 128:(o + 1) * 128], z2[:, kc, 0:T], kc == 0, kc == 15, [wob, z2.b], [bk.b])
                    sc.op("dve", lambda e, bk=bk, oc=oc: e.tensor_tensor(out=xt[:, oc, 0:T], in0=xt[:, oc, 0:T], in1=bk.ap(0, 128, 0, T), op=ALU.add),
                          reads=[bk.b, xt.b], writes=[xt.b])
            sc.barrier()
            bk = rot7.next()
            for kc in range(16):
                sq = sqr.next()
                sc.op("act", lambda e, kc=kc, sq=sq: e.activation(out=sq[:, 0:T], in_=xt[:, kc, 0:T], func=AF.Square), reads=[xt.b], writes=[sq.b])
                mm(bk.ap(0, 128, 0, T), ones[:, 0, :], sq[:, 0:T], kc == 0, kc == 15, [ones.b, sq.b], [bk.b])
            ps_ap_b[0] = [bk.b]
            rsqrt_bc(bk.ap(0, 128, 0, T), "c_eps", st3["rs"], 128, T, st3["rtmp"])
            rs = st3["rs"]
            for kc in range(16):
                sc.op("dve", lambda e, kc=kc: e.scalar_tensor_tensor(out=hT[:, kc, 0:T], in0=xt[:, kc, 0:T], scalar=V("g_ffn", kc), in1=rs[:, 0:T],
                                                                     op0=ALU.mult, op1=ALU.mult), reads=[xt.b, rs.b, vec.b], writes=[hT.b])
            for g in range(11):
                (wg_,), wgb2 = wload(24 + 2 * g)
                (wu_,), wub2 = wload(25 + 2 * g)
                for o in range(4):
                    fc = g * 4 + o
                    bG, bU = rot7.next(), rot7.next()
                    for kc in range(16):
                        mm(bG.ap(0, 128, 0, T), wg_[:, kc, o * 128:(o + 1) * 128], hT[:, kc, 0:T], kc == 0, kc == 15, [wgb2, hT.b], [bG.b])
                    for kc in range(16):
                        mm(bU.ap(0, 128, 0, T), wu_[:, kc, o * 128:(o + 1) * 128], hT[:, kc, 0:T], kc == 0, kc == 15, [wub2, hT.b], [bU.b])
                    s_ = sg.next()
                    sc.op("act", lambda e, bG=bG, s_=s_: e.activation(out=s_[:, 0:T], in_=bG.ap(0, 128, 0, T), func=AF.Silu), reads=[bG.b], writes=[s_.b])
                    sc.op("dve", lambda e, bU=bU, s_=s_, fc=fc: e.tensor_tensor(out=act_t[:, fc, 0:T], in0=bU.ap(0, 128, 0, T), in1=s_[:, 0:T], op=ALU.mult),
                          reads=[bU.b, s_.b], writes=[act_t.b])
            for oc in range(16):
                (wd,), wdb = wload(46 + oc)
                bk = rot7.next()
                for fc in range(44):
                    mm(bk.ap(0, 128, 0, T), wd[:, fc, :], act_t[:, fc, 0:T], fc == 0, fc == 43, [wdb, act_t.b], [bk.b])
                ys = yst.next()
                sc.op("dve", lambda e, bk=bk, oc=oc, ys=ys: e.tensor_tensor(out=ys[:, 0:T], in0=xt[:, oc, 0:T], in1=bk.ap(0, 128, 0, T), op=ALU.add),
                      reads=[bk.b, xt.b], writes=[ys.b])
                sc.dma("sp", y_out[oc * 128:(oc + 1) * 128, :], ys[:, 0:T], reads=[ys.b])
            sc.barrier()

        ycvb = [Buf() for _ in range(8)]
        DBG.update(z2=z2, zc=zc, att=att, xt=xt, hT=hT, arena=arena)
        for t in range(0 if SAMPLE_ONLY else 4):
            c = slice(t * 512, (t + 1) * 512)
            hc = slice(t * 128, (t + 1) * 128)
            uo = None
            if t == 3:
                def uo():
                    sc.dma("sp", uT_last.rearrange("(k p) t -> p k t", p=128), upad[:, :, 3, 128:160], reads=[upad.b])
            post(xT_own[:, c], 512, 4, 128, xT_halo[:, hc], hmask[:, hc], None, t * 512, yT_own[:, c], uo)

        def uo_s():
            for b in range(2):
                sc.dma("sp", uT_s[:, 32 * b:32 * b + 32].rearrange("(k p) t -> p k t", p=128), upad[:, :, b, 32:64], reads=[upad.b])
        post(xT_s[:, 0:64], 64, 2, 32, None, None, [sconvT[0], sconvT[1]], TOWN, yT_s[:, 0:64], uo_s)
        free(p3)

    sc.barrier()
    return nc


_CACHE = {}


def _prep_inputs(inp):
    f = np.float32
    xp = np.asarray(inp["x_prompt"], f)[0]
    xs = np.asarray(inp["x_sample"], f)
    xT_all = np.ascontiguousarray(xp.T)
    xt = xp.reshape(128, 128, D)
    vecs = np.zeros((128, NV), f)

    def put(name, arr):
        arr = np.asarray(arr, f)
        vecs[:arr.shape[0], VO[name]:VO[name] + arr.shape[1]] = arr

    col = lambda v: np.ascontiguousarray(np.asarray(v, f).reshape(-1, 128).T)
    put("g_mix", col(inp["g_mix_norm"][0]))
    put("g_q_a", col(inp["g_q_a"][0]))
    put("g_kv_a", col(inp["g_kv_a"][0]))
    put("b_glu", col(inp["b_glu"][0]))
    put("b_gate", col(inp["b_gate"][0]))
    wdw = np.asarray(inp["w_dw"], f)[0]
    put("w_dw", np.ascontiguousarray(wdw.T.reshape(8, 128, 31).transpose(1, 0, 2).reshape(128, 248)))
    put("b_dw", col(inp["b_dw"][0]))
    put("g_ln", col(inp["g_conv_ln"][0]))
    put("b_ln", col(inp["b_conv_ln"][0]))
    put("b_co", col(inp["b_conv_out"][0]))
    put("g_ffn", col(inp["g_ffn_norm"][0]))
    for nm, key in (("gq", "g_q_norm"), ("gk", "g_k_norm")):
        g = np.asarray(inp[key], f)[0]
        a = np.ones((128, 2), f)
        a[:, 0] = g[0:128]
        a[0:64, 1] = g[128:192]
        a[64:128, 1] = g[128:192]
        put(nm, a)
    invf = (1.0 / (np.float32(10000.0) ** (np.arange(0, 64, 2, dtype=np.float32) / np.float32(64)))).astype(f)
    iv = np.zeros((128, 1), f)
    iv[0:64, 0] = np.concatenate([invf, invf])
    iv[64:128, 0] = np.concatenate([invf, invf])
    put("invf", iv)
    put("c_eps", np.full((128, 1), EPS, f))
    put("c_eps192", np.full((128, 1), 192 * EPS, f))

    ident = np.eye(128, dtype=f)
    common = {
        "xT_all": xT_all, "vecs": vecs, "ident": ident,
        "pos_all": np.arange(S, dtype=f)[None, :],
        "pos_s": np.tile(np.arange(PAST, PAST + TS, dtype=f), 2)[None, :],
        "w_in": np.ascontiguousarray(inp["w_in"][0], f), "w_q_up": np.ascontiguousarray(inp["w_q_up"][0], f),
        "w_kv_up": np.ascontiguousarray(inp["w_kv_up"][0], f), "w_attn_out": np.ascontiguousarray(inp["w_attn_out"][0], f),
        "w_conv_out": np.ascontiguousarray(inp["w_conv_out"][0], f), "w_out": np.ascontiguousarray(inp["w_out"][0], f),
        "w_ffn_gate": np.ascontiguousarray(inp["w_ffn_gate"][0], f), "w_ffn_up": np.ascontiguousarray(inp["w_ffn_up"][0], f),
        "w_ffn_down": np.ascontiguousarray(inp["w_ffn_down"][0], f),
    }
    maps = []
    for c in range(NCORE):
        gt = np.arange(16) * 8 + c
        own = xt[gt].reshape(TOWN, D)
        halo = np.zeros((16, 32, D), f)
        hmask = np.ones((16, 32), f)
        for j, g in enumerate(gt):
            if g == 0:
                hmask[j] = 0.0
            else:
                halo[j] = xt[g - 1, 96:128]
        pos_own = (gt[:, None] * 128 + np.arange(128)[None, :]).reshape(1, TOWN).astype(f)
        mb = np.zeros((128, 8, 128), f)
        for m in range(8):
            if m > c:
                mb[:, m, :] = -30000.0
            elif m == c:
                mb[64:128, m, 0:64] = -30000.0
        d = dict(common)
        d.update({
            "xT_own": np.ascontiguousarray(own.T), "xT_halo": np.ascontiguousarray(halo.reshape(512, D).T),
            "xT_s": np.ascontiguousarray(xs[2 * c:2 * c + 2].reshape(64, D).T),
            "pos_own": pos_own, "hmask": hmask.reshape(1, 512), "mb": mb.reshape(128, 1024),
            "cckvT": np.ascontiguousarray(np.asarray(inp["cache_ckv"], f)[0, 2 * c:2 * c + 2].transpose(0, 2, 1)),
            "ckpeT": np.ascontiguousarray(np.asarray(inp["cache_kpe"], f)[0, 2 * c:2 * c + 2].transpose(0, 2, 1)),
            "sconvT": np.ascontiguousarray(np.asarray(inp["state_conv"], f)[0, 2 * c:2 * c + 2].transpose(0, 2, 1)),
        })
        maps.append(d)
    return maps


def kernel(**inp):
    if "nc" not in _CACHE:
        _CACHE["nc"] = build_program()
    nc = _CACHE["nc"]
    maps = _prep_inputs(inp)
    res = run_bass_kernel_spmd(nc, maps, core_ids=list(range(NCORE)))
    R = res.results
    f = np.float32
    y_p = np.zeros((1, S, D), f)
    ckv_p = np.zeros((1, 1, S, 512), f)
    kpe_p = np.zeros((1, 1, S, 64), f)
    y_s = np.zeros((16, TS, D), f)
    ckv_s = np.zeros((1, 16, TS, 512), f)
    kpe_s = np.zeros((1, 16, TS, 64), f)
    conv_s = np.zeros((1, 16, 30, CONV), f)
    for c in range(NCORE):
        r = R[c]
        gt = np.arange(16) * 8 + c
        idx = (gt[:, None] * 128 + np.arange(128)[None, :]).reshape(-1)
        y_p[0, idx] = np.asarray(r["yT_own"]).T
        ckv_p[0, 0, idx] = np.asarray(r["ckvT_own"]).T
        kpe_p[0, 0, idx] = np.asarray(r["kpeT_own"]).T
        y_s[2 * c:2 * c + 2] = np.asarray(r["yT_s"]).T.reshape(2, TS, D)
        ckv_s[0, 2 * c:2 * c + 2] = np.asarray(r["ckvT_s"]).T.reshape(2, TS, 512)
        kpe_s[0, 2 * c:2 * c + 2] = np.asarray(r["kpeT_s"]).T.reshape(2, TS, 64)
        us = np.asarray(r["uT_s"]).T.reshape(2, TS, CONV)
        conv_s[0, 2 * c:2 * c + 2] = us[:, 2:32]
    conv_p = np.asarray(R[7]["uT_last"]).T[2:32][None, None]
    return (y_p, y_s, ckv_p, kpe_p, conv_p.astype(f), ckv_s, kpe_s, conv_s)
```

```python
import numpy as np
import concourse.bass as bass
import concourse.mybir as mybir
from concourse.bass_utils import run_bass_kernel_spmd

F32 = mybir.dt.float32
BF16 = mybir.dt.bfloat16
AF = mybir.ActivationFunctionType
ALU = mybir.AluOpType

NCORE = 8
D = 2048
S = 16384
TOWN = 2048
NH = 16
IND = 7232
DFF = 5632
CONV = 1024
CW = 31
PAST = 4096
TS = 32
EPS = 1e-6
NKT = 128 + 2 * 33
NK = NKT * 128
MAGIC = 12582912.0
C1 = 6.28125
C2 = 0.0019353071795864769
PI = 3.1415925

VO = {}
_o = 0
for _n, _w in (("g_mix", 16), ("g_q_a", 4), ("g_kv_a", 4), ("b_glu", 16), ("b_gate", 32), ("w_dw", 248),
               ("b_dw", 8), ("g_ln", 8), ("b_ln", 8), ("b_co", 16), ("g_ffn", 16), ("gq", 2), ("gk", 2),
               ("invf", 1), ("c_eps", 1), ("c_eps192", 1)):
    VO[_n] = _o
    _o += _w
NV = _o

import os
PHASES = set(os.environ.get("KPHASES", "p1a,p1b,p2,p3").split(","))
SAMPLE_ONLY = False
DBG = {}


class Buf:
    __slots__ = ("w", "r", "x")

    def __init__(self, x=False):
        self.w = {}
        self.r = {}
        self.x = x


class Sched:
    def __init__(self, nc):
        self.nc = nc
        self.eng = {"pe": nc.tensor, "act": nc.scalar, "dve": nc.vector, "pool": nc.gpsimd, "sp": nc.sync}
        self.sems = {}
        self.cnt = {}
        self.seen = {e: {} for e in self.eng}
        for e in ("pe", "act", "dve", "pool"):
            self.sems[e] = nc.semaphore("s_" + e).__enter__()
            self.cnt[e] = 0
        self.ND = 8
        self.dq = {}
        for q in ("sp", "pool"):
            names = [f"d_{q}{i}" for i in range(self.ND)]
            for n in names:
                self.sems[n] = nc.semaphore(n).__enter__()
                self.cnt[n] = 0
            self.dq[q] = [names, 0]

    def _wait(self, e, key, val):
        if self.seen[e].get(key, 0) >= val:
            return
        self.eng[e].wait_ge(self.sems[key], val)
        self.seen[e][key] = val

    def _deps(self, e, reads, writes):
        need = {}

        def add(k, v, war=False):
            if k == e and e == "pe":
                return
            if need.get(k, 0) < v:
                need[k] = v

        for b in reads:
            for k, v in b.w.items():
                add(k, v)
        for b in writes:
            for k, v in b.w.items():
                add(k, v)
            for k, v in b.r.items():
                add(k, v, True)
        for k, v in need.items():
            self._wait(e, k, v)

    def _mark(self, tok, reads, writes):
        k, v = tok
        for b in reads:
            if b.r.get(k, 0) < v:
                b.r[k] = v
        for b in writes:
            if b.w.get(k, 0) < v:
                b.w[k] = v

    def op(self, e, fn, reads=(), writes=()):
        writes = list(writes) + [b for b in reads if b.x]
        reads = [b for b in reads if not b.x]
        self._deps(e, reads, writes)
        ins = fn(self.eng[e])
        self.cnt[e] += 1
        ins.then_inc(self.sems[e], 1)
        tok = (e, self.cnt[e])
        self._mark(tok, reads, writes)
        return tok

    def dma(self, q, out, in_, reads=(), writes=()):
        self._deps(q, reads, writes)
        names, i = self.dq[q]
        n = names[i]
        self.dq[q][1] = (i + 1) % self.ND
        if self.cnt[n] > 0:
            self._wait(q, n, self.cnt[n])
        ins = self.eng[q].dma_start(out=out, in_=in_)
        self.cnt[n] += 16
        ins.then_inc(self.sems[n], 16)
        tok = (n, self.cnt[n])
        self._mark(tok, reads, writes)
        return tok

    def barrier(self):
        for e in self.eng:
            for k, v in self.cnt.items():
                if k != e and v > 0:
                    self._wait(e, k, v)


class Tl:
    def __init__(self, t):
        self.t = t
        self.b = Buf()

    def __getitem__(self, k):
        return self.t[k]


class Ring:
    def __init__(self, tiles):
        self.tiles = tiles
        self.i = 0

    def next(self):
        t = self.tiles[self.i]
        self.i = (self.i + 1) % len(self.tiles)
        return t


def build_program():
    nc = bass.Bass("TRN2", target_bir_lowering=False)
    sc = Sched(nc)
    ctx = []

    def dram_in(name, shape):
        return nc.dram_tensor(name, list(shape), F32, kind="ExternalInput").ap()

    def dram_out(name, shape):
        return nc.dram_tensor(name, list(shape), F32, kind="ExternalOutput").ap()

    xT_all = dram_in("xT_all", [D, S])
    xT_own = dram_in("xT_own", [D, TOWN])
    xT_halo = dram_in("xT_halo", [D, 512])
    xT_s = dram_in("xT_s", [D, 64])
    pos_all = dram_in("pos_all", [1, S])
    pos_own = dram_in("pos_own", [1, TOWN])
    pos_s = dram_in("pos_s", [1, 64])
    hmask = dram_in("hmask", [1, 512])
    mb_in = dram_in("mb", [128, 1024])
    ident_in = dram_in("ident", [128, 128])
    cckvT = dram_in("cckvT", [2, 512, PAST])
    ckpeT = dram_in("ckpeT", [2, 64, PAST])
    sconvT = dram_in("sconvT", [2, CONV, 30])
    vecs_in = dram_in("vecs", [128, NV])
    w_in = dram_in("w_in", [D, IND])
    w_q_up = dram_in("w_q_up", [512, 3072])
    w_kv_up = dram_in("w_kv_up", [512, 4096])
    w_ao = dram_in("w_attn_out", [D, D])
    w_co = dram_in("w_conv_out", [CONV, D])
    w_o = dram_in("w_out", [D, D])
    w_fg = dram_in("w_ffn_gate", [D, DFF])
    w_fu = dram_in("w_ffn_up", [D, DFF])
    w_fd = dram_in("w_ffn_down", [DFF, D])

    yT_own = dram_out("yT_own", [D, TOWN])
    ckvT_own = dram_out("ckvT_own", [512, TOWN])
    kpeT_own = dram_out("kpeT_own", [64, TOWN])
    uT_last = dram_out("uT_last", [CONV, 32])
    yT_s = dram_out("yT_s", [D, 64])
    ckvT_s = dram_out("ckvT_s", [512, 64])
    kpeT_s = dram_out("kpeT_s", [64, 64])
    uT_s = dram_out("uT_s", [CONV, 64])

    KT = nc.dram_tensor("KT_scr", [NH, 128, NK], BF16).ap()
    KR = nc.dram_tensor("KR_scr", [64, NK], BF16).ap()
    V2 = nc.dram_tensor("V2_scr", [NH, 128, NKT, 128], BF16).ap()
    AT = nc.dram_tensor("AT_scr", [NH, 128, TOWN + 64], BF16).ap()
    WB = nc.dram_tensor("WB_scr", [62, 128, 8192], BF16).ap()

    cnt = [0]

    def sb(shape, dt, stack=None):
        cnt[0] += 1
        g = nc.sbuf_tensor(f"t{cnt[0]}", list(shape), dt)
        t = g.__enter__()
        (stack if stack is not None else ctx).append(g)
        return Tl(t)

    def free(stack):
        sc.barrier()
        while stack:
            stack.pop().__exit__(None, None, None)

    pp = []
    for i in range(4):
        g = nc.psum_tensor(f"pp{i}", [128, 1024], F32)
        pp.append(g.__enter__())
        ctx.append(g)
    bankb = [Buf(True) for _ in range(8)]

    class Bank:
        def __init__(self, i):
            self.i = i
            self.b = bankb[i]

        def ap(self, p0, p1, c0, c1):
            return pp[self.i // 2][p0:p1, (self.i % 2) * 512 + c0:(self.i % 2) * 512 + c1]

    banks = [Bank(i) for i in range(8)]
    rot7 = Ring(banks[0:7])
    rot2 = Ring(banks[6:8])

    def mm(out, lhsT, rhs, start, stop, R, W):
        return sc.op("pe", lambda e: e.matmul(out, lhsT=lhsT, rhs=rhs, start=start, stop=stop), reads=R, writes=W)

    vec = sb([128, NV], F32)
    ones = sb([128, 4, 128], BF16)
    onesf = sb([128, 128], F32)
    cos_s = sb([128, 64], F32)
    sin_s = sb([128, 64], F32)
    cqn_s = sb([128, 4, 64], BF16)
    ckvn_s = sb([128, 4, 64], BF16)
    kpe_s = sb([64, 64], BF16)
    sqr = Ring([sb([128, 512], BF16) for _ in range(3)])
    p12 = []
    scr_b = Buf()
    at_scr_b = Buf()
    ident = sb([128, 128], BF16, p12)
    mbt = sb([128, 8, 128], BF16, p12)
    gg = sb([128, 2], F32, p12)
    rstdk = sb([128, NKT, NH], F32, p12)
    cos_own = sb([128, TOWN], F32, p12)
    sin_own = sb([128, TOWN], F32, p12)
    cqn = sb([128, 4, TOWN], BF16, p12)

    sc.dma("sp", vec[:], vecs_in, writes=[vec.b])

    def V(name, i=0, p1=128):
        o = VO[name] + i
        return vec[0:p1, o:o + 1]

    for i, v in enumerate((1.0 / 2048, 1.0 / 512, 1.0 / 1024, 1.0)):
        sc.op("pool", lambda e, i=i, v=v: e.memset(ones[:, i, :], v), writes=[ones.b])
    sc.op("pool", lambda e: e.memset(onesf[:], 1.0), writes=[onesf.b])
    sc.dma("pool", ident[:], ident_in, writes=[ident.b])
    sc.dma("pool", mbt[:].rearrange("p a b -> p (a b)"), mb_in, writes=[mbt.b])
    sc.op("dve", lambda e: e.scalar_tensor_tensor(out=gg[:], in0=vec[:, VO["gq"]:VO["gq"] + 2], scalar=float(np.sqrt(192.0)),
                                                  in1=vec[:, VO["gk"]:VO["gk"] + 2], op0=ALU.mult, op1=ALU.mult),
          reads=[vec.b], writes=[gg.b])
    sc.op("pool", lambda e: e.memset(rstdk[:], 1.0), writes=[rstdk.b])

    def rsqrt_bc(ps_ap, epsname, out_tl, np_, T, tmp_tl):
        sc.op("act", lambda e: e.activation(out=tmp_tl[0:np_, 0:T], in_=ps_ap, func=AF.Sqrt, bias=V(epsname, 0, np_), scale=1.0),
              reads=[vec.b] + ps_ap_b[0], writes=[tmp_tl.b])
        sc.op("dve", lambda e: e.reciprocal(out=out_tl[0:np_, 0:T], in_=tmp_tl[0:np_, 0:T]), reads=[tmp_tl.b], writes=[out_tl.b])

    ps_ap_b = [[]]

    def sin_of(ang, out, T, t1, t2, shift, NP=64):
        src = ang
        if shift != 0.0:
            sc.op("dve", lambda e: e.tensor_scalar(out=t1[0:NP, 0:T], in0=ang[0:NP, 0:T], scalar1=float(shift), scalar2=None, op0=ALU.add),
                  reads=[ang.b], writes=[t1.b])
            src = t1
        sc.op("dve", lambda e: e.tensor_scalar(out=t2[0:NP, 0:T], in0=src[0:NP, 0:T], scalar1=float(1.0 / (2 * np.pi)), scalar2=MAGIC,
                                               op0=ALU.mult, op1=ALU.add), reads=[src.b], writes=[t2.b])
        sc.op("dve", lambda e: e.tensor_scalar(out=t2[0:NP, 0:T], in0=t2[0:NP, 0:T], scalar1=MAGIC, scalar2=None, op0=ALU.subtract),
              reads=[t2.b], writes=[t2.b])
        sc.op("dve", lambda e: e.scalar_tensor_tensor(out=t1[0:NP, 0:T], in0=t2[0:NP, 0:T], scalar=-C1, in1=src[0:NP, 0:T], op0=ALU.mult, op1=ALU.add),
              reads=[t2.b, src.b], writes=[t1.b])
        sc.op("dve", lambda e: e.scalar_tensor_tensor(out=t1[0:NP, 0:T], in0=t2[0:NP, 0:T], scalar=-C2, in1=t1[0:NP, 0:T], op0=ALU.mult, op1=ALU.add),
              reads=[t2.b, t1.b], writes=[t1.b])
        sc.op("dve", lambda e: e.tensor_scalar(out=t1[0:NP, 0:T], in0=t1[0:NP, 0:T], scalar1=PI, scalar2=-PI, op0=ALU.min, op1=ALU.max),
              reads=[t1.b], writes=[t1.b])
        sc.op("act", lambda e: e.activation(out=out, in_=t1[0:NP, 0:T], func=AF.Sin),
              reads=[t1.b], writes=[out.b if isinstance(out, Tl) else out_b[0]])

    out_b = [None]

    def rope_tables(pos_ap, T, cos_ap, sin_ap, dst_b, tmps, NP=64):
        pb, ang, t1, t2 = tmps
        sc.dma("sp", pb[0:NP, 0:T], pos_ap.partition_broadcast(NP), writes=[pb.b])
        sc.op("dve", lambda e: e.tensor_scalar(out=ang[0:NP, 0:T], in0=pb[0:NP, 0:T], scalar1=V("invf", 0, NP), scalar2=None, op0=ALU.mult),
              reads=[pb.b, vec.b], writes=[ang.b])
        out_b[0] = dst_b
        sin_of(ang, sin_ap, T, t1, t2, 0.0, NP)
        sin_of(ang, cos_ap, T, t1, t2, float(np.pi / 2), NP)

    def rms_h(xt, hT, T, st):
        bk = rot7.next()
        for kc in range(16):
            sq = sqr.next()
            sc.op("act", lambda e: e.activation(out=sq[:, 0:T], in_=xt[:, kc, 0:T], func=AF.Square), reads=[xt.b], writes=[sq.b])
            mm(bk.ap(0, 128, 0, T), ones[:, 0, :], sq[:, 0:T], kc == 0, kc == 15, [ones.b, sq.b], [bk.b])
        ps_ap_b[0] = [bk.b]
        rsqrt_bc(bk.ap(0, 128, 0, T), "c_eps", st["rs"], 128, T, st["rtmp"])
        rs = st["rs"]
        for kc in range(16):
            sc.op("dve", lambda e, kc=kc: e.scalar_tensor_tensor(out=hT[:, kc, 0:T], in0=xt[:, kc, 0:T], scalar=V("g_mix", kc),
                                                                 in1=rs[:, 0:T], op0=ALU.mult, op1=ALU.mult),
                  reads=[xt.b, rs.b, vec.b], writes=[hT.b])

    def proj16(bk, M, wt, c0, hT, T):
        for kc in range(16):
            mm(bk.ap(0, M, 0, T), wt[:, kc, c0:c0 + M], hT[:, kc, 0:T], kc == 0, kc == 15, [wt.b, hT.b], [bk.b])

    def lowrank_norm(wkq, c0, gname, hT, T, st, dst_fn, dst_bufs):
        raw = st["raw"]
        bk2 = rot7.next()
        pend = None
        for oc in range(4):
            bk = rot7.next()
            proj16(bk, 128, wkq, c0 + 128 * oc, hT, T)
            sc.op("dve", lambda e, oc=oc, bk=bk: e.tensor_copy(out=raw[:, oc, 0:T], in_=bk.ap(0, 128, 0, T)), reads=[bk.b], writes=[raw.b])
            sq = sqr.next()
            sc.op("act", lambda e, bk=bk, sq=sq: e.activation(out=sq[:, 0:T], in_=bk.ap(0, 128, 0, T), func=AF.Square), reads=[bk.b], writes=[sq.b])
            if pend is not None:
                mm(bk2.ap(0, 128, 0, T), ones[:, 1, :], pend[1][:, 0:T], pend[0] == 0, False, [ones.b, pend[1].b], [bk2.b])
            pend = (oc, sq)
        mm(bk2.ap(0, 128, 0, T), ones[:, 1, :], pend[1][:, 0:T], False, True, [ones.b, pend[1].b], [bk2.b])
        ps_ap_b[0] = [bk2.b]
        rsqrt_bc(bk2.ap(0, 128, 0, T), "c_eps", st["rs"], 128, T, st["rtmp"])
        rs = st["rs"]
        for oc in range(4):
            sc.op("dve", lambda e, oc=oc: e.scalar_tensor_tensor(out=dst_fn(oc), in0=raw[:, oc, 0:T], scalar=V(gname, oc), in1=rs[:, 0:T],
                                                                 op0=ALU.mult, op1=ALU.mult),
                  reads=[raw.b, rs.b, vec.b], writes=dst_bufs)

    def kpe_rope(wkq, hT, T, cos_ap, sin_ap, cs_bufs, st, dst_ap, dst_bufs, coff=0):
        bA = rot7.next()
        proj16(bA, 64, wkq, 1024 - coff, hT, T)
        bB = rot7.next()
        proj16(bB, 64, wkq, 1088 - coff, hT, T)
        t1, t2 = st["r1"], st["r2"]
        sc.op("dve", lambda e: e.tensor_tensor(out=t1[:, 0:T], in0=bA.ap(0, 64, 0, T), in1=cos_ap, op=ALU.mult), reads=[bA.b] + cs_bufs, writes=[t1.b])
        sc.op("dve", lambda e: e.tensor_tensor(out=t2[:, 0:T], in0=bB.ap(0, 64, 0, T), in1=sin_ap, op=ALU.mult), reads=[bB.b] + cs_bufs, writes=[t2.b])
        sc.op("dve", lambda e: e.tensor_tensor(out=dst_ap, in0=t1[:, 0:T], in1=t2[:, 0:T], op=ALU.add), reads=[t1.b, t2.b], writes=dst_bufs)

    p1 = []
    wkq = sb([128, 16, 1152], BF16, p1)
    for kc in range(16):
        sc.dma("pool", wkq[:, kc, 0:1088], w_in[kc * 128:(kc + 1) * 128, 0:1088], writes=[wkq.b])
    sc.op("pool", lambda e: e.tensor_scalar(out=wkq[:, :, 1088:1120], in0=wkq[:, :, 1056:1088], scalar1=-1.0, scalar2=None, op0=ALU.mult),
          reads=[wkq.b], writes=[wkq.b])
    sc.op("pool", lambda e: e.tensor_copy(out=wkq[:, :, 1120:1152], in_=wkq[:, :, 1024:1056]), reads=[wkq.b], writes=[wkq.b])

    st = {"rs": sb([128, 512], F32, p1), "rtmp": sb([128, 512], F32, p1), "raw": sb([128, 4, 512], F32, p1),
          "r1": sb([64, 512], F32, p1), "r2": sb([64, 512], F32, p1)}
    rtm = [sb([128, 512], F32, p1) for _ in range(4)]
    xts = Ring([sb([128, 16, 512], F32, p1) for _ in range(1)])
    hTs = Ring([sb([128, 16, 512], BF16, p1) for _ in range(1)])
    ckvf = Ring([sb([128, 4, 512], F32, p1) for _ in range(1)])
    kpef = Ring([sb([64, 512], F32, p1) for _ in range(1)])

    for t in range(4):
        out_b[0] = cos_own.b
        rope_tables(pos_own[:, t * 512:(t + 1) * 512], 512, cos_own[:, t * 512:(t + 1) * 512], sin_own[:, t * 512:(t + 1) * 512], cos_own.b, rtm, 128)
    rope_tables(pos_s[:, 0:64], 64, cos_s[:, 0:64], sin_s[:, 0:64], cos_s.b, rtm, 128)
    sin_own.b = cos_own.b
    sin_s.b = cos_s.b

    def own_tokens(x_ap, T, cos_ap, sin_ap, cs_b, cq_dst, cq_b, ckv_out, kpe_out, sample):
        xt = xts.next()
        hT = hTs.next()
        sc.dma("sp", xt[:, :, 0:T], x_ap.rearrange("(k p) t -> p k t", p=128), writes=[xt.b])
        rms_h(xt, hT, T, st)
        lowrank_norm(wkq, 0, "g_q_a", hT, T, st, lambda oc: cq_dst(oc), [cq_b])
        cf = ckvf.next()
        lowrank_norm(wkq, 512, "g_kv_a", hT, T, st, lambda oc: cf[:, oc, 0:T], [cf.b])
        sc.dma("sp", ckv_out.rearrange("(k p) t -> p k t", p=128), cf[:, :, 0:T], reads=[cf.b])
        kf = kpef.next()
        kpe_rope(wkq, hT, T, cos_ap, sin_ap, [cs_b], st, kf[:, 0:T], [kf.b])
        sc.dma("sp", kpe_out, kf[:, 0:T], reads=[kf.b])
        if sample:
            sc.op("dve", lambda e: e.tensor_copy(out=ckvn_s[:, :, 0:T], in_=cf[:, :, 0:T]), reads=[cf.b], writes=[ckvn_s.b])
            sc.op("dve", lambda e: e.tensor_copy(out=kpe_s[:, 0:T], in_=kf[:, 0:T]), reads=[kf.b], writes=[kpe_s.b])

    if "p1a" in PHASES:
        for t in range(0 if SAMPLE_ONLY else 4):
            c = slice(t * 512, (t + 1) * 512)
            own_tokens(xT_own[:, c], 512, cos_own[0:64, c], sin_own[0:64, c], cos_own.b,
                       lambda oc, c=c: cqn[:, oc, c], cqn.b, ckvT_own[:, c], kpeT_own[:, c], False)
        own_tokens(xT_s[:, 0:64], 64, cos_s[0:64, 0:64], sin_s[0:64, 0:64], cos_s.b,
                   lambda oc: cqn_s[:, oc, 0:64], cqn_s.b, ckvT_s[:, 0:64], kpeT_s[:, 0:64], True)

    free(p1)
    p1 = []
    if os.environ.get("KDEBUG"):
        print("sbuf remaining before P1b", nc.sbuf_bytes_remaining)
    wkq = sb([128, 16, 640], BF16, p1)
    for kc in range(16):
        sc.dma("pool", wkq[:, kc, 0:576], w_in[kc * 128:(kc + 1) * 128, 512:1088], writes=[wkq.b])
    sc.op("pool", lambda e: e.tensor_scalar(out=wkq[:, :, 576:608], in0=wkq[:, :, 544:576], scalar1=-1.0, scalar2=None, op0=ALU.mult),
          reads=[wkq.b], writes=[wkq.b])
    sc.op("pool", lambda e: e.tensor_copy(out=wkq[:, :, 608:640], in_=wkq[:, :, 512:544]), reads=[wkq.b], writes=[wkq.b])
    wkv = sb([128, 4, 4096], BF16, p1)
    for kc in range(4):
        sc.dma("pool", wkv[:, kc, :], w_kv_up[kc * 128:(kc + 1) * 128, :], writes=[wkv.b])
    rtm = [sb([64, 512], F32, p1) for _ in range(4)]
    st = {"rs": sb([128, 512], F32, p1), "rtmp": sb([128, 512], F32, p1), "raw": sb([128, 4, 512], F32, p1),
          "r1": rtm[2], "r2": rtm[3]}
    xts = Ring([sb([128, 16, 512], BF16, p1) for _ in range(2)])
    hTs = xts
    kbufs = Ring([sb([128, 16, 512], BF16, p1) for _ in range(1)])
    vbufs = Ring([sb([128, 4, 16, 128], BF16, p1) for _ in range(1)])
    sqrope = Ring([sb([64, 512], BF16, p1) for _ in range(2)])
    ssq_bank = banks[7]
    wkv_v = wkv[:].rearrange("p k (h c) -> p k h c", c=256)

    def gen_kv(ckvn, kpeb, T, kt0):
        nsub = (T + 127) // 128
        nk = min(128, T)
        k0 = kt0 * 128
        kpe_ap, kpe_b = kpeb
        sc.dma("sp", KR[:, k0:k0 + T], kpe_ap, reads=[kpe_b], writes=[scr_b])
        sr = sqrope.next()
        sc.op("pool", lambda e: e.tensor_tensor(out=sr[:, 0:T], in0=kpe_ap, in1=kpe_ap, op=ALU.mult), reads=[kpe_b], writes=[sr.b])
        kb = kbufs.next()

        def k_tiny(h, sq):
            for sub in range(nsub):
                col = sub * 16 + h
                mm(ssq_bank.ap(0, nk, col, col + 1), sq[:, sub * 128:sub * 128 + nk], ones[:, 3, 0:1], True, False, [sq.b, ones.b], [ssq_bank.b])
                mm(ssq_bank.ap(0, nk, col, col + 1), sr[0:64, sub * 128:sub * 128 + nk], ones[0:64, 3, 0:1], False, True, [sr.b, ones.b], [ssq_bank.b])
        pend = None
        for h in range(NH):
            bk = rot7.next()
            for kc in range(4):
                mm(bk.ap(0, 128, 0, T), wkv[:, kc, h * 256:h * 256 + 128], ckvn[:, kc, 0:T], kc == 0, kc == 3, [wkv.b, ckvn.b], [bk.b])
            sc.op("dve", lambda e, h=h, bk=bk: e.tensor_copy(out=kb[:, h, 0:T], in_=bk.ap(0, 128, 0, T)), reads=[bk.b], writes=[kb.b])
            sq = sqr.next()
            sc.op("act", lambda e, bk=bk, sq=sq: e.activation(out=sq[:, 0:T], in_=bk.ap(0, 128, 0, T), func=AF.Square), reads=[bk.b], writes=[sq.b])
            if pend is not None:
                k_tiny(*pend)
            pend = (h, sq)
        k_tiny(*pend)
        tmp = st["rtmp"]
        sc.op("act", lambda e: e.activation(out=tmp[0:nk, 0:nsub * 16], in_=ssq_bank.ap(0, nk, 0, nsub * 16), func=AF.Sqrt,
                                            bias=V("c_eps192", 0, nk), scale=1.0), reads=[ssq_bank.b, vec.b], writes=[tmp.b])
        sc.op("dve", lambda e: e.reciprocal(out=rstdk[0:nk, kt0:kt0 + nsub, :], in_=tmp[0:nk, 0:nsub * 16].rearrange("p (s h) -> p s h", h=16)),
              reads=[tmp.b], writes=[rstdk.b])
        sc.dma("sp", KT[:, :, k0:k0 + T].rearrange("h d t -> d h t"), kb[:, :, 0:T], reads=[kb.b], writes=[scr_b])
        vb = vbufs.next()
        if nk < 128:
            sc.op("pool", lambda e: e.memset(vb[:, 0, :, :], 0.0), writes=[vb.b])
        for sub in range(nsub):
            for hg in range(4):
                bk = rot7.next()
                for kc in range(4):
                    mm(bk.ap(0, nk, 0, 512).rearrange("p (h d) -> p h d", d=128), ckvn[:, kc, sub * 128:sub * 128 + nk],
                       wkv_v[:, kc, hg * 4:(hg + 1) * 4, 128:256], kc == 0, kc == 3, [wkv.b, ckvn.b], [bk.b])
                eng = "act" if (hg % 2 == 0) else "dve"
                if eng == "act":
                    sc.op("act", lambda e, bk=bk, sub=sub, hg=hg: e.copy(out=vb[0:nk, sub, hg * 4:(hg + 1) * 4, :],
                                                                         in_=bk.ap(0, nk, 0, 512).rearrange("p (h d) -> p h d", d=128)),
                          reads=[bk.b], writes=[vb.b])
                else:
                    sc.op("dve", lambda e, bk=bk, sub=sub, hg=hg: e.tensor_copy(out=vb[0:nk, sub, hg * 4:(hg + 1) * 4, :],
                                                                                in_=bk.ap(0, nk, 0, 512).rearrange("p (h d) -> p h d", d=128)),
                          reads=[bk.b], writes=[vb.b])
        for sub in range(nsub):
            sc.dma("sp", V2[:, :, kt0 + sub, :].rearrange("h p d -> p h d"), vb[:, sub, :, :], reads=[vb.b], writes=[scr_b])

    if "p1b" in PHASES:
        ckvb = Ring([sb([128, 4, 512], BF16, p1) for _ in range(2)])
        kpeb = Ring([sb([64, 512], BF16, p1) for _ in range(2)])
        cst = rtm
        cosb = sb([64, 512], F32, p1)
        sinb = sb([64, 512], F32, p1)
        sinb.b = cosb.b
        for t in range(0 if SAMPLE_ONLY else S // 512):
            c = slice(t * 512, (t + 1) * 512)
            xt = xts.next()
            hT = hTs.next()
            sc.dma("pool", xt[:, :, :], xT_all[:, c].rearrange("(k p) t -> p k t", p=128), writes=[xt.b])
            rms_h(xt, hT, 512, st)
            cb = ckvb.next()
            lowrank_norm(wkq, 0, "g_kv_a", hT, 512, st, lambda oc, cb=cb: cb[:, oc, :], [cb.b])
            out_b[0] = cosb.b
            rope_tables(pos_all[:, c], 512, cosb[:, :], sinb[:, :], cosb.b, cst)
            kb_ = kpeb.next()
            kpe_rope(wkq, hT, 512, cosb[:, :], sinb[:, :], [cosb.b], st, kb_[:, :], [kb_.b], coff=512)
            gen_kv(cb, (kb_[:, :], kb_.b), 512, t * 4)
        for b in range(2):
            for t in range(8):
                c = slice(t * 512, (t + 1) * 512)
                cb = ckvb.next()
                sc.dma("pool", cb[:, :, :], cckvT[b, :, c].rearrange("(k p) t -> p k t", p=128), writes=[cb.b])
                kb_ = kpeb.next()
                sc.dma("pool", kb_[:, :], ckpeT[b, :, c], writes=[kb_.b])
                gen_kv(cb, (kb_[:, :], kb_.b), 512, 128 + 33 * b + 4 * t)
            cb = ckvb.next()
            sc.op("dve", lambda e, cb=cb, b=b: e.tensor_copy(out=cb[:, :, 0:32], in_=ckvn_s[:, :, 32 * b:32 * b + 32]), reads=[ckvn_s.b], writes=[cb.b])
            kb_ = kpeb.next()
            sc.op("dve", lambda e, kb_=kb_, b=b: e.tensor_copy(out=kb_[:, 0:32], in_=kpe_s[:, 32 * b:32 * b + 32]), reads=[kpe_s.b], writes=[kb_.b])
            gen_kv(cb, (kb_[:, 0:32], kb_.b), 32, 128 + 33 * b + 32)
    free(p1)


    wblocks = []
    for g in range(2):
        wblocks.append([(0, w_in, 16, 1088 + 512 * g, 512)])
        wblocks.append([(0, w_in, 16, 2112 + 512 * g, 512)])
    for oc in range(16):
        wblocks.append([(0, w_ao, 16, 128 * oc, 128), (2048, w_co, 8, 128 * oc, 128),
                        (3072, w_in, 16, 3136 + 128 * oc, 128), (5120, w_in, 16, 3136 + 2048 + 128 * oc, 128)])
    for g in range(4):
        wblocks.append([(0, w_o, 16, 512 * g, 512)])
    for g in range(11):
        wblocks.append([(0, w_fg, 16, 512 * g, 512)])
        wblocks.append([(0, w_fu, 16, 512 * g, 512)])
    for oc in range(16):
        wblocks.append([(0, w_fd, 44, 128 * oc, 128)])
    assert len(wblocks) == 62
    wb_scr_b = Buf()

    def convert_block(bi):
        for (off, w, KC, c0, ncols) in wblocks[bi]:
            sc.dma("pool", WB[bi, :, off:off + KC * ncols].rearrange("p (k c) -> p k c", c=ncols),
                   w[0:KC * 128, c0:c0 + ncols].rearrange("(k p) c -> p k c", p=128), writes=[wb_scr_b])
    conv_done = [0]

    def convert_upto(n):
        while conv_done[0] < min(n, 62):
            convert_block(conv_done[0])
            conv_done[0] += 1

    if "p2" in PHASES:
        p2 = []
        wqs = Ring([sb([128, 4, 384], BF16, p2) for _ in range(2)])
        qns = Ring([sb([128, TOWN + 64], BF16, p2) for _ in range(2)])
        qrs = Ring([sb([128, TOWN + 64], BF16, p2) for _ in range(2)])
        kch = Ring([sb([128, 33 * 128], BF16, p2) for _ in range(2)])
        krch = Ring([sb([128, 33 * 128], BF16, p2) for _ in range(2)])
        vch = Ring([sb([128, 33, 128], BF16, p2) for _ in range(2)])
        pTs = Ring([sb([128, 1024], BF16, p2) for _ in range(3)])
        pTs_small = Ring([sb([128, 64], BF16, p2) for _ in range(6)])
        racc = sb([128, 1024], F32, p2)
        rr = sb([128, 1024], F32, p2)
        ato = Ring([sb([128, 1024], BF16, p2) for _ in range(2)])
        q32 = sb([128, 512], F32, p2)
        qt1 = sb([128, 512], F32, p2)
        qt2 = sb([128, 512], F32, p2)
        rq = sb([128, 512], F32, p2)
        rqt = sb([128, 512], F32, p2)
        sqq = Ring([sb([128, 512], BF16, p2) for _ in range(2)])
        Sb = [(banks[0], banks[1]), (banks[2], banks[3]), (banks[6], banks[7])]
        Ob = (banks[4], banks[5])
        zero_t = sb([128, 512], BF16, p2)
        sc.op("pool", lambda e: e.memset(zero_t[:, :], 0.0), writes=[zero_t.b])

        rotq = Ring([banks[0], banks[1], banks[2], banks[3], banks[6], banks[7]])

        def compute_q(wq, src, src_b, c0, T, cos_ap, sin_ap, cs_b, qn, qr, d0):
            bA = rotq.next()
            for kc in range(4):
                mm(bA.ap(0, 128, 0, T), wq[:, kc, 0:128], src(kc, c0, T), kc == 0, kc == 3, [wq.b, src_b], [bA.b])
            sq = sqq.next()
            sc.op("act", lambda e: e.activation(out=sq[:, 0:T], in_=bA.ap(0, 128, 0, T), func=AF.Square), reads=[bA.b], writes=[sq.b])
            bB = rotq.next()
            bC = rotq.next()
            bD = rotq.next()
            for kc in range(4):
                mm(bB.ap(0, 128, 0, T), wq[:, kc, 128:256], src(kc, c0, T), kc == 0, kc == 3, [wq.b, src_b], [bB.b])
            sc.op("dve", lambda e: e.tensor_tensor(out=qt1[:, 0:T], in0=bB.ap(0, 128, 0, T), in1=cos_ap, op=ALU.mult), reads=[bB.b, cs_b], writes=[qt1.b])
            for kc in range(4):
                mm(bC.ap(0, 128, 0, T), wq[:, kc, 256:384], src(kc, c0, T), kc == 0, kc == 3, [wq.b, src_b], [bC.b])
            sc.op("dve", lambda e: e.tensor_tensor(out=qt2[:, 0:T], in0=bC.ap(0, 128, 0, T), in1=sin_ap, op=ALU.mult), reads=[bC.b, cs_b], writes=[qt2.b])
            sc.op("dve", lambda e: e.tensor_tensor(out=q32[:, 0:T], in0=qt1[:, 0:T], in1=qt2[:, 0:T], op=ALU.add), reads=[qt1.b, qt2.b], writes=[q32.b])
            sq2 = sqq.next()
            sc.op("act", lambda e: e.activation(out=sq2[0:64, 0:T], in_=q32[0:64, 0:T], func=AF.Square), reads=[q32.b], writes=[sq2.b])
            mm(bD.ap(0, 128, 0, T), ones[:, 3, :], sq[:, 0:T], True, False, [ones.b, sq.b], [bD.b])
            mm(bD.ap(0, 128, 0, T), ones[0:64, 3, :], sq2[0:64, 0:T], False, True, [ones.b, sq2.b], [bD.b])
            sc.op("act", lambda e: e.activation(out=rqt[:, 0:T], in_=bD.ap(0, 128, 0, T), func=AF.Sqrt, bias=V("c_eps192"), scale=1.0),
                  reads=[bD.b, vec.b], writes=[rqt.b])
            sc.op("dve", lambda e: e.reciprocal(out=rq[:, 0:T], in_=rqt[:, 0:T]), reads=[rqt.b], writes=[rq.b])
            sc.op("dve", lambda e: e.scalar_tensor_tensor(out=qn[:, d0:d0 + T], in0=bA.ap(0, 128, 0, T), scalar=gg[:, 0:1], in1=rq[:, 0:T],
                                                          op0=ALU.mult, op1=ALU.mult), reads=[bA.b, gg.b, rq.b], writes=[qn.b])
            sc.op("dve", lambda e: e.scalar_tensor_tensor(out=qr[:, d0:d0 + T], in0=q32[:, 0:T], scalar=gg[:, 1:2], in1=rq[:, 0:T],
                                                          op0=ALU.mult, op1=ALU.mult), reads=[q32.b, gg.b, rq.b], writes=[qr.b])

        def split512(a, b):
            out = []
            while a < b:
                e = min(b, (a // 512 + 1) * 512)
                out.append((a, e))
                a = e
            return out

        def attend(h, qn, qr, q0, NQ, steps, at_dst):
            n = len(steps)
            chunks = {}
            small = NQ <= 64
            slots = [(banks[0],), (banks[1],), (banks[2],), (banks[3],)] if small else Sb
            depth = 3 if small else 2

            def load_chunk(ci):
                i0 = ci * 33
                i1 = min(n, i0 + 33)
                ktA = steps[i0][0]
                nkeys = sum(s[1] for s in steps[i0:i1])
                kc_, kr_, vc_ = kch.next(), krch.next(), vch.next()
                sc.dma("sp", kc_[:, 0:nkeys], KT[h, :, ktA * 128:ktA * 128 + nkeys], reads=[scr_b], writes=[kc_.b])
                sc.dma("sp", kr_[0:64, 0:nkeys], KR[:, ktA * 128:ktA * 128 + nkeys], reads=[scr_b], writes=[kr_.b])
                sc.dma("sp", kr_[64:128, 0:nkeys], KR[:, ktA * 128:ktA * 128 + nkeys], reads=[scr_b], writes=[kr_.b])
                sc.dma("sp", vc_[:, 0:i1 - i0, :], V2[h, :, ktA:ktA + (i1 - i0), :], reads=[scr_b], writes=[vc_.b])
                chunks[ci] = (kc_, kr_, vc_)

            def qk(i):
                kt, nk, a, m = steps[i]
                ci, j = divmod(i, 33)
                if ci not in chunks:
                    load_chunk(ci)
                if j == 0 and (ci + 1) * 33 < n and (ci + 1) not in chunks:
                    load_chunk(ci + 1)
                kc_, kr_, _ = chunks[ci]
                S2 = slots[i % len(slots)]
                grs = split512(a, NQ)
                for (g0, g1) in grs:
                    bk = S2[g0 // 512]
                    o0, o1 = g0 % 512, g0 % 512 + (g1 - g0)
                    mm(bk.ap(0, nk, o0, o1), kc_[:, j * 128:j * 128 + nk], qn[:, q0 + g0:q0 + g1], True, False, [kc_.b, qn.b], [bk.b])
                if m is not None:
                    bk = S2[a // 512]
                    o = a % 512
                    mm(bk.ap(0, nk, o, o + 128), ident[:, 0:nk], mbt[:, m, :], False, False, [ident.b, mbt.b], [bk.b])
                for gi, (g0, g1) in enumerate(grs):
                    bk = S2[g0 // 512]
                    o0, o1 = g0 % 512, g0 % 512 + (g1 - g0)
                    r0 = 64 * (gi % 2)
                    mm(bk.ap(0, nk, o0, o1), kr_[r0:r0 + 64, j * 128:j * 128 + nk], qr[r0:r0 + 64, q0 + g0:q0 + g1], False, True, [kr_.b, qr.b], [bk.b])

            pts = {}

            def expo(i):
                kt, nk, a, m = steps[i]
                S2 = slots[i % len(slots)]
                pt = pTs_small.next() if small else pTs.next()
                pts[i] = pt
                src = S2[0].ap(0, nk, a, NQ) if small else pp[S2[0].i // 2][0:nk, a:NQ]
                sc.op("act", lambda e: e.activation(out=pt[0:nk, a:NQ], in_=src, func=AF.Exp, scale=rstdk[0:nk, kt, h:h + 1]),
                      reads=[b_.b for b_ in S2] + [rstdk.b], writes=[pt.b])
                if i == 0:
                    sc.op("dve", lambda e: e.tensor_copy(out=racc[:, 0:NQ], in_=pt[:, 0:NQ]), reads=[pt.b], writes=[racc.b])
                else:
                    sc.op("dve", lambda e: e.tensor_tensor(out=racc[0:nk, a:NQ], in0=racc[0:nk, a:NQ], in1=pt[0:nk, a:NQ], op=ALU.add),
                          reads=[pt.b, racc.b], writes=[racc.b])

            def pv(i):
                kt, nk, a, m = steps[i]
                ci, j = divmod(i, 33)
                _, _, vc_ = chunks[ci]
                pt = pts.pop(i)
                for (g0, g1) in split512(a, NQ):
                    bk = Ob[g0 // 512]
                    o0, o1 = g0 % 512, g0 % 512 + (g1 - g0)
                    mm(bk.ap(0, 128, o0, o1), vc_[0:nk, j, :], pt[0:nk, g0:g1], i == 0, small and i == n - 1, [vc_.b, pt.b], [bk.b])

            for i in range(min(depth, n)):
                qk(i)
            for i in range(n):
                expo(i)
                if i + depth < n:
                    qk(i + depth)
                pv(i)
            if not small:
                for (g0, g1) in split512(0, NQ):
                    bk = Ob[g0 // 512]
                    mm(bk.ap(0, 128, 0, g1 - g0), zero_t[:, 0:128], zero_t[:, 0:g1 - g0], False, True, [zero_t.b], [bk.b])
            for (g0, g1) in split512(0, NQ):
                bk = Sb[0][g0 // 512]
                w = g1 - g0
                mm(bk.ap(0, 128, 0, w), onesf[:, :], racc[:, g0:g1], True, True, [onesf.b, racc.b], [bk.b])
                sc.op("dve", lambda e, bk=bk, g0=g0, g1=g1, w=w: e.reciprocal(out=rr[:, g0:g1], in_=bk.ap(0, 128, 0, w)), reads=[bk.b], writes=[rr.b])
            at = ato.next()
            for (g0, g1) in split512(0, NQ):
                bk = Ob[g0 // 512]
                w = g1 - g0
                sc.op("dve", lambda e, bk=bk, g0=g0, g1=g1, w=w: e.tensor_tensor(out=at[:, g0:g1], in0=bk.ap(0, 128, 0, w), in1=rr[:, g0:g1], op=ALU.mult),
                      reads=[bk.b, rr.b], writes=[at.b])
            sc.dma("sp", at_dst, at[:, 0:NQ], reads=[at.b], writes=[at_scr_b])

        for h in range(NH):
            convert_upto(4 * (h + 1))
            wq = wqs.next()
            for kc in range(4):
                sc.dma("pool", wq[:, kc, 0:192], w_q_up[kc * 128:(kc + 1) * 128, h * 192:(h + 1) * 192], writes=[wq.b])
            sc.op("pool", lambda e: e.tensor_copy(out=wq[:, :, 192:256], in_=wq[:, :, 128:192]), reads=[wq.b], writes=[wq.b])
            sc.op("pool", lambda e: e.tensor_scalar(out=wq[:, :, 256:288], in0=wq[:, :, 160:192], scalar1=-1.0, scalar2=None, op0=ALU.mult),
                  reads=[wq.b], writes=[wq.b])
            sc.op("pool", lambda e: e.tensor_copy(out=wq[:, :, 288:320], in_=wq[:, :, 128:160]), reads=[wq.b], writes=[wq.b])
            sc.op("pool", lambda e: e.tensor_copy(out=wq[:, :, 320:384], in_=wq[:, :, 256:320]), reads=[wq.b], writes=[wq.b])
            qn, qr = qns.next(), qrs.next()
            for t in range(0 if SAMPLE_ONLY else 4):
                c = slice(t * 512, (t + 1) * 512)
                compute_q(wq, lambda kc, c0, T: cqn[:, kc, c0:c0 + T], cqn.b, t * 512, 512, cos_own[:, c], sin_own[:, c], cos_own.b, qn, qr, t * 512)
            compute_q(wq, lambda kc, c0, T: cqn_s[:, kc, c0:c0 + T], cqn_s.b, 0, 64, cos_s[:, 0:64], sin_s[:, 0:64], cos_s.b, qn, qr, TOWN)
            for qh in range(0 if SAMPLE_ONLY else 2):
                steps = []
                for kt in range(64 * (qh + 1)):
                    j0 = max(kt // 8, 8 * qh)
                    m = (kt % 8) if (kt // 8 >= 8 * qh) else None
                    steps.append((kt, 128, (j0 - 8 * qh) * 128, m))
                attend(h, qn, qr, 1024 * qh, 1024, steps, AT[h, :, 1024 * qh:1024 * (qh + 1)])
            for b in range(2):
                steps = [(128 + 33 * b + i, 128 if i < 32 else 32, 0, None) for i in range(33)]
                attend(h, qn, qr, TOWN + 32 * b, 32, steps, AT[h, :, TOWN + 32 * b:TOWN + 32 * b + 32])
        free(p2)
    free(p12)

    if "p3" in PHASES:
        convert_upto(62)
        sc.barrier()
        p3 = []
        wb = Ring([sb([128, 8192], BF16, p3) for _ in range(3)])

        def wload(bi):
            t = wb.next()
            n = max(off + KC * ncols for (off, w, KC, c0, ncols) in wblocks[bi])
            sc.dma("pool" if (bi % 2) else "sp", t[:, 0:n], WB[bi, :, 0:n], reads=[wb_scr_b], writes=[t.b])
            views = [t[:, off:off + KC * ncols].rearrange("p (k c) -> p k c", c=ncols) for (off, w, KC, c0, ncols) in wblocks[bi]]
            return views, t.b

        xt = sb([128, 16, 512], F32, p3)
        hT = sb([128, 16, 512], BF16, p3)
        hm = sb([128, 128], F32, p3)
        arena = sb([128, 11264], F32, p3)
        upad = Tl(arena.t[:, 0:5120].rearrange("p (a b c) -> p a b c", a=8, b=4))
        ycv = Tl(arena.t[:, 5120:9216].rearrange("p (a c) -> p a c", a=8))
        ybf = sb([128, 8, 512], BF16, p3)
        zc = ybf
        att_raw = sb([128, 8192], BF16, p3)
        att = Tl(att_raw.t[:, :].rearrange("p (h t) -> p h t", t=512))
        xh = Tl(att_raw.t[:, 0:4096].bitcast(F32).rearrange("p (k t) -> p k t", t=128))
        hhT = Tl(att_raw.t[:, 4096:6144].rearrange("p (k t) -> p k t", t=128))
        att.b = xh.b = hhT.b = att_raw.b
        z2 = sb([128, 16, 512], BF16, p3)
        act_t = Tl(arena.t[:, :].bitcast(BF16).rearrange("p (a c) -> p a c", a=44))
        st3 = {"rs": sb([128, 512], F32, p3), "rtmp": sb([128, 512], F32, p3)}
        ident3 = sb([128, 128], BF16, p3)
        sc.dma("pool", ident3[:], ident_in, writes=[ident3.b])
        diags = Ring([sb([128, 128], BF16, p3) for _ in range(4)])
        upbs = Ring([sb([128, 4, 160], BF16, p3) for _ in range(2)])
        sg = Ring([sb([128, 512], F32, p3) for _ in range(2)])
        tA = Ring([sb([128, 512], F32, p3) for _ in range(1)])
        tB = Ring([sb([128, 512], F32, p3) for _ in range(1)])
        ln_m = st3["rs"]
        ln_r = st3["rtmp"]
        yst = Ring([sb([128, 512], F32, p3) for _ in range(1)])

        def post(x_ap, T, J, L, halo_ap, hm_ap, state, at_c0, y_out, u_out):
            sc.dma("sp", xt[:, :, 0:T], x_ap.rearrange("(k p) t -> p k t", p=128), writes=[xt.b])
            rms_h(xt, hT, T, st3)
            HT = J * 32
            if halo_ap is not None:
                sc.dma("sp", xh[:, :, 0:HT], halo_ap.rearrange("(k p) t -> p k t", p=128), writes=[xh.b])
                sc.dma("sp", hm[:, 0:HT], hm_ap.partition_broadcast(128), writes=[hm.b])
                rms_h(xh, hhT, HT, st3)
            else:
                for b in range(J):
                    sc.dma("sp", upad[:, :, b, 2:32], state[b].rearrange("(k p) r -> p k r", p=128), writes=[upad.b])
            for g in range(2):
                (wa,), wab = wload(2 * g)
                (wg,), wgb = wload(2 * g + 1)
                for o in range(4):
                    oc = g * 4 + o
                    bA, bG = rot7.next(), rot7.next()
                    for kc in range(16):
                        mm(bA.ap(0, 128, 0, T), wa[:, kc, o * 128:(o + 1) * 128], hT[:, kc, 0:T], kc == 0, kc == 15, [wab, hT.b], [bA.b])
                    for kc in range(16):
                        mm(bG.ap(0, 128, 0, T), wg[:, kc, o * 128:(o + 1) * 128], hT[:, kc, 0:T], kc == 0, kc == 15, [wgb, hT.b], [bG.b])
                    s_ = sg.next()
                    sc.op("act", lambda e, bG=bG, s_=s_, oc=oc: e.activation(out=s_[:, 0:T], in_=bG.ap(0, 128, 0, T), func=AF.Sigmoid,
                                                                            bias=V("b_glu", 8 + oc), scale=1.0), reads=[bG.b, vec.b], writes=[s_.b])
                    sc.op("dve", lambda e, bA=bA, s_=s_, oc=oc: e.scalar_tensor_tensor(
                        out=upad[:, oc, 0:J, 32:32 + L], in0=bA.ap(0, 128, 0, T).rearrange("p (j l) -> p j l", l=L), scalar=V("b_glu", oc),
                        in1=s_[:, 0:T].rearrange("p (j l) -> p j l", l=L), op0=ALU.add, op1=ALU.mult), reads=[bA.b, s_.b, vec.b], writes=[upad.b])
                    if halo_ap is not None:
                        bA, bG = rot7.next(), rot7.next()
                        for kc in range(16):
                            mm(bA.ap(0, 128, 0, HT), wa[:, kc, o * 128:(o + 1) * 128], hhT[:, kc, 0:HT], kc == 0, kc == 15, [wab, hhT.b], [bA.b])
                        for kc in range(16):
                            mm(bG.ap(0, 128, 0, HT), wg[:, kc, o * 128:(o + 1) * 128], hhT[:, kc, 0:HT], kc == 0, kc == 15, [wgb, hhT.b], [bG.b])
                        s_ = sg.next()
                        sc.op("act", lambda e, bG=bG, s_=s_, oc=oc: e.activation(out=s_[:, 0:HT], in_=bG.ap(0, 128, 0, HT), func=AF.Sigmoid,
                                                                                bias=V("b_glu", 8 + oc), scale=1.0), reads=[bG.b, vec.b], writes=[s_.b])
                        t_ = tA.next()
                        sc.op("dve", lambda e, bA=bA, s_=s_, oc=oc, t_=t_: e.scalar_tensor_tensor(
                            out=t_[:, 0:HT], in0=bA.ap(0, 128, 0, HT), scalar=V("b_glu", oc), in1=s_[:, 0:HT], op0=ALU.add, op1=ALU.mult),
                            reads=[bA.b, s_.b, vec.b], writes=[t_.b])
                        sc.op("dve", lambda e, oc=oc, t_=t_: e.tensor_tensor(
                            out=upad[:, oc, 0:J, 0:32], in0=t_[:, 0:HT].rearrange("p (j l) -> p j l", l=32),
                            in1=hm[:, 0:HT].rearrange("p (j l) -> p j l", l=32), op=ALU.mult), reads=[t_.b, hm.b], writes=[upad.b])
            if u_out is not None:
                u_out()
            sc.dma("sp", att[:, :, 0:T], AT[:, :, at_c0:at_c0 + T].rearrange("h d t -> d h t"), reads=[at_scr_b], writes=[att.b])
            for oc in range(8):
                upb = upbs.next()
                sc.op("pool", lambda e, oc=oc, upb=upb: e.tensor_copy(out=upb[:, 0:J, 2:32 + L], in_=upad[:, oc, 0:J, 2:32 + L]), reads=[upad.b], writes=[upb.b])
                bk = rot7.next()
                for k in range(CW):
                    dg = diags.next()
                    wcol = vec[:, VO["w_dw"] + oc * 31 + k:VO["w_dw"] + oc * 31 + k + 1]
                    sc.op("dve", lambda e, dg=dg, wcol=wcol: e.tensor_scalar(out=dg[:, :], in0=ident3[:, :], scalar1=wcol, scalar2=None, op0=ALU.mult),
                          reads=[ident3.b, vec.b], writes=[dg.b])
                    mm(bk.ap(0, 128, 0, T).rearrange("p (j l) -> p j l", l=L), dg[:, :], upb[:, 0:J, 2 + k:2 + k + L], k == 0, k == CW - 1,
                       [dg.b, upb.b], [bk.b])
                sc.op("act", lambda e, bk=bk, oc=oc: e.activation(out=ycv[:, oc, 0:T], in_=bk.ap(0, 128, 0, T), func=AF.Identity, bias=V("b_dw", oc), scale=1.0),
                      reads=[bk.b, vec.b], writes=[ycvb[oc]])
            b1, b2 = rot7.next(), rot7.next()
            for oc in range(8):
                sc.op("pool", lambda e, oc=oc: e.tensor_copy(out=ybf[:, oc, 0:T], in_=ycv[:, oc, 0:T]), reads=[ycvb[oc]], writes=[ybf.b])
                sq = sqr.next()
                sc.op("act", lambda e, oc=oc, sq=sq: e.activation(out=sq[:, 0:T], in_=ycv[:, oc, 0:T], func=AF.Square), reads=[ycvb[oc]], writes=[sq.b])
                mm(b1.ap(0, 128, 0, T), ones[:, 2, :], ybf[:, oc, 0:T], oc == 0, oc == 7, [ones.b, ybf.b], [b1.b])
                mm(b2.ap(0, 128, 0, T), ones[:, 2, :], sq[:, 0:T], oc == 0, oc == 7, [ones.b, sq.b], [b2.b])
            sc.op("dve", lambda e: e.tensor_copy(out=ln_m[:, 0:T], in_=b1.ap(0, 128, 0, T)), reads=[b1.b], writes=[ln_m.b])
            t_ = tA.next()
            sc.op("dve", lambda e: e.tensor_tensor(out=t_[:, 0:T], in0=ln_m[:, 0:T], in1=ln_m[:, 0:T], op=ALU.mult), reads=[ln_m.b], writes=[t_.b])
            t2_ = tB.next()
            sc.op("dve", lambda e: e.tensor_tensor(out=t2_[:, 0:T], in0=b2.ap(0, 128, 0, T), in1=t_[:, 0:T], op=ALU.subtract), reads=[b2.b, t_.b], writes=[t2_.b])
            sc.op("dve", lambda e: e.tensor_scalar(out=t2_[:, 0:T], in0=t2_[:, 0:T], scalar1=0.0, scalar2=None, op0=ALU.max), reads=[t2_.b], writes=[t2_.b])
            sc.op("act", lambda e: e.activation(out=t_[:, 0:T], in_=t2_[:, 0:T], func=AF.Sqrt, bias=V("c_eps"), scale=1.0), reads=[t2_.b, vec.b], writes=[t_.b])
            sc.op("dve", lambda e: e.reciprocal(out=ln_r[:, 0:T], in_=t_[:, 0:T]), reads=[t_.b], writes=[ln_r.b])
            for oc in range(8):
                t_ = tA.next()
                sc.op("dve", lambda e, oc=oc, t_=t_: e.tensor_tensor(out=t_[:, 0:T], in0=ycv[:, oc, 0:T], in1=ln_m[:, 0:T], op=ALU.subtract),
                      reads=[ycvb[oc], ln_m.b], writes=[t_.b])
                t2_ = tB.next()
                sc.op("dve", lambda e, t_=t_, t2_=t2_: e.tensor_tensor(out=t2_[:, 0:T], in0=t_[:, 0:T], in1=ln_r[:, 0:T], op=ALU.mult),
                      reads=[t_.b, ln_r.b], writes=[t2_.b])
                sc.op("act", lambda e, oc=oc, t2_=t2_: e.activation(out=zc[:, oc, 0:T], in_=t2_[:, 0:T], func=AF.Silu, bias=V("b_ln", oc), scale=V("g_ln", oc)),
                      reads=[t2_.b, vec.b], writes=[zc.b])
            for oc in range(16):
                (wao, wco, wga, wgb_), wtb = wload(4 + oc)
                bYa, bYb, bGa, bGb = rot7.next(), rot7.next(), rot7.next(), rot7.next()
                for kc in range(16):
                    mm(bYa.ap(0, 128, 0, T), wao[:, kc, :], att[:, kc, 0:T], kc == 0, kc == 15, [wtb, att.b], [bYa.b])
                for kc in range(8):
                    mm(bYb.ap(0, 128, 0, T), wco[:, kc, :], zc[:, kc, 0:T], kc == 0, kc == 7, [wtb, zc.b], [bYb.b])
                for kc in range(16):
                    mm(bGa.ap(0, 128, 0, T), wga[:, kc, :], hT[:, kc, 0:T], kc == 0, kc == 15, [wtb, hT.b], [bGa.b])
                for kc in range(16):
                    mm(bGb.ap(0, 128, 0, T), wgb_[:, kc, :], hT[:, kc, 0:T], kc == 0, kc == 15, [wtb, hT.b], [bGb.b])
                sa, sb_ = sg.next(), sg.next()
                sc.op("act", lambda e, bGa=bGa, sa=sa, oc=oc: e.activation(out=sa[:, 0:T], in_=bGa.ap(0, 128, 0, T), func=AF.Sigmoid,
                                                                          bias=V("b_gate", oc), scale=1.0), reads=[bGa.b, vec.b], writes=[sa.b])
                sc.op("act", lambda e, bGb=bGb, sb_=sb_, oc=oc: e.activation(out=sb_[:, 0:T], in_=bGb.ap(0, 128, 0, T), func=AF.Sigmoid,
                                                                            bias=V("b_gate", 16 + oc), scale=1.0), reads=[bGb.b, vec.b], writes=[sb_.b])
                t_, t2_ = tA.next(), tB.next()
                sc.op("dve", lambda e, bYa=bYa, sa=sa, t_=t_: e.tensor_tensor(out=t_[:, 0:T], in0=bYa.ap(0, 128, 0, T), in1=sa[:, 0:T], op=ALU.mult),
                      reads=[bYa.b, sa.b], writes=[t_.b])
                sc.op("dve", lambda e, bYb=bYb, sb_=sb_, t2_=t2_, oc=oc: e.scalar_tensor_tensor(
                    out=t2_[:, 0:T], in0=bYb.ap(0, 128, 0, T), scalar=V("b_co", oc), in1=sb_[:, 0:T], op0=ALU.add, op1=ALU.mult),
                    reads=[bYb.b, sb_.b, vec.b], writes=[t2_.b])
                sc.op("dve", lambda e, t_=t_, t2_=t2_, oc=oc: e.tensor_tensor(out=z2[:, oc, 0:T], in0=t_[:, 0:T], in1=t2_[:, 0:T], op=ALU.add),
                      reads=[t_.b, t2_.b], writes=[z2.b])
            for g in range(4):
                (wo,), wob = wload(20 + g)
                for o in range(4):
                    oc = g * 4 + o
                    bk = rot7.next()
                    for kc in range(16):
                        mm(bk.ap(0, 128, 0, T), wo[:, kc, o * 128:(o + 1) * 128], z2[:, kc, 0:T], kc == 0, kc == 15, [wob, z2.b], [bk.b])
                    sc.op("dve", lambda e, bk=bk, oc=oc: e.tensor_tensor(out=xt[:, oc, 0:T], in0=xt[:, oc, 0:T], in1=bk.ap(0, 128, 0, T), op=ALU.add),
                          reads=[bk.b, xt.b], writes=[xt.b])
            sc.barrier()
            bk = rot7.next()
            for kc in range(16):
                sq = sqr.next()
                sc.op("act", lambda e, kc=kc, sq=sq: e.activation(out=sq[:, 0:T], in_=xt[:, kc, 0:T], func=AF.Square), reads=[xt.b], writes=[sq.b])
                mm(bk.ap(0, 128, 0, T), ones[:, 0, :], sq[:, 0:T], kc == 0, kc == 15, [ones.b, sq.b], [bk.b])
            ps_ap_b[0] = [bk.b]
            rsqrt_bc(bk.ap(0, 128, 0, T), "c_eps", st3["rs"], 128, T, st3["rtmp"])
            rs = st3["rs"]
            for kc in range(16):
                sc.op("dve", lambda e, kc=kc: e.scalar_tensor_tensor(out=hT[:, kc, 0:T], in0=xt[:, kc, 0:T], scalar=V("g_ffn", kc), in1=rs[:, 0:T],
                                                                     op0=ALU.mult, op1=ALU.mult), reads=[xt.b, rs.b, vec.b], writes=[hT.b])
            for g in range(11):
                (wg_,), wgb2 = wload(24 + 2 * g)
                (wu_,), wub2 = wload(25 + 2 * g)
                for o in range(4):
                    fc = g * 4 + o
                    bG, bU = rot7.next(), rot7.next()
                    for kc in range(16):
                        mm(bG.ap(0, 128, 0, T), wg_[:, kc, o * 128:(o + 1) * 128], hT[:, kc, 0:T], kc == 0, kc == 15, [wgb2, hT.b], [bG.b])
                    for kc in range(16):
                        mm(bU.ap(0, 128, 0, T), wu_[:, kc, o * 128:(o + 1) * 128], hT[:, kc, 0:T], kc == 0, kc == 15, [wub2, hT.b], [bU.b])
                    s_ = sg.next()
                    sc.op("act", lambda e, bG=bG, s_=s_: e.activation(out=s_[:, 0:T], in_=bG.ap(0, 128, 0, T), func=AF.Silu), reads=[bG.b], writes=[s_.b])
                    sc.op("dve", lambda e, bU=bU, s_=s_, fc=fc: e.tensor_tensor(out=act_t[:, fc, 0:T], in0=bU.ap(0, 128, 0, T), in1=s_[:, 0:T], op=ALU.mult),
                          reads=[bU.b, s_.b], writes=[act_t.b])
            for oc in range(16):
                (wd,), wdb = wload(46 + oc)
                bk = rot7.next()
                for fc in range(44):
                    mm(bk.ap(0, 128, 0, T), wd[:, fc, :], act_t[:, fc, 0:T], fc == 0, fc == 43, [wdb, act_t.b], [bk.b])
                ys = yst.next()
                sc.op("dve", lambda e, bk=bk, oc=oc, ys=ys: e.tensor_tensor(out=ys[:, 0:T], in0=xt[:, oc, 0:T], in1=bk.ap(0, 128, 0, T), op=ALU.add),
                      reads=[bk.b, xt.b], writes=[ys.b])
                sc.dma("sp", y_out[oc * 128:(oc + 1) * 128, :], ys[:, 0:T], reads=[ys.b])
            sc.barrier()

        ycvb = [Buf() for _ in range(8)]
        DBG.update(z2=z2, zc=zc, att=att, xt=xt, hT=hT, arena=arena)
        for t in range(0 if SAMPLE_ONLY else 4):
            c = slice(t * 512, (t + 1) * 512)
            hc = slice(t * 128, (t + 1) * 128)
            uo = None
            if t == 3:
                def uo():
                    sc.dma("sp", uT_last.rearrange("(k p) t -> p k t", p=128), upad[:, :, 3, 128:160], reads=[upad.b])
            post(xT_own[:, c], 512, 4, 128, xT_halo[:, hc], hmask[:, hc], None, t * 512, yT_own[:, c], uo)

        def uo_s():
            for b in range(2):
                sc.dma("sp", uT_s[:, 32 * b:32 * b + 32].rearrange("(k p) t -> p k t", p=128), upad[:, :, b, 32:64], reads=[upad.b])
        post(xT_s[:, 0:64], 64, 2, 32, None, None, [sconvT[0], sconvT[1]], TOWN, yT_s[:, 0:64], uo_s)
        free(p3)

    sc.barrier()
    return nc


_CACHE = {}


def _prep_inputs(inp):
    f = np.float32
    xp = np.asarray(inp["x_prompt"], f)[0]
    xs = np.asarray(inp["x_sample"], f)
    xT_all = np.ascontiguousarray(xp.T)
    xt = xp.reshape(128, 128, D)
    vecs = np.zeros((128, NV), f)

    def put(name, arr):
        arr = np.asarray(arr, f)
        vecs[:arr.shape[0], VO[name]:VO[name] + arr.shape[1]] = arr

    col = lambda v: np.ascontiguousarray(np.asarray(v, f).reshape(-1, 128).T)
    put("g_mix", col(inp["g_mix_norm"][0]))
    put("g_q_a", col(inp["g_q_a"][0]))
    put("g_kv_a", col(inp["g_kv_a"][0]))
    put("b_glu", col(inp["b_glu"][0]))
    put("b_gate", col(inp["b_gate"][0]))
    wdw = np.asarray(inp["w_dw"], f)[0]
    put("w_dw", np.ascontiguousarray(wdw.T.reshape(8, 128, 31).transpose(1, 0, 2).reshape(128, 248)))
    put("b_dw", col(inp["b_dw"][0]))
    put("g_ln", col(inp["g_conv_ln"][0]))
    put("b_ln", col(inp["b_conv_ln"][0]))
    put("b_co", col(inp["b_conv_out"][0]))
    put("g_ffn", col(inp["g_ffn_norm"][0]))
    for nm, key in (("gq", "g_q_norm"), ("gk", "g_k_norm")):
        g = np.asarray(inp[key], f)[0]
        a = np.ones((128, 2), f)
        a[:, 0] = g[0:128]
        a[0:64, 1] = g[128:192]
        a[64:128, 1] = g[128:192]
        put(nm, a)
    invf = (1.0 / (np.float32(10000.0) ** (np.arange(0, 64, 2, dtype=np.float32) / np.float32(64)))).astype(f)
    iv = np.zeros((128, 1), f)
    iv[0:64, 0] = np.concatenate([invf, invf])
    iv[64:128, 0] = np.concatenate([invf, invf])
    put("invf", iv)
    put("c_eps", np.full((128, 1), EPS, f))
    put("c_eps192", np.full((128, 1), 192 * EPS, f))

    ident = np.eye(128, dtype=f)
    common = {
        "xT_all": xT_all, "vecs": vecs, "ident": ident,
        "pos_all": np.arange(S, dtype=f)[None, :],
        "pos_s": np.tile(np.arange(PAST, PAST + TS, dtype=f), 2)[None, :],
        "w_in": np.ascontiguousarray(inp["w_in"][0], f), "w_q_up": np.ascontiguousarray(inp["w_q_up"][0], f),
        "w_kv_up": np.ascontiguousarray(inp["w_kv_up"][0], f), "w_attn_out": np.ascontiguousarray(inp["w_attn_out"][0], f),
        "w_conv_out": np.ascontiguousarray(inp["w_conv_out"][0], f), "w_out": np.ascontiguousarray(inp["w_out"][0], f),
        "w_ffn_gate": np.ascontiguousarray(inp["w_ffn_gate"][0], f), "w_ffn_up": np.ascontiguousarray(inp["w_ffn_up"][0], f),
        "w_ffn_down": np.ascontiguousarray(inp["w_ffn_down"][0], f),
    }
    maps = []
    for c in range(NCORE):
        gt = np.arange(16) * 8 + c
        own = xt[gt].reshape(TOWN, D)
        halo = np.zeros((16, 32, D), f)
        hmask = np.ones((16, 32), f)
        for j, g in enumerate(gt):
            if g == 0:
                hmask[j] = 0.0
            else:
                halo[j] = xt[g - 1, 96:128]
        pos_own = (gt[:, None] * 128 + np.arange(128)[None, :]).reshape(1, TOWN).astype(f)
        mb = np.zeros((128, 8, 128), f)
        for m in range(8):
            if m > c:
                mb[:, m, :] = -30000.0
            elif m == c:
                mb[64:128, m, 0:64] = -30000.0
        d = dict(common)
        d.update({
            "xT_own": np.ascontiguousarray(own.T), "xT_halo": np.ascontiguousarray(halo.reshape(512, D).T),
            "xT_s": np.ascontiguousarray(xs[2 * c:2 * c + 2].reshape(64, D).T),
            "pos_own": pos_own, "hmask": hmask.reshape(1, 512), "mb": mb.reshape(128, 1024),
            "cckvT": np.ascontiguousarray(np.asarray(inp["cache_ckv"], f)[0, 2 * c:2 * c + 2].transpose(0, 2, 1)),
            "ckpeT": np.ascontiguousarray(np.asarray(inp["cache_kpe"], f)[0, 2 * c:2 * c + 2].transpose(0, 2, 1)),
            "sconvT": np.ascontiguousarray(np.asarray(inp["state_conv"], f)[0, 2 * c:2 * c + 2].transpose(0, 2, 1)),
        })
        maps.append(d)
    return maps


def kernel(**inp):
    if "nc" not in _CACHE:
        _CACHE["nc"] = build_program()
    nc = _CACHE["nc"]
    maps = _prep_inputs(inp)
    res = run_bass_kernel_spmd(nc, maps, core_ids=list(range(NCORE)))
    R = res.results
    f = np.float32
    y_p = np.zeros((1, S, D), f)
    ckv_p = np.zeros((1, 1, S, 512), f)
    kpe_p = np.zeros((1, 1, S, 64), f)
    y_s = np.zeros((16, TS, D), f)
    ckv_s = np.zeros((1, 16, TS, 512), f)
    kpe_s = np.zeros((1, 16, TS, 64), f)
    conv_s = np.zeros((1, 16, 30, CONV), f)
    for c in range(NCORE):
        r = R[c]
        gt = np.arange(16) * 8 + c
        idx = (gt[:, None] * 128 + np.arange(128)[None, :]).reshape(-1)
        y_p[0, idx] = np.asarray(r["yT_own"]).T
        ckv_p[0, 0, idx] = np.asarray(r["ckvT_own"]).T
        kpe_p[0, 0, idx] = np.asarray(r["kpeT_own"]).T
        y_s[2 * c:2 * c + 2] = np.asarray(r["yT_s"]).T.reshape(2, TS, D)
        ckv_s[0, 2 * c:2 * c + 2] = np.asarray(r["ckvT_s"]).T.reshape(2, TS, 512)
        kpe_s[0, 2 * c:2 * c + 2] = np.asarray(r["kpeT_s"]).T.reshape(2, TS, 64)
        us = np.asarray(r["uT_s"]).T.reshape(2, TS, CONV)
        conv_s[0, 2 * c:2 * c + 2] = us[:, 2:32]
    conv_p = np.asarray(R[7]["uT_last"]).T[2:32][None, None]
    return (y_p, y_s, ckv_p, kpe_p, conv_p.astype(f), ckv_s, kpe_s, conv_s)
```
